# Optimizing a Trainium2 kernel written in Bass

```python
import math
import jax, jax.numpy as jnp
from jax import lax
import numpy as np

D_MODEL = 2048
BATCH = 8
SEQ = 2048
DEPTH = 4
DEC_BATCH = 2
DEC_SEQ = 4096
PAST_LEN = 128

N_MIXERS = 2
N_DIFF = (DEPTH + 1) // 2
N_WIN = DEPTH // 2
DIFF_HEAD_DIM = 64
DIFF_HEADS = D_MODEL // (2 * DIFF_HEAD_DIM)
DIFF_IN = 6 * DIFF_HEADS * DIFF_HEAD_DIM
WIN_HEAD_DIM = 128
WIN_Q_HEADS = D_MODEL // WIN_HEAD_DIM
WIN_KV_HEADS = 4
WIN_IN = (WIN_Q_HEADS + 2 * WIN_KV_HEADS) * WIN_HEAD_DIM
WINDOW = 128
Q_BLOCK = 128
FFN_DIM = ((8 * D_MODEL // 3 + 127) // 128) * 128
CONV_WIDTH = 3
ROPE_THETA = 500000.0
ROPE_FRACTION = 4
ALPHA = (2 * DEPTH) ** 0.25
BETA = (8 * DEPTH) ** -0.25
LN_EPS = 1e-5

kernel_name = "hybrid_diffattn_swa_sink_convffn_encoder"


def layer_norm(x, g, b):
    xf = x.astype(jnp.float32)
    mu = jnp.mean(xf, axis=-1, keepdims=True)
    var = jnp.mean(jnp.square(xf - mu), axis=-1, keepdims=True)
    y = (xf - mu) * lax.rsqrt(var + LN_EPS) * g.astype(jnp.float32) + b.astype(jnp.float32)
    return y.astype(x.dtype)


def rope_tables(seq, rot_dim):
    pos = jnp.arange(seq, dtype=jnp.float32)
    inv_freq = ROPE_THETA ** (-jnp.arange(0, rot_dim, 2, dtype=jnp.float32) / rot_dim)
    ang = pos[:, None] * inv_freq[None, :]
    return jnp.cos(ang), jnp.sin(ang)


def apply_partial_rope(x, cos, sin):
    half = cos.shape[-1]
    rot = 2 * half
    c = cos[:, None, :].astype(x.dtype)
    s = sin[:, None, :].astype(x.dtype)
    x1 = x[..., :half]
    x2 = x[..., half:rot]
    return jnp.concatenate([x1 * c - x2 * s, x2 * c + x1 * s, x[..., rot:]], axis=-1)


def diff_attention(x, w_in, lam, subln_g, lambda_init):
    B, S, _ = x.shape
    H, d = DIFF_HEADS, DIFF_HEAD_DIM
    qkv = x @ w_in
    q, k, v = jnp.split(qkv, [2 * H * d, 4 * H * d], axis=-1)
    cos, sin = rope_tables(S, d // ROPE_FRACTION)
    q = apply_partial_rope(q.reshape(B, S, 2 * H, d), cos, sin).reshape(B, S, H, 2, d)
    k = apply_partial_rope(k.reshape(B, S, 2 * H, d), cos, sin).reshape(B, S, H, 2, d)
    v = v.reshape(B, S, H, 2 * d)
    lf = lam.astype(jnp.float32)
    lam_full = jnp.exp(jnp.sum(lf[0] * lf[1])) - jnp.exp(jnp.sum(lf[2] * lf[3])) + lambda_init
    g = subln_g.astype(jnp.float32)
    nb = S // Q_BLOCK
    qb = q.reshape(B, nb, Q_BLOCK, H, 2, d).transpose(1, 0, 2, 3, 4, 5)
    scale = d ** -0.5

    def block(qi):
        s = jnp.einsum('bqhcd,bkhcd->bhcqk', qi, k, preferred_element_type=jnp.float32) * scale
        p = jax.nn.softmax(s, axis=-1)
        a = p[:, :, 0] - lam_full * p[:, :, 1]
        o = jnp.einsum('bhqk,bkhe->bqhe', a.astype(v.dtype), v,
                       preferred_element_type=jnp.float32)
        o = o * lax.rsqrt(jnp.mean(jnp.square(o), axis=-1, keepdims=True) + LN_EPS) * g
        o = o * (1.0 - lambda_init)
        return o.reshape(B, Q_BLOCK, H * 2 * d).astype(x.dtype)

    o = lax.map(block, qb)
    return o.transpose(1, 0, 2, 3).reshape(B, S, H * 2 * d)


def window_attention(x, w_in, sink):
    B, S, _ = x.shape
    G, R, hd, W = WIN_KV_HEADS, WIN_Q_HEADS // WIN_KV_HEADS, WIN_HEAD_DIM, WINDOW
    nb = S // W
    qkv = x @ w_in
    q, k, v = jnp.split(qkv, [WIN_Q_HEADS * hd, (WIN_Q_HEADS + G) * hd], axis=-1)
    cos, sin = rope_tables(S, hd // ROPE_FRACTION)
    q = apply_partial_rope(q.reshape(B, S, WIN_Q_HEADS, hd), cos, sin).reshape(B, nb, W, G, R, hd)
    k = apply_partial_rope(k.reshape(B, S, G, hd), cos, sin)
    v = v.reshape(B, S, G, hd)

    def band(t):
        tp = jnp.pad(t, ((0, 0), (W, W), (0, 0), (0, 0))).reshape(B, nb + 2, W, G, hd)
        return jnp.concatenate([tp[:, :-2], tp[:, 1:-1], tp[:, 2:]], axis=2)

    kb, vb = band(k), band(v)
    s = jnp.einsum('bnqgrd,bnkgd->bngrqk', q, kb, preferred_element_type=jnp.float32) * hd ** -0.5
    qpos = jnp.arange(nb)[:, None, None] * W + jnp.arange(W)[None, :, None]
    kpos = jnp.arange(nb)[:, None, None] * W - W + jnp.arange(3 * W)[None, None, :]
    valid = (jnp.abs(qpos - kpos) <= W) & (kpos >= 0) & (kpos < S)
    s = jnp.where(valid[None, :, None, None], s, -jnp.inf)
    sk = sink.astype(jnp.float32).reshape(G, R)[None, None, :, :, None, None]
    m = jnp.maximum(jnp.max(s, axis=-1, keepdims=True), sk)
    e = jnp.exp(s - m)
    w = e / (jnp.sum(e, axis=-1, keepdims=True) + jnp.exp(sk - m))
    o = jnp.einsum('bngrqk,bnkgd->bnqgrd', w.astype(vb.dtype), vb)
    return o.reshape(B, S, WIN_Q_HEADS * hd)


def conv_ffn(x, w_up, conv_w, conv_b, w_down):
    h = x @ w_up
    C = h.shape[-1]
    h = lax.conv_general_dilated(h, conv_w[:, None, :], window_strides=(1,),
                                 padding=((CONV_WIDTH // 2, CONV_WIDTH // 2),),
                                 dimension_numbers=('NWC', 'WIO', 'NWC'),
                                 feature_group_count=C) + conv_b
    g, u = jnp.split(h, 2, axis=-1)
    return (jax.nn.silu(g) * u) @ w_down


def trunk(x, diff_w_in, diff_lam, diff_subln_g, diff_w_out, win_w_in, win_sink, win_w_out,
          ln_mix_g, ln_mix_b, ffn_w_up, ffn_conv_w, ffn_conv_b, ffn_w_down, ln_ffn_g, ln_ffn_b):
    for i in range(DEPTH):
        j = i // N_MIXERS
        if i % N_MIXERS == 0:
            lambda_init = 0.8 - 0.6 * math.exp(-0.3 * i)
            o = diff_attention(x, diff_w_in[j], diff_lam[j], diff_subln_g[j], lambda_init) @ diff_w_out[j]
        else:
            o = window_attention(x, win_w_in[j], win_sink[j]) @ win_w_out[j]
        x = layer_norm(ALPHA * x + o, ln_mix_g[i], ln_mix_b[i])
        f = conv_ffn(x, ffn_w_up[i], ffn_conv_w[i], ffn_conv_b[i], ffn_w_down[i])
        x = layer_norm(ALPHA * x + f, ln_ffn_g[i], ln_ffn_b[i])
    return x


def setup_inputs(seed: int = 0) -> dict:
    key = jax.random.key(seed)
    ks = jax.random.split(key, 20)
    f32 = jnp.float32
    nrm = lambda k, shape: jax.random.normal(k, shape, dtype=f32)
    d_in = D_MODEL ** -0.5
    qk_cols = 4 * DIFF_HEADS * DIFF_HEAD_DIM
    v_cols = DIFF_IN - qk_cols
    diff_w_in = jnp.concatenate([nrm(ks[2], (N_DIFF, D_MODEL, qk_cols)) * d_in,
                                 nrm(ks[3], (N_DIFF, D_MODEL, v_cols)) * d_in * BETA], axis=-1)
    win_qk_cols = (WIN_Q_HEADS + WIN_KV_HEADS) * WIN_HEAD_DIM
    win_w_in = jnp.concatenate([nrm(ks[4], (N_WIN, D_MODEL, win_qk_cols)) * d_in,
                                nrm(ks[5], (N_WIN, D_MODEL, WIN_IN - win_qk_cols)) * d_in * BETA], axis=-1)
    return {
        "x_prompt": nrm(ks[0], (BATCH, SEQ, D_MODEL)),
        "x_sample": nrm(ks[1], (DEC_BATCH, DEC_SEQ, D_MODEL)),
        "diff_w_in": diff_w_in,
        "diff_lam": nrm(ks[6], (N_DIFF, 4, DIFF_HEAD_DIM)) * 0.1,
        "diff_subln_g": 1.0 + 0.02 * nrm(ks[7], (N_DIFF, 2 * DIFF_HEAD_DIM)),
        "diff_w_out": nrm(ks[8], (N_DIFF, D_MODEL, D_MODEL)) * d_in * BETA,
        "win_w_in": win_w_in,
        "win_sink": nrm(ks[9], (N_WIN, WIN_Q_HEADS)) * 0.5,
        "win_w_out": nrm(ks[10], (N_WIN, D_MODEL, D_MODEL)) * d_in * BETA,
        "ln_mix_g": 1.0 + 0.02 * nrm(ks[11], (DEPTH, D_MODEL)),
        "ln_mix_b": 0.02 * nrm(ks[12], (DEPTH, D_MODEL)),
        "ffn_w_up": nrm(ks[13], (DEPTH, D_MODEL, 2 * FFN_DIM)) * d_in * BETA,
        "ffn_conv_w": nrm(ks[14], (DEPTH, CONV_WIDTH, 2 * FFN_DIM)) * CONV_WIDTH ** -0.5,
        "ffn_conv_b": 0.02 * nrm(ks[15], (DEPTH, 2 * FFN_DIM)),
        "ffn_w_down": nrm(ks[16], (DEPTH, FFN_DIM, D_MODEL)) * FFN_DIM ** -0.5 * BETA,
        "ln_ffn_g": 1.0 + 0.02 * nrm(ks[17], (DEPTH, D_MODEL)),
        "ln_ffn_b": 0.02 * nrm(ks[18], (DEPTH, D_MODEL)),
    }


def reference(x_prompt, x_sample, diff_w_in, diff_lam, diff_subln_g, diff_w_out, win_w_in, win_sink,
              win_w_out, ln_mix_g, ln_mix_b, ffn_w_up, ffn_conv_w, ffn_conv_b, ffn_w_down, ln_ffn_g, ln_ffn_b):
    y_prompt = trunk(x_prompt, diff_w_in, diff_lam, diff_subln_g, diff_w_out, win_w_in, win_sink, win_w_out,
                     ln_mix_g, ln_mix_b, ffn_w_up, ffn_conv_w, ffn_conv_b, ffn_w_down, ln_ffn_g, ln_ffn_b)
    y_sample = trunk(x_sample, diff_w_in, diff_lam, diff_subln_g, diff_w_out, win_w_in, win_sink, win_w_out,
                     ln_mix_g, ln_mix_b, ffn_w_up, ffn_conv_w, ffn_conv_b, ffn_w_down, ln_ffn_g, ln_ffn_b)
    return (y_prompt, y_sample)
```

```python
import math
from contextlib import ExitStack
import numpy as np
import ml_dtypes
import concourse.bass as bass
import concourse.mybir as mybir
from concourse.bass_utils import run_bass_kernel_spmd

F32 = mybir.dt.float32
BF16 = mybir.dt.bfloat16
AF = mybir.ActivationFunctionType
ALU = mybir.AluOpType
AX = mybir.AxisListType

T = 4096
D = 2048
NTB = 32
FF = 5504
NFC = 43
DEPTH = 4
ALPHA = (2 * DEPTH) ** 0.25
EPS = 1e-5
NEG = -30000.0
KD = 6


class Buf:
    def __init__(self, excl=False):
        self.w = None
        self.r = {}
        self.excl = excl


class Eng:
    def __init__(self, sem, sid):
        self.sem, self.sid, self.n, self.seen, self.q = sem, sid, 0, {}, []
        self.dsems, self.dcnt, self.k = [], [], 0

    def wait(self, dep):
        if dep is None:
            return
        sid, sem, val = dep
        if self.seen.get(sid, 0) >= val:
            return
        self.e.wait_ge(sem, val)
        self.seen[sid] = val


def _deps(E, reads, writes):
    for b in reads:
        E.wait(b.w)
        if b.excl:
            for sid, d in b.r.items():
                if sid != E.sid:
                    E.wait(d)
    for b in writes:
        if b.w is not None and b.w[0] != E.sid:
            E.wait(b.w)
        for sid, d in b.r.items():
            if sid != E.sid:
                E.wait(d)


def _mark(d, reads, writes):
    for b in reads:
        b.r[d[0]] = d
    for b in writes:
        b.w = d
        b.r = {}


def op(E, fn, reads=(), writes=()):
    _deps(E, reads, writes)
    E.n += 1
    fn().then_inc(E.sem, 1)
    _mark((E.sid, E.sem, E.n), reads, writes)


def dma(Q, fn, reads=(), writes=()):
    slot = Q.k % KD
    Q.k += 1
    sid = 100 + Q.sid * 10 + slot
    sem = Q.dsems[slot]
    Q.wait((sid, sem, Q.dcnt[slot] * 16))
    for b in reads:
        Q.wait(b.w)
    for b in writes:
        Q.wait(b.w)
        for d in b.r.values():
            Q.wait(d)
    fn().then_inc(sem, 16)
    Q.dcnt[slot] += 1
    _mark((sid, sem, Q.dcnt[slot] * 16), reads, writes)


def build():
    nc = bass.Bass("TRN2", target_bir_lowering=False)
    dt = lambda n, s, d, k="ExternalInput": nc.dram_tensor(n, s, d, kind=k).ap()
    xin = dt("xin", [T, D], F32)
    diff_w_in = dt("diff_w_in", [2, D, 6144], F32)
    diff_w_out = dt("diff_w_out", [2, D, D], F32)
    win_w_in = dt("win_w_in", [2, D, 3072], F32)
    win_w_out = dt("win_w_out", [2, D, D], F32)
    ffn_w_up = dt("ffn_w_up", [4, D, 2 * FF], F32)
    ffn_w_down = dt("ffn_w_down", [4, FF, D], F32)
    lam_r = dt("lam_r", [2, 128, 256], F32)
    subg_r = dt("subg_r", [2, 128, 128], F32)
    sink_r = dt("sink_r", [2, 128, 16], F32)
    lnm_g = dt("lnm_g", [4, 128, D], F32)
    lnm_b = dt("lnm_b", [4, 128, D], F32)
    lnf_g = dt("lnf_g", [4, 128, D], F32)
    lnf_b = dt("lnf_b", [4, 128, D], F32)
    convp = dt("convp", [4, 128, 2 * NFC, 4], F32)
    ropd = dt("ropd", [2, 128, NTB, 64], F32)
    ropw = dt("ropw", [2, 128, NTB, 64], F32)
    maskb = dt("maskb", [128, 2, 32], F32)
    flags = dt("flags", [128, 2], F32)
    ident_d = dt("ident", [128, 128], BF16)
    trimask = dt("trimask", [2, 128, 512], BF16)
    y = dt("y", [T, D], F32, "ExternalOutput")
    X = nc.dram_tensor("X", [T, D], F32).ap()
    X2 = nc.dram_tensor("X2", [T, D], F32).ap()
    XT = nc.dram_tensor("XT", [16, 128, T], BF16).ap()
    XT2 = nc.dram_tensor("XT2", [16, 128, T], BF16).ap()
    QT = nc.dram_tensor("QT", [16, 128, T], BF16).ap()
    KT = nc.dram_tensor("KT", [16, 128, T], BF16).ap()
    Vd = nc.dram_tensor("Vd", [T, D], BF16).ap()
    OT = nc.dram_tensor("OT", [16, 128, T], BF16).ap()

    top = ExitStack()
    with top:
        nsem = 5 + 2 * KD
        sems = [top.enter_context(nc.semaphore(f"s{i}")) for i in range(nsem)]
        PE, ACT, DVE, POOL, SP = [Eng(sems[i], i) for i in range(5)]
        ENGS = [PE, ACT, DVE, POOL, SP]
        for qi, Q in enumerate((POOL, SP)):
            Q.dsems = sems[5 + qi * KD: 5 + (qi + 1) * KD]
            Q.dcnt = [0] * KD
        tE, sE, vE, gE, yE = nc.tensor, nc.scalar, nc.vector, nc.gpsimd, nc.sync
        for E_, h_ in zip(ENGS, (tE, sE, vE, gE, yE)):
            E_.e = h_

        def barrier():
            for E in ENGS:
                for F in ENGS:
                    if F is not E and F.n > 0:
                        E.wait((F.sid, F.sem, F.n))
                for Q in (POOL, SP):
                    for s in range(KD):
                        if Q.dcnt[s]:
                            E.wait((100 + Q.sid * 10 + s, Q.dsems[s], Q.dcnt[s] * 16))

        PSALL = top.enter_context(nc.psum_tensor("psall", [128, 8 * 512], F32))

        class _Bank:
            def __init__(self, i):
                self.i = i

            def __getitem__(self, key):
                return PSALL[:, self.i * 512:(self.i + 1) * 512][key]
        PS = [_Bank(i) for i in range(8)]
        PB = [Buf(excl=True) for _ in range(8)]
        ident = top.enter_context(nc.sbuf_tensor("identt", [128, 128], BF16))
        flg = top.enter_context(nc.sbuf_tensor("flg", [128, 2], F32))
        cb = Buf()
        dma(SP, lambda: yE.dma_start(out=ident[:], in_=ident_d), writes=[cb])
        dma(SP, lambda: yE.dma_start(out=flg[:], in_=flags), writes=[cb])

        uid = [0]

        def sbt(es, name, shape, dtype):
            uid[0] += 1
            return es.enter_context(nc.sbuf_tensor(f"{name}_{uid[0]}", shape, dtype)), Buf()

        def transpose_store(xb, xbB, dst3, stg, stgB, tpi):
            for q4 in range(4):
                bk = tpi[q4 % 2]
                tp = PS[bk][:].bitcast(BF16)
                for j in range(4):
                    kc = q4 * 4 + j
                    op(PE, lambda tp=tp, j=j, kc=kc: tE.transpose(tp[:, j * 128:(j + 1) * 128], xb[:, kc * 128:(kc + 1) * 128], ident[:]),
                       reads=[xbB, cb], writes=[PB[bk]])
                op(DVE, lambda tp=tp, q4=q4: vE.tensor_copy(stg[:, q4 * 4:(q4 + 1) * 4, :], tp[:, 0:512].rearrange("p (a b) -> p a b", a=4)),
                   reads=[PB[bk]], writes=[stgB])
            dma(SP, lambda: yE.dma_start(out=dst3, in_=stg[:]), reads=[stgB])

        def xt_view(XTd):
            return XTd.rearrange("k p t -> p k t")

        def layer_norm(es_t, xr, xrB, Gt, Bt, gB, Xrows, XTd, tb, tpi, tiles, defer=None):
            st, stB, mv, mvB, sc, scB, xb, xbB, stg, stgB = tiles
            for c in range(4):
                op(DVE, lambda c=c: vE.bn_stats(st[:, c, :], xr[:, c * 512:(c + 1) * 512]), reads=[xrB], writes=[stB])
            op(DVE, lambda: vE.bn_aggr(mv[:], st[:].rearrange("p a b -> p (a b)")), reads=[stB], writes=[mvB])
            op(DVE, lambda: vE.tensor_scalar_add(sc[:, 0:1], mv[:, 1:2], EPS), reads=[mvB], writes=[scB])
            op(ACT, lambda: sE.sqrt(sc[:, 1:2], sc[:, 0:1]), reads=[scB], writes=[scB])
            op(DVE, lambda: vE.reciprocal(sc[:, 2:3], sc[:, 1:2]), reads=[scB], writes=[scB])
            op(DVE, lambda: vE.tensor_scalar(xr, xr, mv[:, 0:1], sc[:, 2:3], ALU.subtract, ALU.mult), reads=[xrB, mvB, scB], writes=[xrB])
            op(DVE, lambda: vE.tensor_tensor(xr, xr, Gt[:], ALU.mult), reads=[xrB, gB], writes=[xrB])
            op(DVE, lambda: vE.tensor_tensor(xr, xr, Bt[:], ALU.add), reads=[xrB, gB], writes=[xrB])
            dma(SP, lambda: yE.dma_start(out=Xrows, in_=xr), reads=[xrB])
            op(ACT, lambda: sE.copy(xb[:], xr), reads=[xrB], writes=[xbB])
            if defer is None:
                transpose_store(xb, xbB, xt_view(XTd)[:, :, tb * 128:(tb + 1) * 128], stg, stgB, tpi)
            else:
                defer.append(lambda: transpose_store(xb, xbB, xt_view(XTd)[:, :, tb * 128:(tb + 1) * 128], stg, stgB, tpi))

        def ln_tiles(es, tag):
            st, stB = sbt(es, "st" + tag, [128, 4, 6], F32)
            mv, mvB = sbt(es, "mv" + tag, [128, 2], F32)
            sc, scB = sbt(es, "sc" + tag, [128, 4], F32)
            xb, xbB = sbt(es, "xb" + tag, [128, D], BF16)
            stg, stgB = sbt(es, "stg" + tag, [128, 16, 128], BF16)
            return (st, stB, mv, mvB, sc, scB, xb, xbB, stg, stgB)

        with ExitStack() as es:
            xrs = [sbt(es, f"pxr{i}", [128, D], F32) for i in range(2)]
            xbs = [sbt(es, f"pxb{i}", [128, D], BF16) for i in range(2)]
            stgs = [sbt(es, f"pstg{i}", [128, 16, 128], BF16) for i in range(2)]
            for tb in range(NTB):
                xr, xrB = xrs[tb % 2]
                xb, xbB = xbs[tb % 2]
                stg, stgB = stgs[tb % 2]
                dma(SP, lambda xr=xr, tb=tb: yE.dma_start(out=xr[:], in_=xin[tb * 128:(tb + 1) * 128, :]), writes=[xrB])
                op(ACT, lambda xr=xr, xb=xb: sE.copy(xb[:], xr[:]), reads=[xrB], writes=[xbB])
                transpose_store(xb, xbB, xt_view(XT)[:, :, tb * 128:(tb + 1) * 128], stg, stgB, [6, 7])
            barrier()

        for L in range(DEPTH):
            j = L // 2
            is_diff = (L % 2 == 0)
            Xsrc = xin if L == 0 else X
            w_in = diff_w_in[j] if is_diff else win_w_in[j]
            ncols = 6144 if is_diff else 3072
            m, hh = (8, 8) if is_diff else (4, 16)
            dh = 512 // m
            w_v = w_in.rearrange("(k p) n -> p k n", p=128)
            with ExitStack() as es:
                xts = [sbt(es, f"qxt{i}", [128, 16, 1024], BF16) for i in range(2)]
                wts = [sbt(es, f"qwt{i}", [128, 16, 512], BF16) for i in range(2)]
                rop, ropB = sbt(es, "rop", [128, 2, NTB, 64], F32)
                t1, t1B = sbt(es, "rt1", [128, m, hh], F32)
                t2, t2B = sbt(es, "rt2", [128, m, hh], F32)
                qbs = [sbt(es, f"qb{i}", [128, 512], BF16) for i in range(2)]
                stgs = [sbt(es, f"qstg{i}", [128, 4, 1024], BF16) for i in range(2)]
                rsrc = ropd if is_diff else ropw
                dma(SP, lambda: yE.dma_start(out=rop[:], in_=rsrc.rearrange("c p t f -> p c t f")), writes=[ropB])
                cnt = 0
                wcnt = 0
                qpend = []

                def q_load(tg):
                    xt, xtB = xts[tg % 2]
                    dma(SP, lambda: yE.dma_start(out=xt[:], in_=xt_view(XT)[:, :, tg * 1024:(tg + 1) * 1024]), writes=[xtB])
                q_load(0)
                for tg in range(4):
                    xt, xtB = xts[tg % 2]
                    if tg + 1 < 4:
                        q_load(tg + 1)
                    for nt in range(ncols // 512):
                        wt, wtB = wts[wcnt % 2]
                        stg, stgB = stgs[wcnt % 2]
                        wcnt += 1
                        dma(POOL, lambda wt=wt, nt=nt: gE.dma_start(out=wt[:], in_=w_v[:, :, nt * 512:(nt + 1) * 512]), writes=[wtB])
                        if is_diff:
                            kind = "q" if nt < 4 else ("k" if nt < 8 else "v")
                            h0 = (nt % 4) * 4
                        else:
                            kind = "q" if nt < 4 else ("k" if nt == 4 else "v")
                            h0 = nt * 4 if nt < 4 else 0
                        for tb in range(8):
                            bk = cnt % 4
                            cnt += 1
                            ps = PS[bk]
                            tbg = tg * 8 + tb
                            for kc in range(16):
                                op(PE, lambda ps=ps, xt=xt, wt=wt, kc=kc, tb=tb: tE.matmul(ps[:], lhsT=xt[:, kc, tb * 128:(tb + 1) * 128], rhs=wt[:, kc, :], start=(kc == 0), stop=(kc == 15)),
                                   reads=[xtB, wtB], writes=[PB[bk]])
                            while qpend:
                                qpend.pop(0)()
                            qb, qbB = qbs[cnt % 2]
                            if kind == "v":
                                op(DVE, lambda qb=qb, ps=ps: vE.tensor_copy(qb[:], ps[:]), reads=[PB[bk]], writes=[qbB])
                                vcols = (nt - 8) * 512 if is_diff else 0
                                dma(SP, lambda qb=qb, tbg=tbg, vcols=vcols: yE.dma_start(out=Vd[tbg * 128:(tbg + 1) * 128, vcols:vcols + 512], in_=qb[:]), reads=[qbB])
                                continue
                            ps3 = ps[:].rearrange("p (m d) -> p m d", m=m)
                            qb3 = qb[:].rearrange("p (m d) -> p m d", m=m)
                            c3 = rop[:, 0, tbg, :].rearrange("p (m h) -> p m h", m=m)
                            s3 = rop[:, 1, tbg, :].rearrange("p (m h) -> p m h", m=m)
                            x1, x2 = ps3[:, :, 0:hh], ps3[:, :, hh:2 * hh]
                            R = [PB[bk], ropB]
                            op(DVE, lambda x1=x1, c3=c3: vE.tensor_tensor(t1[:], x1, c3, ALU.mult), reads=R, writes=[t1B])
                            op(DVE, lambda x2=x2, s3=s3: vE.tensor_tensor(t2[:], x2, s3, ALU.mult), reads=R, writes=[t2B])
                            op(DVE, lambda qb3=qb3: vE.tensor_tensor(qb3[:, :, 0:hh], t1[:], t2[:], ALU.subtract), reads=[t1B, t2B], writes=[qbB])
                            op(DVE, lambda x2=x2, c3=c3: vE.tensor_tensor(t1[:], x2, c3, ALU.mult), reads=R, writes=[t1B])
                            op(DVE, lambda x1=x1, s3=s3: vE.tensor_tensor(t2[:], x1, s3, ALU.mult), reads=R, writes=[t2B])
                            op(DVE, lambda qb3=qb3: vE.tensor_tensor(qb3[:, :, hh:2 * hh], t1[:], t2[:], ALU.add), reads=[t1B, t2B], writes=[qbB])
                            op(DVE, lambda qb3=qb3, ps3=ps3: vE.tensor_copy(qb3[:, :, 2 * hh:], ps3[:, :, 2 * hh:]), reads=[PB[bk]], writes=[qbB])

                            def trp(qb=qb, qbB=qbB, stg=stg, stgB=stgB, tb=tb, tbk=4 + (cnt % 2), last=(tb == 7), kind=kind, h0=h0, tg=tg):
                                tp = PS[tbk][:].bitcast(BF16)
                                for jj in range(4):
                                    op(PE, lambda jj=jj: tE.transpose(tp[:, jj * 128:(jj + 1) * 128], qb[:, jj * 128:(jj + 1) * 128], ident[:]),
                                       reads=[qbB, cb], writes=[PB[tbk]])
                                op(DVE, lambda: vE.tensor_copy(stg[:, :, tb * 128:(tb + 1) * 128], tp[:, 0:512].rearrange("p (a b) -> p a b", a=4)),
                                   reads=[PB[tbk]], writes=[stgB])
                                if last:
                                    dstT = QT if kind == "q" else KT
                                    dma(SP, lambda: yE.dma_start(out=dstT[h0:h0 + 4].rearrange("h p t -> p h t")[:, :, tg * 1024:(tg + 1) * 1024], in_=stg[:]), reads=[stgB])
                            qpend.append(trp)
                while qpend:
                    qpend.pop(0)()
                barrier()
            if is_diff:
                lam_init = 0.8 - 0.6 * math.exp(-0.3 * L)
                with ExitStack() as es:
                    qTs = [sbt(es, f"aq{i}", [128, T], BF16) for i in range(2)]
                    kTs = [sbt(es, f"ak{i}", [128, T], BF16) for i in range(2)]
                    vas = [sbt(es, f"av{i}", [128, 32, 130], BF16) for i in range(2)]
                    Pb = [sbt(es, f"ap{i}", [128, 512], BF16) for i in range(3)]
                    lm, lmB = sbt(es, "lm", [128, 256], F32)
                    gs, gsB = sbt(es, "gs", [128, 128], F32)
                    mk, mkB = sbt(es, "mk", [128, 2, 32], F32)
                    sm, smB = sbt(es, "sm", [128, 16], F32)
                    of, ofB = sbt(es, "of", [128, 128], F32)
                    sq, sqB = sbt(es, "sq", [128, 128], F32)
                    obs = [sbt(es, f"ob{i}", [128, 128], BF16) for i in range(4)]
                    ostgs = [sbt(es, f"aost{i}", [128, 512], BF16) for i in range(2)]
                    dma(SP, lambda: yE.dma_start(out=lm[:], in_=lam_r[j]), writes=[lmB])
                    dma(SP, lambda: yE.dma_start(out=gs[:], in_=subg_r[j]), writes=[gsB])
                    dma(SP, lambda: yE.dma_start(out=mk[:], in_=maskb), writes=[mkB])
                    for i in range(2):
                        op(POOL, lambda i=i: gE.memset(vas[i][0][:], 1.0), writes=[vas[i][1]])
                    op(DVE, lambda: vE.tensor_scalar_mul(gs[:], gs[:], 1.0 - lam_init), reads=[gsB], writes=[gsB])
                    op(DVE, lambda: vE.tensor_tensor(sq[:, 0:64], lm[:, 0:64], lm[:, 64:128], ALU.mult), reads=[lmB], writes=[sqB])
                    op(DVE, lambda: vE.tensor_tensor(sq[:, 64:128], lm[:, 128:192], lm[:, 192:256], ALU.mult), reads=[lmB], writes=[sqB])
                    op(DVE, lambda: vE.reduce_sum(sm[:, 0:1], sq[:, 0:64], AX.X), reads=[sqB], writes=[smB])
                    op(DVE, lambda: vE.reduce_sum(sm[:, 1:2], sq[:, 64:128], AX.X), reads=[sqB], writes=[smB])
                    op(ACT, lambda: sE.activation(sm[:, 2:4], sm[:, 0:2], AF.Exp), reads=[smB], writes=[smB])
                    op(DVE, lambda: vE.tensor_tensor(sm[:, 4:5], sm[:, 3:4], sm[:, 2:3], ALU.subtract), reads=[smB], writes=[smB])
                    op(DVE, lambda: vE.tensor_scalar_add(sm[:, 5:6], sm[:, 4:5], -lam_init), reads=[smB], writes=[smB])
                    Vv = Vd.rearrange("(kb p) e -> p kb e", p=128)
                    P2 = [sbt(es, f"ap2{i}", [128, 1024], BF16) for i in range(3)]
                    accS, accSB = sbt(es, "accS", [128, 3, 512], F32)

                    def a_load(h):
                        qT, qTB = qTs[h % 2]
                        kT, kTB = kTs[h % 2]
                        va, vaB = vas[h % 2]
                        dma(SP, lambda: yE.dma_start(out=qT[:], in_=QT[h]), writes=[qTB])
                        dma(SP, lambda: yE.dma_start(out=kT[:], in_=KT[h]), writes=[kTB])
                        dma(SP, lambda: yE.dma_start(out=va[:, :, 0:128], in_=Vv[:, :, h * 128:(h + 1) * 128]), writes=[vaB])

                    steps = [(h, qg, kb) for h in range(16) for qg in range(8) for kb in range(32)]
                    accs = {}
                    for qb_ in range(4):
                        for c in range(2):
                            a = qb_ * 2 + c
                            accs[(qb_, c)] = (4 + a // 3, (a % 3) * 130)

                    def emit_qk(idx):
                        h, qg, kb = steps[idx]
                        qT, qTB = qTs[h % 2]
                        kT, kTB = kTs[h % 2]
                        sp = idx % 2
                        for c in range(2):
                            bk = 2 * sp + c
                            op(PE, lambda c=c, bk=bk: tE.matmul(PS[bk][:], lhsT=kT[64 * c:64 * c + 64, kb * 128:(kb + 1) * 128], rhs=qT[64 * c:64 * c + 64, qg * 512:(qg + 1) * 512], start=True, stop=True),
                               reads=[kTB, qTB], writes=[PB[bk]])

                    def emit_exp(idx):
                        h, qg, kb = steps[idx]
                        sp = idx % 2
                        P, PBf = P2[idx % 3]
                        seg = qg // 4
                        op(ACT, lambda: sE.activation(P[:], PSALL[:, sp * 1024:(sp + 1) * 1024], AF.Exp, bias=mk[:, seg, kb:kb + 1], scale=0.125),
                           reads=[PB[2 * sp], PB[2 * sp + 1], mkB], writes=[PBf])

                    def emit_pv(idx):
                        h, qg, kb = steps[idx]
                        va, vaB = vas[h % 2]
                        P, PBf = P2[idx % 3]
                        for c in range(2):
                            for qb_ in range(4):
                                bk, off = accs[(qb_, c)]
                                op(PE, lambda bk=bk, off=off, qb_=qb_, c=c: tE.matmul(PS[bk][:, off:off + 129], lhsT=P[:, c * 512 + qb_ * 128:c * 512 + (qb_ + 1) * 128], rhs=va[:, kb, 0:129], start=(kb == 0 and c == 0 and qb_ in (0, 2, 3)), stop=(kb == 31), skip_group_check=True),
                                   reads=[PBf, vaB], writes=[PB[bk]])

                    pending = []

                    def emit_epilogue(h, qg, ocnt):
                        ostg, ostgB = ostgs[ocnt % 2]
                        for b3 in range(3):
                            op(DVE, lambda b3=b3: vE.tensor_copy(accS[:, b3, :], PS[4 + b3][:]), reads=[PB[4 + b3]], writes=[accSB])
                        for qb_ in range(4):
                            b1, o1 = accs[(qb_, 0)]
                            b2, o2 = accs[(qb_, 1)]
                            ob, obB = obs[qb_]
                            A1, A2 = accS[:, b1 - 4, :], accS[:, b2 - 4, :]
                            op(DVE, lambda A1=A1, o1=o1: vE.reciprocal(sm[:, 8:9], A1[:, o1 + 128:o1 + 129]), reads=[accSB], writes=[smB])
                            op(DVE, lambda A2=A2, o2=o2: vE.reciprocal(sm[:, 9:10], A2[:, o2 + 128:o2 + 129]), reads=[accSB], writes=[smB])
                            op(DVE, lambda: vE.tensor_tensor(sm[:, 10:11], sm[:, 9:10], sm[:, 5:6], ALU.mult), reads=[smB], writes=[smB])
                            op(DVE, lambda A1=A1, o1=o1: vE.tensor_scalar_mul(of[:], A1[:, o1:o1 + 128], sm[:, 8:9]), reads=[accSB, smB], writes=[ofB])
                            op(DVE, lambda A2=A2, o2=o2: vE.scalar_tensor_tensor(of[:], A2[:, o2:o2 + 128], sm[:, 10:11], of[:], ALU.mult, ALU.add), reads=[accSB, smB, ofB], writes=[ofB])
                            op(DVE, lambda: vE.tensor_tensor(sq[:], of[:], of[:], ALU.mult), reads=[ofB], writes=[sqB])
                            op(DVE, lambda: vE.reduce_sum(sm[:, 11:12], sq[:], AX.X), reads=[sqB], writes=[smB])
                            op(DVE, lambda: vE.tensor_scalar(sm[:, 12:13], sm[:, 11:12], 1.0 / 128, EPS, ALU.mult, ALU.add), reads=[smB], writes=[smB])
                            op(ACT, lambda: sE.sqrt(sm[:, 13:14], sm[:, 12:13]), reads=[smB], writes=[smB])
                            op(DVE, lambda: vE.reciprocal(sm[:, 14:15], sm[:, 13:14]), reads=[smB], writes=[smB])
                            op(DVE, lambda ob=ob: vE.scalar_tensor_tensor(ob[:], of[:], sm[:, 14:15], gs[:], ALU.mult, ALU.mult), reads=[ofB, smB, gsB], writes=[obB])

                        def part2():
                            tp = PS[7][:].bitcast(BF16)
                            for qb_ in range(4):
                                ob, obB = obs[qb_]
                                op(PE, lambda qb_=qb_, ob=ob: tE.transpose(tp[:, qb_ * 128:(qb_ + 1) * 128], ob[:], ident[:]), reads=[obB, cb], writes=[PB[7]])
                            op(DVE, lambda: vE.tensor_copy(ostg[:], tp[:, 0:512]), reads=[PB[7]], writes=[ostgB])
                            dma(SP, lambda: yE.dma_start(out=OT[h][:, qg * 512:(qg + 1) * 512], in_=ostg[:]), reads=[ostgB])
                        pending.append(part2)

                    a_load(0)
                    emit_qk(0)
                    emit_qk(1)
                    ocnt = 0
                    for idx, (h, qg, kb) in enumerate(steps):
                        if qg == 0 and kb == 0 and h + 1 < 16:
                            a_load(h + 1)
                        emit_exp(idx)
                        emit_pv(idx)
                        if idx + 2 < len(steps):
                            emit_qk(idx + 2)
                        if kb == 8 and pending:
                            pending.pop(0)()
                        if kb == 31:
                            emit_epilogue(h, qg, ocnt)
                            ocnt += 1
                    while pending:
                        pending.pop(0)()
                    barrier()
            else:
                with ExitStack() as es:
                    q4s = [sbt(es, f"wq{i}", [128, 4, T], BF16) for i in range(2)]
                    kTs = [sbt(es, f"wk{i}", [128, T], BF16) for i in range(2)]
                    vas = [sbt(es, f"wv{i}", [128, 32, 130], BF16) for i in range(2)]
                    Pb = [sbt(es, f"wp{i}", [128, 512], BF16) for i in range(3)]
                    tm, tmB = sbt(es, "tm", [128, 2, 512], BF16)
                    sk, skB = sbt(es, "sk", [128, 16], F32)
                    sm, smB = sbt(es, "wsm", [128, 8], F32)
                    obs = [sbt(es, f"wob{i}", [128, 128], BF16) for i in range(2)]
                    ostgs = [sbt(es, f"wost{i}", [128, 4, 128], BF16) for i in range(2)]
                    dma(SP, lambda: yE.dma_start(out=tm[:], in_=trimask.rearrange("c p f -> p c f")), writes=[tmB])
                    dma(SP, lambda: yE.dma_start(out=sk[:], in_=sink_r[j]), writes=[skB])
                    op(ACT, lambda: sE.activation(sk[:], sk[:], AF.Exp), reads=[skB], writes=[skB])
                    for i in range(2):
                        op(POOL, lambda i=i: gE.memset(vas[i][0][:], 1.0), writes=[vas[i][1]])
                    Vv = Vd.rearrange("(kb p) e -> p kb e", p=128)
                    scnt = 0
                    ocnt = 0
                    for g in range(4):
                        q4, q4B = q4s[g % 2]
                        kT, kTB = kTs[g % 2]
                        va, vaB = vas[g % 2]
                        dma(SP, lambda q4=q4, g=g: yE.dma_start(out=q4[:], in_=QT[4 * g:4 * g + 4].rearrange("h p t -> p h t")), writes=[q4B])
                        dma(SP, lambda kT=kT, g=g: yE.dma_start(out=kT[:], in_=KT[g]), writes=[kTB])
                        dma(SP, lambda va=va, g=g: yE.dma_start(out=va[:, :, 0:128], in_=Vv[:, :, g * 128:(g + 1) * 128]), writes=[vaB])
                        for n in range(32):
                            kbs = [kb for kb in (n - 1, n, n + 1) if 0 <= kb < 32]
                            accb = [(4, 0), (4, 130), (4, 260), (5, 0)]
                            for idx, kb in enumerate(kbs):
                                sb_ = scnt % 4
                                P, PBf = Pb[scnt % 3]
                                scnt += 1
                                S = PS[sb_]
                                op(PE, lambda S=S, kT=kT, q4=q4, kb=kb, n=n: tE.matmul(S[:], lhsT=kT[:, kb * 128:(kb + 1) * 128], rhs=q4[:, :, n * 128:(n + 1) * 128], start=True, stop=True),
                                   reads=[kTB, q4B], writes=[PB[sb_]])
                                cross = (n, kb) in ((15, 16), (16, 15))
                                if cross:
                                    op(ACT, lambda P=P, S=S: sE.activation(P[:], S[:], AF.Exp, bias=flg[:, 1:2], scale=128 ** -0.5), reads=[PB[sb_], cb], writes=[PBf])
                                else:
                                    op(ACT, lambda P=P, S=S: sE.activation(P[:], S[:], AF.Exp, scale=128 ** -0.5), reads=[PB[sb_]], writes=[PBf])
                                if kb != n:
                                    mi = 0 if kb < n else 1
                                    op(POOL, lambda P=P, mi=mi: gE.tensor_tensor(P[:], P[:], tm[:, mi, :], ALU.mult), reads=[PBf, tmB], writes=[PBf])
                                for r in range(4):
                                    bk, off = accb[r]
                                    op(PE, lambda bk=bk, off=off, P=P, r=r, va=va, kb=kb, idx=idx, kbs=kbs: tE.matmul(PS[bk][:, off:off + 129], lhsT=P[:, r * 128:(r + 1) * 128], rhs=va[:, kb, 0:129], start=(idx == 0 and r in (0, 3)), stop=(idx == len(kbs) - 1), skip_group_check=True),
                                       reads=[PBf, vaB], writes=[PB[bk]])
                            ostg, ostgB = ostgs[ocnt % 2]
                            ocnt += 1
                            tp = PS[7][:].bitcast(BF16)
                            for r in range(4):
                                bk, off = accb[r]
                                ob, obB = obs[r % 2]
                                A = PS[bk]
                                hq = 4 * g + r
                                op(DVE, lambda A=A, off=off, hq=hq: vE.tensor_tensor(sm[:, 0:1], A[:, off + 128:off + 129], sk[:, hq:hq + 1], ALU.add), reads=[PB[bk], skB], writes=[smB])
                                op(DVE, lambda: vE.reciprocal(sm[:, 1:2], sm[:, 0:1]), reads=[smB], writes=[smB])
                                op(DVE, lambda A=A, off=off, ob=ob: vE.tensor_scalar_mul(ob[:], A[:, off:off + 128], sm[:, 1:2]), reads=[PB[bk], smB], writes=[obB])
                                op(PE, lambda tp=tp, r=r, ob=ob: tE.transpose(tp[:, r * 128:(r + 1) * 128], ob[:], ident[:]), reads=[obB, cb], writes=[PB[7]])
                            op(DVE, lambda tp=tp, ostg=ostg: vE.tensor_copy(ostg[:], tp[:, 0:512].rearrange("p (a b) -> p a b", a=4)), reads=[PB[7]], writes=[ostgB])
                            dma(SP, lambda ostg=ostg, g=g, n=n: yE.dma_start(out=OT[4 * g:4 * g + 4].rearrange("h p t -> p h t")[:, :, n * 128:(n + 1) * 128], in_=ostg[:]), reads=[ostgB])
                    barrier()
            w_out = (diff_w_out if is_diff else win_w_out)[j].rearrange("(k p) n -> p k n", p=128)
            with ExitStack() as es:
                wo, woB = sbt(es, "wo", [128, 16, D], BF16)
                Gt, gB = sbt(es, "lng", [128, D], F32)
                Bt, _ = sbt(es, "lnb", [128, D], F32)
                ots = [sbt(es, f"oot{i}", [128, 16, 128], BF16) for i in range(2)]
                xrs = [sbt(es, f"oxr{i}", [128, D], F32) for i in range(2)]
                lts = [ln_tiles(es, f"o{i}") for i in range(2)]
                for q in range(4):
                    dma(POOL, lambda q=q: gE.dma_start(out=wo[:, 4 * q:4 * q + 4, :], in_=w_out[:, 4 * q:4 * q + 4, :]), writes=[woB])
                dma(SP, lambda: yE.dma_start(out=Gt[:], in_=lnm_g[L]), writes=[gB])
                dma(SP, lambda: yE.dma_start(out=Bt[:], in_=lnm_b[L]), writes=[gB])
                cnt = 0
                opend = []

                def op_load(tb):
                    ot, otB = ots[tb % 2]
                    xr, xrB = xrs[tb % 2]
                    dma(SP, lambda: yE.dma_start(out=ot[:], in_=OT.rearrange("h p t -> p h t")[:, :, tb * 128:(tb + 1) * 128]), writes=[otB])
                    dma(SP, lambda: yE.dma_start(out=xr[:], in_=Xsrc[tb * 128:(tb + 1) * 128, :]), writes=[xrB])
                op_load(0)
                for tb in range(NTB):
                    ot, otB = ots[tb % 2]
                    xr, xrB = xrs[tb % 2]
                    if tb + 1 < NTB:
                        op_load(tb + 1)
                    for nt in range(4):
                        bk = cnt % 6
                        cnt += 1
                        ps = PS[bk]
                        for h in range(16):
                            op(PE, lambda ps=ps, ot=ot, h=h, nt=nt: tE.matmul(ps[:], lhsT=ot[:, h, :], rhs=wo[:, h, nt * 512:(nt + 1) * 512], start=(h == 0), stop=(h == 15)),
                               reads=[otB, woB], writes=[PB[bk]])
                        op(DVE, lambda xr=xr, ps=ps, nt=nt: vE.scalar_tensor_tensor(xr[:, nt * 512:(nt + 1) * 512], xr[:, nt * 512:(nt + 1) * 512], ALPHA, ps[:], ALU.mult, ALU.add),
                           reads=[PB[bk], xrB], writes=[xrB])
                    while opend:
                        opend.pop(0)()
                    layer_norm(es, xr[:], xrB, Gt, Bt, gB, X2[tb * 128:(tb + 1) * 128, :], XT2, tb, [6, 7], lts[tb % 2], defer=opend)
                while opend:
                    opend.pop(0)()
                barrier()
            wup = ffn_w_up[L].rearrange("(k p) n -> p k n", p=128)
            wdn = ffn_w_down[L].rearrange("(i p) n -> p i n", p=128)
            Xdst = y if L == DEPTH - 1 else X
            with ExitStack() as es:
                xt, xtB = sbt(es, "fxt", [128, 16, 514], BF16)
                xr4, xr4B = sbt(es, "fxr", [128, 4, D], F32)
                wgs = [sbt(es, f"fwg{i}", [128, 16, 512], BF16) for i in range(2)]
                wus = [sbt(es, f"fwu{i}", [128, 16, 512], BF16) for i in range(2)]
                tgs = [sbt(es, f"ftg{i}", [128, 512], F32) for i in range(2)]
                tus = [sbt(es, f"ftu{i}", [128, 512], F32) for i in range(2)]
                sgs = [sbt(es, f"fsg{i}", [128, 512], F32) for i in range(2)]
                aT, aTB = sbt(es, "faT", [128, 11, 512], BF16)
                wds = [sbt(es, f"fwd{i}", [128, 11, 512], BF16) for i in range(2)]
                cp, cpB = sbt(es, "fcp", [128, 2 * NFC, 4], F32)
                Gt, gB = sbt(es, "flng", [128, D], F32)
                Bt, _ = sbt(es, "flnb", [128, D], F32)
                lts = [ln_tiles(es, f"f{i}") for i in range(2)]
                dma(SP, lambda: yE.dma_start(out=cp[:], in_=convp[L]), writes=[cpB])
                dma(SP, lambda: yE.dma_start(out=Gt[:], in_=lnf_g[L]), writes=[gB])
                dma(SP, lambda: yE.dma_start(out=Bt[:], in_=lnf_b[L]), writes=[gB])
                XT2v = xt_view(XT2)

                def load_xt(tg):
                    t0 = tg * 512
                    dma(SP, lambda: yE.dma_start(out=xt[:, :, 1:513], in_=XT2v[:, :, t0:t0 + 512]), writes=[xtB])
                    with nc.allow_non_contiguous_dma(reason="halo column"):
                        if tg > 0:
                            dma(SP, lambda: yE.dma_start(out=xt[:, :, 0:1], in_=XT2v[:, :, t0 - 1:t0]), writes=[xtB])
                        else:
                            op(DVE, lambda: vE.memset(xt[:, :, 0:1], 0.0), writes=[xtB])
                        if tg < 7:
                            dma(SP, lambda: yE.dma_start(out=xt[:, :, 513:514], in_=XT2v[:, :, t0 + 512:t0 + 513]), writes=[xtB])
                        else:
                            op(DVE, lambda: vE.memset(xt[:, :, 513:514], 0.0), writes=[xtB])
                    if tg == 4:
                        op(DVE, lambda: vE.tensor_scalar_mul(xt[:, :, 0:1], xt[:, :, 0:1], flg[:, 0:1]), reads=[xtB, cb], writes=[xtB])
                    if tg == 3:
                        op(DVE, lambda: vE.tensor_scalar_mul(xt[:, :, 513:514], xt[:, :, 513:514], flg[:, 0:1]), reads=[xtB, cb], writes=[xtB])

                ccnt = 0
                wcnt = 0
                dcn = 0
                ycnt = 0
                load_xt(0)
                for tg in range(8):
                    t0 = tg * 512
                    dma(SP, lambda t0=t0: yE.dma_start(out=xr4[:], in_=X2[t0:t0 + 512, :].rearrange("(a p) d -> p a d", p=128)), writes=[xr4B])
                    for qf in range(4):
                        c0 = qf * 11
                        c1 = min(NFC, c0 + 11)
                        ca = c0
                        while ca < c1:
                            ncw = min(4, c1 - ca)
                            wg, wgB = wgs[wcnt % 2]
                            wu, wuB = wus[wcnt % 2]
                            wcnt += 1
                            dma(POOL, lambda wg=wg, ca=ca, ncw=ncw: gE.dma_start(out=wg[:, :, 0:ncw * 128], in_=wup[:, :, ca * 128:(ca + ncw) * 128]), writes=[wgB])
                            dma(POOL, lambda wu=wu, ca=ca, ncw=ncw: gE.dma_start(out=wu[:, :, 0:ncw * 128], in_=wup[:, :, FF + ca * 128:FF + (ca + ncw) * 128]), writes=[wuB])
                            for ii in range(ncw):
                                i = ca + ii
                                tg_, tgB = tgs[ccnt % 2]
                                tu_, tuB = tus[ccnt % 2]
                                sg_, sgB = sgs[ccnt % 2]
                                gb, ub, hb = ccnt % 2, 2 + ccnt % 2, 4 + ccnt % 2
                                ccnt += 1
                                for (w_, wB_, bnk, hoff) in ((wg, wgB, gb, 0), (wu, wuB, ub, 2)):
                                    for kc in range(16):
                                        op(PE, lambda w_=w_, bnk=bnk, kc=kc, ii=ii: tE.matmul(PS[bnk][:], lhsT=w_[:, kc, ii * 128:(ii + 1) * 128], rhs=xt[:, kc, 1:513], start=(kc == 0), stop=(kc == 15)),
                                           reads=[wB_, xtB], writes=[PB[bnk]])
                                    for kc in range(16):
                                        op(PE, lambda w_=w_, hb=hb, hoff=hoff, kc=kc, ii=ii: tE.matmul(PS[hb][:, hoff:hoff + 2], lhsT=w_[:, kc, ii * 128:(ii + 1) * 128], rhs=xt[:, kc, 0:514:513], start=(kc == 0), stop=(kc == 15)),
                                           reads=[wB_, xtB], writes=[PB[hb]])
                                for (tt, ttB, bnk, hoff, ci) in ((tg_, tgB, gb, 0, i), (tu_, tuB, ub, 2, NFC + i)):
                                    Gp, Hp = PS[bnk], PS[hb]
                                    w0, w1, w2, bb = cp[:, ci, 0:1], cp[:, ci, 1:2], cp[:, ci, 2:3], cp[:, ci, 3:4]
                                    op(DVE, lambda tt=tt, Gp=Gp, w1=w1, bb=bb: vE.tensor_scalar(tt[:], Gp[:], w1, bb, ALU.mult, ALU.add), reads=[PB[bnk], cpB], writes=[ttB])
                                    op(DVE, lambda tt=tt, Gp=Gp, w0=w0: vE.scalar_tensor_tensor(tt[:, 1:512], Gp[:, 0:511], w0, tt[:, 1:512], ALU.mult, ALU.add), reads=[PB[bnk], cpB, ttB], writes=[ttB])
                                    op(DVE, lambda tt=tt, Hp=Hp, w0=w0, hoff=hoff: vE.scalar_tensor_tensor(tt[:, 0:1], Hp[:, hoff:hoff + 1], w0, tt[:, 0:1], ALU.mult, ALU.add), reads=[PB[hb], cpB, ttB], writes=[ttB])
                                    op(DVE, lambda tt=tt, Gp=Gp, w2=w2: vE.scalar_tensor_tensor(tt[:, 0:511], Gp[:, 1:512], w2, tt[:, 0:511], ALU.mult, ALU.add), reads=[PB[bnk], cpB, ttB], writes=[ttB])
                                    op(DVE, lambda tt=tt, Hp=Hp, w2=w2, hoff=hoff: vE.scalar_tensor_tensor(tt[:, 511:512], Hp[:, hoff + 1:hoff + 2], w2, tt[:, 511:512], ALU.mult, ALU.add), reads=[PB[hb], cpB, ttB], writes=[ttB])
                                op(ACT, lambda sg_=sg_, tg_=tg_: sE.activation(sg_[:], tg_[:], AF.Silu), reads=[tgB], writes=[sgB])
                                op(DVE, lambda sg_=sg_, tu_=tu_, i=i, c0=c0: vE.tensor_tensor(aT[:, i - c0, :], sg_[:], tu_[:], ALU.mult), reads=[sgB, tuB], writes=[aTB])
                            ca += ncw
                        if qf == 3 and tg < 7:
                            load_xt(tg + 1)
                        nch = c1 - c0
                        for nt in range(4):
                            wd, wdB = wds[dcn % 2]
                            dcn += 1
                            dma(POOL, lambda wd=wd, nt=nt, c0=c0, nch=nch: gE.dma_start(out=wd[:, 0:nch, :], in_=wdn[:, c0:c0 + nch, nt * 512:(nt + 1) * 512]), writes=[wdB])
                            for tb in range(4):
                                bk = 6 + ycnt % 2
                                ycnt += 1
                                for jj in range(nch):
                                    op(PE, lambda bk=bk, jj=jj, tb=tb, wd=wd, nch=nch: tE.matmul(PS[bk][:], lhsT=aT[:, jj, tb * 128:(tb + 1) * 128], rhs=wd[:, jj, :], start=(jj == 0), stop=(jj == nch - 1)),
                                       reads=[aTB, wdB], writes=[PB[bk]])
                                dst = xr4[:, tb, nt * 512:(nt + 1) * 512]
                                if qf == 0:
                                    op(DVE, lambda dst=dst, bk=bk: vE.scalar_tensor_tensor(dst, dst, ALPHA, PS[bk][:], ALU.mult, ALU.add), reads=[PB[bk], xr4B], writes=[xr4B])
                                else:
                                    op(DVE, lambda dst=dst, bk=bk: vE.tensor_tensor(dst, dst, PS[bk][:], ALU.add), reads=[PB[bk], xr4B], writes=[xr4B])
                    fpend = []
                    for tb in range(4):
                        tbg = tg * 4 + tb
                        layer_norm(es, xr4[:, tb, :], xr4B, Gt, Bt, gB, Xdst[tbg * 128:(tbg + 1) * 128, :], XT, tbg, [6, 7], lts[tb % 2], defer=fpend)
                        if len(fpend) > 1:
                            fpend.pop(0)()
                    while fpend:
                        fpend.pop(0)()
                barrier()
        barrier()

    return nc


def _rope_rep(pos, rot, m):
    inv = (np.float32(500000.0) ** (-(np.arange(0, rot, 2, dtype=np.float32)) / np.float32(rot))).astype(np.float32)
    ang = (pos[:, None].astype(np.float32) * inv[None, :]).astype(np.float32)
    c, s = np.cos(ang).astype(np.float32), np.sin(ang).astype(np.float32)
    rep = lambda a: np.tile(a[:, None, :], (1, m, 1)).reshape(T, -1)
    out = np.stack([rep(c), rep(s)], 0)
    return np.ascontiguousarray(out.reshape(2, NTB, 128, -1).transpose(0, 2, 1, 3))


def kernel(**inp):
    f = lambda k: np.ascontiguousarray(np.asarray(inp[k], dtype=np.float32))
    xp, xs = f("x_prompt"), f("x_sample")
    rep = lambda a: np.ascontiguousarray(np.broadcast_to(a[:, None, :], (a.shape[0], 128, a.shape[1])))
    cw, cbias = f("ffn_conv_w"), f("ffn_conv_b")
    cpar = np.concatenate([cw, cbias[:, None, :]], axis=1)
    convp = np.ascontiguousarray(cpar.reshape(4, 4, 2 * NFC, 128).transpose(0, 3, 2, 1))
    shared = {
        "diff_w_in": f("diff_w_in"), "diff_w_out": f("diff_w_out"), "win_w_in": f("win_w_in"),
        "win_w_out": f("win_w_out"), "ffn_w_up": f("ffn_w_up"), "ffn_w_down": f("ffn_w_down"),
        "lam_r": rep(f("diff_lam").reshape(2, 256)), "subg_r": rep(f("diff_subln_g")), "sink_r": rep(f("win_sink")),
        "lnm_g": rep(f("ln_mix_g")), "lnm_b": rep(f("ln_mix_b")), "lnf_g": rep(f("ln_ffn_g")), "lnf_b": rep(f("ln_ffn_b")),
        "convp": convp, "ident": np.eye(128).astype(ml_dtypes.bfloat16),
    }
    jj, ii = np.meshgrid(np.arange(128), np.arange(128), indexing="ij")
    mL = (ii <= jj).astype(np.float32)
    mU = (jj <= ii).astype(np.float32)
    shared["trimask"] = np.stack([np.tile(mL, (1, 4)), np.tile(mU, (1, 4))], 0).astype(ml_dtypes.bfloat16)
    in_maps = []
    for c in range(8):
        two = c < 4
        if two:
            xc = np.concatenate([xp[2 * c], xp[2 * c + 1]], 0)
            pos = np.concatenate([np.arange(2048), np.arange(2048)])
        else:
            xc = xs[c % 2]
            pos = np.arange(4096)
        mb = np.zeros((128, 2, 32), np.float32)
        if two:
            mb[:, 0, 16:] = NEG
            mb[:, 1, :16] = NEG
        fl = np.zeros((128, 2), np.float32)
        fl[:, 0] = 0.0 if two else 1.0
        fl[:, 1] = NEG if two else 0.0
        d = dict(shared)
        d.update({"xin": np.ascontiguousarray(xc), "ropd": _rope_rep(pos, 16, 8), "ropw": _rope_rep(pos, 32, 4), "maskb": mb, "flags": fl})
        in_maps.append(d)
    nc = build()
    res = run_bass_kernel_spmd(nc, in_maps, core_ids=list(range(8)))
    ys = [np.asarray(r["y"], dtype=np.float32) for r in res.results]
    y_prompt = np.stack([ys[c // 2][(c % 2) * 2048:(c % 2 + 1) * 2048] for c in range(8)], 0)
    y_sample = np.stack([ys[4], ys[5]], 0)
    return (y_prompt, y_sample)
```

```python
import math
from contextlib import ExitStack
import numpy as np
import ml_dtypes
import concourse.bass as bass
import concourse.mybir as mybir
from concourse.bass_utils import run_bass_kernel_spmd

F32 = mybir.dt.float32
BF16 = mybir.dt.bfloat16
AF = mybir.ActivationFunctionType
ALU = mybir.AluOpType
AX = mybir.AxisListType

T = 4096
D = 2048
NTB = 32
FF = 5504
NFC = 43
DEPTH = 4
ALPHA = (2 * DEPTH) ** 0.25
EPS = 1e-5
NEG = -30000.0
KD = 6


class Buf:
    def __init__(self, excl=False):
        self.w = None
        self.r = {}
        self.excl = excl


class Eng:
    def __init__(self, sem, sid):
        self.sem, self.sid, self.n, self.seen, self.q = sem, sid, 0, {}, []
        self.dsems, self.dcnt, self.k = [], [], 0

    def wait(self, dep):
        if dep is None:
            return
        sid, sem, val = dep
        if self.seen.get(sid, 0) >= val:
            return
        self.e.wait_ge(sem, val)
        self.seen[sid] = val


def _deps(E, reads, writes):
    for b in reads:
        E.wait(b.w)
        if b.excl:
            for sid, d in b.r.items():
                if sid != E.sid:
                    E.wait(d)
    for b in writes:
        if b.w is not None and b.w[0] != E.sid:
            E.wait(b.w)
        for sid, d in b.r.items():
            if sid != E.sid:
                E.wait(d)


def _mark(d, reads, writes):
    for b in reads:
        b.r[d[0]] = d
    for b in writes:
        b.w = d
        b.r = {}


def op(E, fn, reads=(), writes=()):
    _deps(E, reads, writes)
    E.n += 1
    fn().then_inc(E.sem, 1)
    _mark((E.sid, E.sem, E.n), reads, writes)


def dma(Q, fn, reads=(), writes=()):
    slot = Q.k % KD
    Q.k += 1
    sid = 100 + Q.sid * 10 + slot
    sem = Q.dsems[slot]
    Q.wait((sid, sem, Q.dcnt[slot] * 16))
    for b in reads:
        Q.wait(b.w)
    for b in writes:
        Q.wait(b.w)
        for d in b.r.values():
            Q.wait(d)
    fn().then_inc(sem, 16)
    Q.dcnt[slot] += 1
    _mark((sid, sem, Q.dcnt[slot] * 16), reads, writes)


def build():
    nc = bass.Bass("TRN2", target_bir_lowering=False)
    dt = lambda n, s, d, k="ExternalInput": nc.dram_tensor(n, s, d, kind=k).ap()
    xin = dt("xin", [T, D], F32)
    diff_w_in = dt("diff_w_in", [2, D, 6144], F32)
    diff_w_out = dt("diff_w_out", [2, D, D], F32)
    win_w_in = dt("win_w_in", [2, D, 3072], F32)
    win_w_out = dt("win_w_out", [2, D, D], F32)
    ffn_w_up = dt("ffn_w_up", [4, D, 2 * FF], F32)
    ffn_w_down = dt("ffn_w_down", [4, FF, D], F32)
    lam_r = dt("lam_r", [2, 128, 256], F32)
    subg_r = dt("subg_r", [2, 128, 128], F32)
    sink_r = dt("sink_r", [2, 128, 16], F32)
    lnm_g = dt("lnm_g", [4, 128, D], F32)
    lnm_b = dt("lnm_b", [4, 128, D], F32)
    lnf_g = dt("lnf_g", [4, 128, D], F32)
    lnf_b = dt("lnf_b", [4, 128, D], F32)
    convp = dt("convp", [4, 128, 2 * NFC, 4], F32)
    ropd = dt("ropd", [2, 128, NTB, 64], F32)
    ropw = dt("ropw", [2, 128, NTB, 64], F32)
    maskb = dt("maskb", [128, 2, 32], F32)
    flags = dt("flags", [128, 2], F32)
    ident_d = dt("ident", [128, 128], BF16)
    trimask = dt("trimask", [2, 128, 512], BF16)
    y = dt("y", [T, D], F32, "ExternalOutput")
    X = nc.dram_tensor("X", [T, D], F32).ap()
    X2 = nc.dram_tensor("X2", [T, D], F32).ap()
    XT = nc.dram_tensor("XT", [16, 128, T], BF16).ap()
    XT2 = nc.dram_tensor("XT2", [16, 128, T], BF16).ap()
    QT = nc.dram_tensor("QT", [16, 128, T], BF16).ap()
    KT = nc.dram_tensor("KT", [16, 128, T], BF16).ap()
    Vd = nc.dram_tensor("Vd", [T, D], BF16).ap()
    OT = nc.dram_tensor("OT", [16, 128, T], BF16).ap()

    top = ExitStack()
    with top:
        nsem = 5 + 2 * KD
        sems = [top.enter_context(nc.semaphore(f"s{i}")) for i in range(nsem)]
        PE, ACT, DVE, POOL, SP = [Eng(sems[i], i) for i in range(5)]
        ENGS = [PE, ACT, DVE, POOL, SP]
        for qi, Q in enumerate((POOL, SP)):
            Q.dsems = sems[5 + qi * KD: 5 + (qi + 1) * KD]
            Q.dcnt = [0] * KD
        tE, sE, vE, gE, yE = nc.tensor, nc.scalar, nc.vector, nc.gpsimd, nc.sync
        for E_, h_ in zip(ENGS, (tE, sE, vE, gE, yE)):
            E_.e = h_

        def barrier():
            for E in ENGS:
                for F in ENGS:
                    if F is not E and F.n > 0:
                        E.wait((F.sid, F.sem, F.n))
                for Q in (POOL, SP):
                    for s in range(KD):
                        if Q.dcnt[s]:
                            E.wait((100 + Q.sid * 10 + s, Q.dsems[s], Q.dcnt[s] * 16))

        PSALL = top.enter_context(nc.psum_tensor("psall", [128, 8 * 512], F32))

        class _Bank:
            def __init__(self, i):
                self.i = i

            def __getitem__(self, key):
                return PSALL[:, self.i * 512:(self.i + 1) * 512][key]
        PS = [_Bank(i) for i in range(8)]
        PB = [Buf(excl=True) for _ in range(8)]
        ident = top.enter_context(nc.sbuf_tensor("identt", [128, 128], BF16))
        flg = top.enter_context(nc.sbuf_tensor("flg", [128, 2], F32))
        cb = Buf()
        dma(SP, lambda: yE.dma_start(out=ident[:], in_=ident_d), writes=[cb])
        dma(SP, lambda: yE.dma_start(out=flg[:], in_=flags), writes=[cb])

        uid = [0]

        def sbt(es, name, shape, dtype):
            uid[0] += 1
            return es.enter_context(nc.sbuf_tensor(f"{name}_{uid[0]}", shape, dtype)), Buf()

        def transpose_store(xb, xbB, dst3, stg, stgB, tpi):
            for q4 in range(4):
                bk = tpi[q4 % 2]
                tp = PS[bk][:].bitcast(BF16)
                for j in range(4):
                    kc = q4 * 4 + j
                    op(PE, lambda tp=tp, j=j, kc=kc: tE.transpose(tp[:, j * 128:(j + 1) * 128], xb[:, kc * 128:(kc + 1) * 128], ident[:]),
                       reads=[xbB, cb], writes=[PB[bk]])
                op(DVE, lambda tp=tp, q4=q4: vE.tensor_copy(stg[:, q4 * 4:(q4 + 1) * 4, :], tp[:, 0:512].rearrange("p (a b) -> p a b", a=4)),
                   reads=[PB[bk]], writes=[stgB])
            dma(SP, lambda: yE.dma_start(out=dst3, in_=stg[:]), reads=[stgB])

        def xt_view(XTd):
            return XTd.rearrange("k p t -> p k t")

        def layer_norm(es_t, xr, xrB, Gt, Bt, gB, Xrows, XTd, tb, tpi, tiles, defer=None):
            st, stB, mv, mvB, sc, scB, xb, xbB, stg, stgB = tiles
            for c in range(4):
                op(DVE, lambda c=c: vE.bn_stats(st[:, c, :], xr[:, c * 512:(c + 1) * 512]), reads=[xrB], writes=[stB])
            op(DVE, lambda: vE.bn_aggr(mv[:], st[:].rearrange("p a b -> p (a b)")), reads=[stB], writes=[mvB])
            op(DVE, lambda: vE.tensor_scalar_add(sc[:, 0:1], mv[:, 1:2], EPS), reads=[mvB], writes=[scB])
            op(ACT, lambda: sE.sqrt(sc[:, 1:2], sc[:, 0:1]), reads=[scB], writes=[scB])
            op(DVE, lambda: vE.reciprocal(sc[:, 2:3], sc[:, 1:2]), reads=[scB], writes=[scB])
            op(DVE, lambda: vE.tensor_scalar(xr, xr, mv[:, 0:1], sc[:, 2:3], ALU.subtract, ALU.mult), reads=[xrB, mvB, scB], writes=[xrB])
            op(DVE, lambda: vE.tensor_tensor(xr, xr, Gt[:], ALU.mult), reads=[xrB, gB], writes=[xrB])
            op(DVE, lambda: vE.tensor_tensor(xr, xr, Bt[:], ALU.add), reads=[xrB, gB], writes=[xrB])
            dma(SP, lambda: yE.dma_start(out=Xrows, in_=xr), reads=[xrB])
            op(ACT, lambda: sE.copy(xb[:], xr), reads=[xrB], writes=[xbB])
            if defer is None:
                transpose_store(xb, xbB, xt_view(XTd)[:, :, tb * 128:(tb + 1) * 128], stg, stgB, tpi)
            else:
                defer.append(lambda: transpose_store(xb, xbB, xt_view(XTd)[:, :, tb * 128:(tb + 1) * 128], stg, stgB, tpi))

        def ln_tiles(es, tag):
            st, stB = sbt(es, "st" + tag, [128, 4, 6], F32)
            mv, mvB = sbt(es, "mv" + tag, [128, 2], F32)
            sc, scB = sbt(es, "sc" + tag, [128, 4], F32)
            xb, xbB = sbt(es, "xb" + tag, [128, D], BF16)
            stg, stgB = sbt(es, "stg" + tag, [128, 16, 128], BF16)
            return (st, stB, mv, mvB, sc, scB, xb, xbB, stg, stgB)

        with ExitStack() as es:
            xrs = [sbt(es, f"pxr{i}", [128, D], F32) for i in range(2)]
            xbs = [sbt(es, f"pxb{i}", [128, D], BF16) for i in range(2)]
            stgs = [sbt(es, f"pstg{i}", [128, 16, 128], BF16) for i in range(2)]
            for tb in range(NTB):
                xr, xrB = xrs[tb % 2]
                xb, xbB = xbs[tb % 2]
                stg, stgB = stgs[tb % 2]
                dma(SP, lambda xr=xr, tb=tb: yE.dma_start(out=xr[:], in_=xin[tb * 128:(tb + 1) * 128, :]), writes=[xrB])
                op(ACT, lambda xr=xr, xb=xb: sE.copy(xb[:], xr[:]), reads=[xrB], writes=[xbB])
                transpose_store(xb, xbB, xt_view(XT)[:, :, tb * 128:(tb + 1) * 128], stg, stgB, [6, 7])
            barrier()

        for L in range(DEPTH):
            j = L // 2
            is_diff = (L % 2 == 0)
            Xsrc = xin if L == 0 else X
            w_in = diff_w_in[j] if is_diff else win_w_in[j]
            ncols = 6144 if is_diff else 3072
            m, hh = (8, 8) if is_diff else (4, 16)
            dh = 512 // m
            w_v = w_in.rearrange("(k p) n -> p k n", p=128)
            with ExitStack() as es:
                xts = [sbt(es, f"qxt{i}", [128, 16, 1024], BF16) for i in range(2)]
                wts = [sbt(es, f"qwt{i}", [128, 16, 512], BF16) for i in range(2)]
                rop, ropB = sbt(es, "rop", [128, 2, NTB, 64], F32)
                t1, t1B = sbt(es, "rt1", [128, m, hh], F32)
                t2, t2B = sbt(es, "rt2", [128, m, hh], F32)
                qbs = [sbt(es, f"qb{i}", [128, 512], BF16) for i in range(2)]
                stgs = [sbt(es, f"qstg{i}", [128, 4, 1024], BF16) for i in range(2)]
                rsrc = ropd if is_diff else ropw
                dma(SP, lambda: yE.dma_start(out=rop[:], in_=rsrc.rearrange("c p t f -> p c t f")), writes=[ropB])
                cnt = 0
                wcnt = 0
                qpend = []

                def q_load(tg):
                    xt, xtB = xts[tg % 2]
                    dma(SP, lambda: yE.dma_start(out=xt[:], in_=xt_view(XT)[:, :, tg * 1024:(tg + 1) * 1024]), writes=[xtB])
                q_load(0)
                for tg in range(4):
                    xt, xtB = xts[tg % 2]
                    if tg + 1 < 4:
                        q_load(tg + 1)
                    for nt in range(ncols // 512):
                        wt, wtB = wts[wcnt % 2]
                        stg, stgB = stgs[wcnt % 2]
                        wcnt += 1
                        dma(POOL, lambda wt=wt, nt=nt: gE.dma_start(out=wt[:], in_=w_v[:, :, nt * 512:(nt + 1) * 512]), writes=[wtB])
                        if is_diff:
                            kind = "q" if nt < 4 else ("k" if nt < 8 else "v")
                            h0 = (nt % 4) * 4
                        else:
                            kind = "q" if nt < 4 else ("k" if nt == 4 else "v")
                            h0 = nt * 4 if nt < 4 else 0
                        for tb in range(8):
                            bk = cnt % 4
                            cnt += 1
                            ps = PS[bk]
                            tbg = tg * 8 + tb
                            for kc in range(16):
                                op(PE, lambda ps=ps, xt=xt, wt=wt, kc=kc, tb=tb: tE.matmul(ps[:], lhsT=xt[:, kc, tb * 128:(tb + 1) * 128], rhs=wt[:, kc, :], start=(kc == 0), stop=(kc == 15)),
                                   reads=[xtB, wtB], writes=[PB[bk]])
                            while qpend:
                                qpend.pop(0)()
                            qb, qbB = qbs[cnt % 2]
                            if kind == "v":
                                op(DVE, lambda qb=qb, ps=ps: vE.tensor_copy(qb[:], ps[:]), reads=[PB[bk]], writes=[qbB])
                                vcols = (nt - 8) * 512 if is_diff else 0
                                dma(SP, lambda qb=qb, tbg=tbg, vcols=vcols: yE.dma_start(out=Vd[tbg * 128:(tbg + 1) * 128, vcols:vcols + 512], in_=qb[:]), reads=[qbB])
                                continue
                            ps3 = ps[:].rearrange("p (m d) -> p m d", m=m)
                            qb3 = qb[:].rearrange("p (m d) -> p m d", m=m)
                            c3 = rop[:, 0, tbg, :].rearrange("p (m h) -> p m h", m=m)
                            s3 = rop[:, 1, tbg, :].rearrange("p (m h) -> p m h", m=m)
                            x1, x2 = ps3[:, :, 0:hh], ps3[:, :, hh:2 * hh]
                            R = [PB[bk], ropB]
                            op(DVE, lambda x1=x1, c3=c3: vE.tensor_tensor(t1[:], x1, c3, ALU.mult), reads=R, writes=[t1B])
                            op(DVE, lambda x2=x2, s3=s3: vE.tensor_tensor(t2[:], x2, s3, ALU.mult), reads=R, writes=[t2B])
                            op(DVE, lambda qb3=qb3: vE.tensor_tensor(qb3[:, :, 0:hh], t1[:], t2[:], ALU.subtract), reads=[t1B, t2B], writes=[qbB])
                            op(DVE, lambda x2=x2, c3=c3: vE.tensor_tensor(t1[:], x2, c3, ALU.mult), reads=R, writes=[t1B])
                            op(DVE, lambda x1=x1, s3=s3: vE.tensor_tensor(t2[:], x1, s3, ALU.mult), reads=R, writes=[t2B])
                            op(DVE, lambda qb3=qb3: vE.tensor_tensor(qb3[:, :, hh:2 * hh], t1[:], t2[:], ALU.add), reads=[t1B, t2B], writes=[qbB])
                            op(DVE, lambda qb3=qb3, ps3=ps3: vE.tensor_copy(qb3[:, :, 2 * hh:], ps3[:, :, 2 * hh:]), reads=[PB[bk]], writes=[qbB])

                            def trp(qb=qb, qbB=qbB, stg=stg, stgB=stgB, tb=tb, tbk=4 + (cnt % 2), last=(tb == 7), kind=kind, h0=h0, tg=tg):
                                tp = PS[tbk][:].bitcast(BF16)
                                for jj in range(4):
                                    op(PE, lambda jj=jj: tE.transpose(tp[:, jj * 128:(jj + 1) * 128], qb[:, jj * 128:(jj + 1) * 128], ident[:]),
                                       reads=[qbB, cb], writes=[PB[tbk]])
                                op(DVE, lambda: vE.tensor_copy(stg[:, :, tb * 128:(tb + 1) * 128], tp[:, 0:512].rearrange("p (a b) -> p a b", a=4)),
                                   reads=[PB[tbk]], writes=[stgB])
                                if last:
                                    dstT = QT if kind == "q" else KT
                                    dma(SP, lambda: yE.dma_start(out=dstT[h0:h0 + 4].rearrange("h p t -> p h t")[:, :, tg * 1024:(tg + 1) * 1024], in_=stg[:]), reads=[stgB])
                            qpend.append(trp)
                while qpend:
                    qpend.pop(0)()
                barrier()
            if is_diff:
                lam_init = 0.8 - 0.6 * math.exp(-0.3 * L)
                with ExitStack() as es:
                    qTs = [sbt(es, f"aq{i}", [128, T], BF16) for i in range(2)]
                    kTs = [sbt(es, f"ak{i}", [128, T], BF16) for i in range(2)]
                    vas = [sbt(es, f"av{i}", [128, 32, 130], BF16) for i in range(2)]
                    Pb = [sbt(es, f"ap{i}", [128, 512], BF16) for i in range(3)]
                    lm, lmB = sbt(es, "lm", [128, 256], F32)
                    gs, gsB = sbt(es, "gs", [128, 128], F32)
                    mk, mkB = sbt(es, "mk", [128, 2, 32], F32)
                    sm, smB = sbt(es, "sm", [128, 16], F32)
                    of, ofB = sbt(es, "of", [128, 128], F32)
                    sq, sqB = sbt(es, "sq", [128, 128], F32)
                    obs = [sbt(es, f"ob{i}", [128, 128], BF16) for i in range(4)]
                    ostgs = [sbt(es, f"aost{i}", [128, 512], BF16) for i in range(2)]
                    dma(SP, lambda: yE.dma_start(out=lm[:], in_=lam_r[j]), writes=[lmB])
                    dma(SP, lambda: yE.dma_start(out=gs[:], in_=subg_r[j]), writes=[gsB])
                    dma(SP, lambda: yE.dma_start(out=mk[:], in_=maskb), writes=[mkB])
                    for i in range(2):
                        op(POOL, lambda i=i: gE.memset(vas[i][0][:], 1.0), writes=[vas[i][1]])
                    op(DVE, lambda: vE.tensor_scalar_mul(gs[:], gs[:], 1.0 - lam_init), reads=[gsB], writes=[gsB])
                    op(DVE, lambda: vE.tensor_tensor(sq[:, 0:64], lm[:, 0:64], lm[:, 64:128], ALU.mult), reads=[lmB], writes=[sqB])
                    op(DVE, lambda: vE.tensor_tensor(sq[:, 64:128], lm[:, 128:192], lm[:, 192:256], ALU.mult), reads=[lmB], writes=[sqB])
                    op(DVE, lambda: vE.reduce_sum(sm[:, 0:1], sq[:, 0:64], AX.X), reads=[sqB], writes=[smB])
                    op(DVE, lambda: vE.reduce_sum(sm[:, 1:2], sq[:, 64:128], AX.X), reads=[sqB], writes=[smB])
                    op(ACT, lambda: sE.activation(sm[:, 2:4], sm[:, 0:2], AF.Exp), reads=[smB], writes=[smB])
                    op(DVE, lambda: vE.tensor_tensor(sm[:, 4:5], sm[:, 3:4], sm[:, 2:3], ALU.subtract), reads=[smB], writes=[smB])
                    op(DVE, lambda: vE.tensor_scalar_add(sm[:, 5:6], sm[:, 4:5], -lam_init), reads=[smB], writes=[smB])
                    Vv = Vd.rearrange("(kb p) e -> p kb e", p=128)
                    P2 = [sbt(es, f"ap2{i}", [128, 1024], BF16) for i in range(3)]
                    accS8, accSB = sbt(es, "accS8", [128, 8, 130], F32)
                    rz, rzB = sbt(es, "rz", [128, 28], F32)
                    of4, ofB = sbt(es, "of4", [128, 4, 128], F32)
                    sq4, sqB4 = sbt(es, "sq4", [128, 4, 128], F32)

                    def a_load(h):
                        qT, qTB = qTs[h % 2]
                        kT, kTB = kTs[h % 2]
                        va, vaB = vas[h % 2]
                        dma(SP, lambda: yE.dma_start(out=qT[:], in_=QT[h]), writes=[qTB])
                        dma(SP, lambda: yE.dma_start(out=kT[:], in_=KT[h]), writes=[kTB])
                        dma(SP, lambda: yE.dma_start(out=va[:, :, 0:128], in_=Vv[:, :, h * 128:(h + 1) * 128]), writes=[vaB])

                    steps = [(h, qg, kb) for h in range(16) for qg in range(8) for kb in range(32)]
                    accs = {}
                    for qb_ in range(4):
                        for c in range(2):
                            a = qb_ * 2 + c
                            accs[(qb_, c)] = (4 + a // 3, (a % 3) * 130)

                    def emit_qk(idx):
                        h, qg, kb = steps[idx]
                        qT, qTB = qTs[h % 2]
                        kT, kTB = kTs[h % 2]
                        sp = idx % 2
                        for c in range(2):
                            bk = 2 * sp + c
                            op(PE, lambda c=c, bk=bk: tE.matmul(PS[bk][:], lhsT=kT[64 * c:64 * c + 64, kb * 128:(kb + 1) * 128], rhs=qT[64 * c:64 * c + 64, qg * 512:(qg + 1) * 512], start=True, stop=True),
                               reads=[kTB, qTB], writes=[PB[bk]])

                    def emit_exp(idx):
                        h, qg, kb = steps[idx]
                        sp = idx % 2
                        P, PBf = P2[idx % 3]
                        seg = qg // 4
                        op(ACT, lambda: sE.activation(P[:], PSALL[:, sp * 1024:(sp + 1) * 1024], AF.Exp, bias=mk[:, seg, kb:kb + 1], scale=0.125),
                           reads=[PB[2 * sp], PB[2 * sp + 1], mkB], writes=[PBf])

                    def emit_pv(idx):
                        h, qg, kb = steps[idx]
                        va, vaB = vas[h % 2]
                        P, PBf = P2[idx % 3]
                        for c in range(2):
                            for qb_ in range(4):
                                bk, off = accs[(qb_, c)]
                                op(PE, lambda bk=bk, off=off, qb_=qb_, c=c: tE.matmul(PS[bk][:, off:off + 129], lhsT=P[:, c * 512 + qb_ * 128:c * 512 + (qb_ + 1) * 128], rhs=va[:, kb, 0:129], start=(kb == 0 and c == 0 and qb_ in (0, 2, 3)), stop=(kb == 31), skip_group_check=True),
                                   reads=[PBf, vaB], writes=[PB[bk]])

                    pending = []

                    def emit_epilogue(h, qg, ocnt):
                        ostg, ostgB = ostgs[ocnt % 2]
                        for b3, na in ((0, 3), (1, 3), (2, 2)):
                            op(DVE, lambda b3=b3, na=na: vE.tensor_copy(accS8[:, 3 * b3:3 * b3 + na, :], PS[4 + b3][:, 0:130 * na].rearrange("p (a c) -> p a c", a=na)),
                               reads=[PB[4 + b3]], writes=[accSB])
                        op(DVE, lambda: vE.reciprocal(rz[:, 0:8].rearrange("p (a o) -> p a o", o=1), accS8[:, :, 128:129]), reads=[accSB], writes=[rzB])
                        op(DVE, lambda: vE.tensor_scalar_mul(rz[:, 8:12], rz[:, 1:8:2], sm[:, 5:6]), reads=[rzB, smB], writes=[rzB])
                        for qb_ in range(4):
                            op(DVE, lambda qb_=qb_: vE.tensor_scalar_mul(of4[:, qb_, :], accS8[:, 2 * qb_, 0:128], rz[:, 2 * qb_:2 * qb_ + 1]), reads=[accSB, rzB], writes=[ofB])
                            op(DVE, lambda qb_=qb_: vE.scalar_tensor_tensor(of4[:, qb_, :], accS8[:, 2 * qb_ + 1, 0:128], rz[:, 8 + qb_:9 + qb_], of4[:, qb_, :], ALU.mult, ALU.add), reads=[accSB, rzB, ofB], writes=[ofB])
                        op(DVE, lambda: vE.tensor_tensor(sq4[:], of4[:], of4[:], ALU.mult), reads=[ofB], writes=[sqB])
                        op(DVE, lambda: vE.reduce_sum(rz[:, 12:16], sq4[:], AX.X), reads=[sqB], writes=[rzB])
                        op(DVE, lambda: vE.tensor_scalar(rz[:, 16:20], rz[:, 12:16], 1.0 / 128, EPS, ALU.mult, ALU.add), reads=[rzB], writes=[rzB])
                        op(ACT, lambda: sE.activation(rz[:, 20:24], rz[:, 16:20], AF.Ln), reads=[rzB], writes=[rzB])
                        op(ACT, lambda: sE.activation(rz[:, 24:28], rz[:, 20:24], AF.Exp, scale=-0.5), reads=[rzB], writes=[rzB])
                        for qb_ in range(4):
                            ob, obB = obs[qb_]
                            op(DVE, lambda qb_=qb_, ob=ob: vE.scalar_tensor_tensor(ob[:], of4[:, qb_, :], rz[:, 24 + qb_:25 + qb_], gs[:], ALU.mult, ALU.mult), reads=[ofB, rzB, gsB], writes=[obB])

                        def part2():
                            tp = PS[7][:].bitcast(BF16)
                            for qb_ in range(4):
                                ob, obB = obs[qb_]
                                op(PE, lambda qb_=qb_, ob=ob: tE.transpose(tp[:, qb_ * 128:(qb_ + 1) * 128], ob[:], ident[:]), reads=[obB, cb], writes=[PB[7]])
                            op(DVE, lambda: vE.tensor_copy(ostg[:], tp[:, 0:512]), reads=[PB[7]], writes=[ostgB])
                            dma(SP, lambda: yE.dma_start(out=OT[h][:, qg * 512:(qg + 1) * 512], in_=ostg[:]), reads=[ostgB])
                        pending.append(part2)

                    a_load(0)
                    emit_qk(0)
                    emit_qk(1)
                    ocnt = 0
                    for idx, (h, qg, kb) in enumerate(steps):
                        if qg == 0 and kb == 0 and h + 1 < 16:
                            a_load(h + 1)
                        emit_exp(idx)
                        emit_pv(idx)
                        if idx + 2 < len(steps):
                            emit_qk(idx + 2)
                        if kb == 8 and pending:
                            pending.pop(0)()
                        if kb == 31:
                            emit_epilogue(h, qg, ocnt)
                            ocnt += 1
                    while pending:
                        pending.pop(0)()
                    barrier()
            else:
                with ExitStack() as es:
                    q4s = [sbt(es, f"wq{i}", [128, 4, T], BF16) for i in range(2)]
                    kTs = [sbt(es, f"wk{i}", [128, T], BF16) for i in range(2)]
                    vas = [sbt(es, f"wv{i}", [128, 32, 130], BF16) for i in range(2)]
                    Pb = [sbt(es, f"wp{i}", [128, 512], BF16) for i in range(3)]
                    tm, tmB = sbt(es, "tm", [128, 2, 512], BF16)
                    sk, skB = sbt(es, "sk", [128, 16], F32)
                    sm, smB = sbt(es, "wsm", [128, 8], F32)
                    wacc, waccB = sbt(es, "wacc", [128, 4, 130], F32)
                    wz, wzB = sbt(es, "wz", [128, 8], F32)
                    obs4 = [sbt(es, f"wob4{i}", [128, 128], BF16) for i in range(4)]
                    obs = [sbt(es, f"wob{i}", [128, 128], BF16) for i in range(2)]
                    ostgs = [sbt(es, f"wost{i}", [128, 4, 128], BF16) for i in range(2)]
                    dma(SP, lambda: yE.dma_start(out=tm[:], in_=trimask.rearrange("c p f -> p c f")), writes=[tmB])
                    dma(SP, lambda: yE.dma_start(out=sk[:], in_=sink_r[j]), writes=[skB])
                    op(ACT, lambda: sE.activation(sk[:], sk[:], AF.Exp), reads=[skB], writes=[skB])
                    for i in range(2):
                        op(POOL, lambda i=i: gE.memset(vas[i][0][:], 1.0), writes=[vas[i][1]])
                    Vv = Vd.rearrange("(kb p) e -> p kb e", p=128)
                    accb = [(4, 0), (4, 130), (4, 260), (5, 0)]
                    wsteps = []
                    for g in range(4):
                        for n in range(32):
                            kbs = [kb for kb in (n - 1, n, n + 1) if 0 <= kb < 32]
                            for i_, kb in enumerate(kbs):
                                wsteps.append((g, n, kb, i_, len(kbs)))

                    def w_load(g):
                        q4, q4B = q4s[g % 2]
                        kT, kTB = kTs[g % 2]
                        va, vaB = vas[g % 2]
                        dma(SP, lambda: yE.dma_start(out=q4[:], in_=QT[4 * g:4 * g + 4].rearrange("h p t -> p h t")), writes=[q4B])
                        dma(SP, lambda: yE.dma_start(out=kT[:], in_=KT[g]), writes=[kTB])
                        dma(SP, lambda: yE.dma_start(out=va[:, :, 0:128], in_=Vv[:, :, g * 128:(g + 1) * 128]), writes=[vaB])

                    def w_qk(idx):
                        g, n, kb, i_, nk = wsteps[idx]
                        q4, q4B = q4s[g % 2]
                        kT, kTB = kTs[g % 2]
                        sb_ = idx % 4
                        op(PE, lambda: tE.matmul(PS[sb_][:], lhsT=kT[:, kb * 128:(kb + 1) * 128], rhs=q4[:, :, n * 128:(n + 1) * 128], start=True, stop=True),
                           reads=[kTB, q4B], writes=[PB[sb_]])

                    def w_exp(idx):
                        g, n, kb, i_, nk = wsteps[idx]
                        sb_ = idx % 4
                        P, PBf = Pb[idx % 3]
                        if (n, kb) in ((15, 16), (16, 15)):
                            op(ACT, lambda: sE.activation(P[:], PS[sb_][:], AF.Exp, bias=flg[:, 1:2], scale=128 ** -0.5), reads=[PB[sb_], cb], writes=[PBf])
                        else:
                            op(ACT, lambda: sE.activation(P[:], PS[sb_][:], AF.Exp, scale=128 ** -0.5), reads=[PB[sb_]], writes=[PBf])
                        if kb != n:
                            mi = 0 if kb < n else 1
                            op(POOL, lambda: gE.tensor_tensor(P[:], P[:], tm[:, mi, :], ALU.mult), reads=[PBf, tmB], writes=[PBf])

                    def w_pv(idx):
                        g, n, kb, i_, nk = wsteps[idx]
                        va, vaB = vas[g % 2]
                        P, PBf = Pb[idx % 3]
                        for r in range(4):
                            bk, off = accb[r]
                            op(PE, lambda bk=bk, off=off, r=r: tE.matmul(PS[bk][:, off:off + 129], lhsT=P[:, r * 128:(r + 1) * 128], rhs=va[:, kb, 0:129], start=(i_ == 0 and r in (0, 3)), stop=(i_ == nk - 1), skip_group_check=True),
                               reads=[PBf, vaB], writes=[PB[bk]])

                    wpend = []

                    def w_epi(g, n, ocnt):
                        ostg, ostgB = ostgs[ocnt % 2]
                        op(DVE, lambda: vE.tensor_copy(wacc[:, 0:3, :], PS[4][:, 0:390].rearrange("p (a c) -> p a c", a=3)), reads=[PB[4]], writes=[waccB])
                        op(DVE, lambda: vE.tensor_copy(wacc[:, 3:4, :], PS[5][:, 0:130].rearrange("p (a c) -> p a c", a=1)), reads=[PB[5]], writes=[waccB])
                        op(DVE, lambda: vE.tensor_tensor(wz[:, 0:4].rearrange("p (a o) -> p a o", o=1), wacc[:, :, 128:129], sk[:, 4 * g:4 * g + 4].rearrange("p (a o) -> p a o", o=1), ALU.add), reads=[waccB, skB], writes=[wzB])
                        op(DVE, lambda: vE.reciprocal(wz[:, 4:8], wz[:, 0:4]), reads=[wzB], writes=[wzB])
                        for r in range(4):
                            ob, obB = obs4[r]
                            op(DVE, lambda r=r, ob=ob: vE.tensor_scalar_mul(ob[:], wacc[:, r, 0:128], wz[:, 4 + r:5 + r]), reads=[waccB, wzB], writes=[obB])

                        def part2():
                            tp = PS[7][:].bitcast(BF16)
                            for r in range(4):
                                ob, obB = obs4[r]
                                op(PE, lambda r=r, ob=ob: tE.transpose(tp[:, r * 128:(r + 1) * 128], ob[:], ident[:]), reads=[obB, cb], writes=[PB[7]])
                            op(DVE, lambda: vE.tensor_copy(ostg[:], tp[:, 0:512].rearrange("p (a b) -> p a b", a=4)), reads=[PB[7]], writes=[ostgB])
                            dma(SP, lambda: yE.dma_start(out=OT[4 * g:4 * g + 4].rearrange("h p t -> p h t")[:, :, n * 128:(n + 1) * 128], in_=ostg[:]), reads=[ostgB])
                        wpend.append(part2)

                    w_load(0)
                    w_qk(0)
                    w_qk(1)
                    ocnt = 0
                    for idx, (g, n, kb, i_, nk) in enumerate(wsteps):
                        if n == 0 and i_ == 0 and g + 1 < 4:
                            w_load(g + 1)
                        w_exp(idx)
                        w_pv(idx)
                        if idx + 2 < len(wsteps):
                            w_qk(idx + 2)
                        if i_ == 1 and wpend:
                            wpend.pop(0)()
                        if i_ == nk - 1:
                            w_epi(g, n, ocnt)
                            ocnt += 1
                    while wpend:
                        wpend.pop(0)()
                    barrier()
            w_out = (diff_w_out if is_diff else win_w_out)[j].rearrange("(k p) n -> p k n", p=128)
            with ExitStack() as es:
                wo, woB = sbt(es, "wo", [128, 16, D], BF16)
                Gt, gB = sbt(es, "lng", [128, D], F32)
                Bt, _ = sbt(es, "lnb", [128, D], F32)
                ots = [sbt(es, f"oot{i}", [128, 16, 128], BF16) for i in range(2)]
                xrs = [sbt(es, f"oxr{i}", [128, D], F32) for i in range(2)]
                lts = [ln_tiles(es, f"o{i}") for i in range(2)]
                for q in range(4):
                    dma(POOL, lambda q=q: gE.dma_start(out=wo[:, 4 * q:4 * q + 4, :], in_=w_out[:, 4 * q:4 * q + 4, :]), writes=[woB])
                dma(SP, lambda: yE.dma_start(out=Gt[:], in_=lnm_g[L]), writes=[gB])
                dma(SP, lambda: yE.dma_start(out=Bt[:], in_=lnm_b[L]), writes=[gB])
                cnt = 0
                opend = []

                def op_load(tb):
                    ot, otB = ots[tb % 2]
                    xr, xrB = xrs[tb % 2]
                    dma(SP, lambda: yE.dma_start(out=ot[:], in_=OT.rearrange("h p t -> p h t")[:, :, tb * 128:(tb + 1) * 128]), writes=[otB])
                    dma(SP, lambda: yE.dma_start(out=xr[:], in_=Xsrc[tb * 128:(tb + 1) * 128, :]), writes=[xrB])
                op_load(0)
                for tb in range(NTB):
                    ot, otB = ots[tb % 2]
                    xr, xrB = xrs[tb % 2]
                    if tb + 1 < NTB:
                        op_load(tb + 1)
                    for nt in range(4):
                        bk = cnt % 6
                        cnt += 1
                        ps = PS[bk]
                        for h in range(16):
                            op(PE, lambda ps=ps, ot=ot, h=h, nt=nt: tE.matmul(ps[:], lhsT=ot[:, h, :], rhs=wo[:, h, nt * 512:(nt + 1) * 512], start=(h == 0), stop=(h == 15)),
                               reads=[otB, woB], writes=[PB[bk]])
                        op(DVE, lambda xr=xr, ps=ps, nt=nt: vE.scalar_tensor_tensor(xr[:, nt * 512:(nt + 1) * 512], xr[:, nt * 512:(nt + 1) * 512], ALPHA, ps[:], ALU.mult, ALU.add),
                           reads=[PB[bk], xrB], writes=[xrB])
                    while opend:
                        opend.pop(0)()
                    layer_norm(es, xr[:], xrB, Gt, Bt, gB, X2[tb * 128:(tb + 1) * 128, :], XT2, tb, [6, 7], lts[tb % 2], defer=opend)
                while opend:
                    opend.pop(0)()
                barrier()
            wup = ffn_w_up[L].rearrange("(k p) n -> p k n", p=128)
            wdn = ffn_w_down[L].rearrange("(i p) n -> p i n", p=128)
            Xdst = y if L == DEPTH - 1 else X
            with ExitStack() as es:
                xt, xtB = sbt(es, "fxt", [128, 16, 514], BF16)
                xr4, xr4B = sbt(es, "fxr", [128, 4, D], F32)
                wgs = [sbt(es, f"fwg{i}", [128, 16, 512], BF16) for i in range(2)]
                wus = [sbt(es, f"fwu{i}", [128, 16, 512], BF16) for i in range(2)]
                tgs = [sbt(es, f"ftg{i}", [128, 512], F32) for i in range(2)]
                tus = [sbt(es, f"ftu{i}", [128, 512], F32) for i in range(2)]
                sgs = [sbt(es, f"fsg{i}", [128, 512], F32) for i in range(2)]
                aT, aTB = sbt(es, "faT", [128, 11, 512], BF16)
                wds = [sbt(es, f"fwd{i}", [128, 11, 512], BF16) for i in range(2)]
                cp, cpB = sbt(es, "fcp", [128, 2 * NFC, 4], F32)
                Gt, gB = sbt(es, "flng", [128, D], F32)
                Bt, _ = sbt(es, "flnb", [128, D], F32)
                lts = [ln_tiles(es, f"f{i}") for i in range(2)]
                dma(SP, lambda: yE.dma_start(out=cp[:], in_=convp[L]), writes=[cpB])
                dma(SP, lambda: yE.dma_start(out=Gt[:], in_=lnf_g[L]), writes=[gB])
                dma(SP, lambda: yE.dma_start(out=Bt[:], in_=lnf_b[L]), writes=[gB])
                XT2v = xt_view(XT2)

                def load_xt(tg):
                    t0 = tg * 512
                    dma(SP, lambda: yE.dma_start(out=xt[:, :, 1:513], in_=XT2v[:, :, t0:t0 + 512]), writes=[xtB])
                    with nc.allow_non_contiguous_dma(reason="halo column"):
                        if tg > 0:
                            dma(SP, lambda: yE.dma_start(out=xt[:, :, 0:1], in_=XT2v[:, :, t0 - 1:t0]), writes=[xtB])
                        else:
                            op(DVE, lambda: vE.memset(xt[:, :, 0:1], 0.0), writes=[xtB])
                        if tg < 7:
                            dma(SP, lambda: yE.dma_start(out=xt[:, :, 513:514], in_=XT2v[:, :, t0 + 512:t0 + 513]), writes=[xtB])
                        else:
                            op(DVE, lambda: vE.memset(xt[:, :, 513:514], 0.0), writes=[xtB])
                    if tg == 4:
                        op(DVE, lambda: vE.tensor_scalar_mul(xt[:, :, 0:1], xt[:, :, 0:1], flg[:, 0:1]), reads=[xtB, cb], writes=[xtB])
                    if tg == 3:
                        op(DVE, lambda: vE.tensor_scalar_mul(xt[:, :, 513:514], xt[:, :, 513:514], flg[:, 0:1]), reads=[xtB, cb], writes=[xtB])

                ccnt = 0
                wcnt = 0
                dcn = 0
                ycnt = 0
                load_xt(0)
                for tg in range(8):
                    t0 = tg * 512
                    dma(SP, lambda t0=t0: yE.dma_start(out=xr4[:], in_=X2[t0:t0 + 512, :].rearrange("(a p) d -> p a d", p=128)), writes=[xr4B])
                    for qf in range(4):
                        c0 = qf * 11
                        c1 = min(NFC, c0 + 11)
                        ca = c0
                        while ca < c1:
                            ncw = min(4, c1 - ca)
                            wg, wgB = wgs[wcnt % 2]
                            wu, wuB = wus[wcnt % 2]
                            wcnt += 1
                            dma(POOL, lambda wg=wg, ca=ca, ncw=ncw: gE.dma_start(out=wg[:, :, 0:ncw * 128], in_=wup[:, :, ca * 128:(ca + ncw) * 128]), writes=[wgB])
                            dma(POOL, lambda wu=wu, ca=ca, ncw=ncw: gE.dma_start(out=wu[:, :, 0:ncw * 128], in_=wup[:, :, FF + ca * 128:FF + (ca + ncw) * 128]), writes=[wuB])
                            for ii in range(ncw):
                                i = ca + ii
                                tg_, tgB = tgs[ccnt % 2]
                                tu_, tuB = tus[ccnt % 2]
                                sg_, sgB = sgs[ccnt % 2]
                                gb, ub, hb = ccnt % 2, 2 + ccnt % 2, 4 + ccnt % 2
                                ccnt += 1
                                for (w_, wB_, bnk, hoff) in ((wg, wgB, gb, 0), (wu, wuB, ub, 2)):
                                    for kc in range(16):
                                        op(PE, lambda w_=w_, bnk=bnk, kc=kc, ii=ii: tE.matmul(PS[bnk][:], lhsT=w_[:, kc, ii * 128:(ii + 1) * 128], rhs=xt[:, kc, 1:513], start=(kc == 0), stop=(kc == 15)),
                                           reads=[wB_, xtB], writes=[PB[bnk]])
                                    for kc in range(16):
                                        op(PE, lambda w_=w_, hb=hb, hoff=hoff, kc=kc, ii=ii: tE.matmul(PS[hb][:, hoff:hoff + 2], lhsT=w_[:, kc, ii * 128:(ii + 1) * 128], rhs=xt[:, kc, 0:514:513], start=(kc == 0), stop=(kc == 15)),
                                           reads=[wB_, xtB], writes=[PB[hb]])
                                for (tt, ttB, bnk, hoff, ci) in ((tg_, tgB, gb, 0, i), (tu_, tuB, ub, 2, NFC + i)):
                                    Gp, Hp = PS[bnk], PS[hb]
                                    w0, w1, w2, bb = cp[:, ci, 0:1], cp[:, ci, 1:2], cp[:, ci, 2:3], cp[:, ci, 3:4]
                                    op(DVE, lambda tt=tt, Gp=Gp, w1=w1, bb=bb: vE.tensor_scalar(tt[:], Gp[:], w1, bb, ALU.mult, ALU.add), reads=[PB[bnk], cpB], writes=[ttB])
                                    op(DVE, lambda tt=tt, Gp=Gp, w0=w0: vE.scalar_tensor_tensor(tt[:, 1:512], Gp[:, 0:511], w0, tt[:, 1:512], ALU.mult, ALU.add), reads=[PB[bnk], cpB, ttB], writes=[ttB])
                                    op(DVE, lambda tt=tt, Hp=Hp, w0=w0, hoff=hoff: vE.scalar_tensor_tensor(tt[:, 0:1], Hp[:, hoff:hoff + 1], w0, tt[:, 0:1], ALU.mult, ALU.add), reads=[PB[hb], cpB, ttB], writes=[ttB])
                                    op(DVE, lambda tt=tt, Gp=Gp, w2=w2: vE.scalar_tensor_tensor(tt[:, 0:511], Gp[:, 1:512], w2, tt[:, 0:511], ALU.mult, ALU.add), reads=[PB[bnk], cpB, ttB], writes=[ttB])
                                    op(DVE, lambda tt=tt, Hp=Hp, w2=w2, hoff=hoff: vE.scalar_tensor_tensor(tt[:, 511:512], Hp[:, hoff + 1:hoff + 2], w2, tt[:, 511:512], ALU.mult, ALU.add), reads=[PB[hb], cpB, ttB], writes=[ttB])
                                op(ACT, lambda sg_=sg_, tg_=tg_: sE.activation(sg_[:], tg_[:], AF.Silu), reads=[tgB], writes=[sgB])
                                op(DVE, lambda sg_=sg_, tu_=tu_, i=i, c0=c0: vE.tensor_tensor(aT[:, i - c0, :], sg_[:], tu_[:], ALU.mult), reads=[sgB, tuB], writes=[aTB])
                            ca += ncw
                        if qf == 3 and tg < 7:
                            load_xt(tg + 1)
                        nch = c1 - c0
                        for nt in range(4):
                            wd, wdB = wds[dcn % 2]
                            dcn += 1
                            dma(POOL, lambda wd=wd, nt=nt, c0=c0, nch=nch: gE.dma_start(out=wd[:, 0:nch, :], in_=wdn[:, c0:c0 + nch, nt * 512:(nt + 1) * 512]), writes=[wdB])
                            for tb in range(4):
                                bk = 6 + ycnt % 2
                                ycnt += 1
                                for jj in range(nch):
                                    op(PE, lambda bk=bk, jj=jj, tb=tb, wd=wd, nch=nch: tE.matmul(PS[bk][:], lhsT=aT[:, jj, tb * 128:(tb + 1) * 128], rhs=wd[:, jj, :], start=(jj == 0), stop=(jj == nch - 1)),
                                       reads=[aTB, wdB], writes=[PB[bk]])
                                dst = xr4[:, tb, nt * 512:(nt + 1) * 512]
                                if qf == 0:
                                    op(DVE, lambda dst=dst, bk=bk: vE.scalar_tensor_tensor(dst, dst, ALPHA, PS[bk][:], ALU.mult, ALU.add), reads=[PB[bk], xr4B], writes=[xr4B])
                                else:
                                    op(DVE, lambda dst=dst, bk=bk: vE.tensor_tensor(dst, dst, PS[bk][:], ALU.add), reads=[PB[bk], xr4B], writes=[xr4B])
                    fpend = []
                    for tb in range(4):
                        tbg = tg * 4 + tb
                        layer_norm(es, xr4[:, tb, :], xr4B, Gt, Bt, gB, Xdst[tbg * 128:(tbg + 1) * 128, :], XT, tbg, [6, 7], lts[tb % 2], defer=fpend)
                        if len(fpend) > 1:
                            fpend.pop(0)()
                    while fpend:
                        fpend.pop(0)()
                barrier()
        barrier()

    return nc


def _rope_rep(pos, rot, m):
    inv = (np.float32(500000.0) ** (-(np.arange(0, rot, 2, dtype=np.float32)) / np.float32(rot))).astype(np.float32)
    ang = (pos[:, None].astype(np.float32) * inv[None, :]).astype(np.float32)
    c, s = np.cos(ang).astype(np.float32), np.sin(ang).astype(np.float32)
    rep = lambda a: np.tile(a[:, None, :], (1, m, 1)).reshape(T, -1)
    out = np.stack([rep(c), rep(s)], 0)
    return np.ascontiguousarray(out.reshape(2, NTB, 128, -1).transpose(0, 2, 1, 3))


def kernel(**inp):
    f = lambda k: np.ascontiguousarray(np.asarray(inp[k], dtype=np.float32))
    xp, xs = f("x_prompt"), f("x_sample")
    rep = lambda a: np.ascontiguousarray(np.broadcast_to(a[:, None, :], (a.shape[0], 128, a.shape[1])))
    cw, cbias = f("ffn_conv_w"), f("ffn_conv_b")
    cpar = np.concatenate([cw, cbias[:, None, :]], axis=1)
    convp = np.ascontiguousarray(cpar.reshape(4, 4, 2 * NFC, 128).transpose(0, 3, 2, 1))
    shared = {
        "diff_w_in": f("diff_w_in"), "diff_w_out": f("diff_w_out"), "win_w_in": f("win_w_in"),
        "win_w_out": f("win_w_out"), "ffn_w_up": f("ffn_w_up"), "ffn_w_down": f("ffn_w_down"),
        "lam_r": rep(f("diff_lam").reshape(2, 256)), "subg_r": rep(f("diff_subln_g")), "sink_r": rep(f("win_sink")),
        "lnm_g": rep(f("ln_mix_g")), "lnm_b": rep(f("ln_mix_b")), "lnf_g": rep(f("ln_ffn_g")), "lnf_b": rep(f("ln_ffn_b")),
        "convp": convp, "ident": np.eye(128).astype(ml_dtypes.bfloat16),
    }
    jj, ii = np.meshgrid(np.arange(128), np.arange(128), indexing="ij")
    mL = (ii <= jj).astype(np.float32)
    mU = (jj <= ii).astype(np.float32)
    shared["trimask"] = np.stack([np.tile(mL, (1, 4)), np.tile(mU, (1, 4))], 0).astype(ml_dtypes.bfloat16)
    in_maps = []
    for c in range(8):
        two = c < 4
        if two:
            xc = np.concatenate([xp[2 * c], xp[2 * c + 1]], 0)
            pos = np.concatenate([np.arange(2048), np.arange(2048)])
        else:
            xc = xs[c % 2]
            pos = np.arange(4096)
        mb = np.zeros((128, 2, 32), np.float32)
        if two:
            mb[:, 0, 16:] = NEG
            mb[:, 1, :16] = NEG
        fl = np.zeros((128, 2), np.float32)
        fl[:, 0] = 0.0 if two else 1.0
        fl[:, 1] = NEG if two else 0.0
        d = dict(shared)
        d.update({"xin": np.ascontiguousarray(xc), "ropd": _rope_rep(pos, 16, 8), "ropw": _rope_rep(pos, 32, 4), "maskb": mb, "flags": fl})
        in_maps.append(d)
    nc = build()
    res = run_bass_kernel_spmd(nc, in_maps, core_ids=list(range(8)))
    ys = [np.asarray(r["y"], dtype=np.float32) for r in res.results]
    y_prompt = np.stack([ys[c // 2][(c % 2) * 2048:(c % 2 + 1) * 2048] for c in range(8)], 0)
    y_sample = np.stack([ys[4], ys[5]], 0)
    return (y_prompt, y_sample)
```

```python
import math
from contextlib import ExitStack
import numpy as np
import ml_dtypes
import concourse.bass as bass
import concourse.mybir as mybir
from concourse.bass_utils import run_bass_kernel_spmd

F32 = mybir.dt.float32
BF16 = mybir.dt.bfloat16
AF = mybir.ActivationFunctionType
ALU = mybir.AluOpType
AX = mybir.AxisListType

T = 4096
D = 2048
NTB = 32
FF = 5504
NFC = 43
DEPTH = 4
ALPHA = (2 * DEPTH) ** 0.25
EPS = 1e-5
NEG = -30000.0
KD = 6


class Buf:
    def __init__(self, excl=False):
        self.w = None
        self.r = {}
        self.excl = excl


class Eng:
    def __init__(self, sem, sid):
        self.sem, self.sid, self.n, self.seen, self.q = sem, sid, 0, {}, []
        self.dsems, self.dcnt, self.k = [], [], 0

    def wait(self, dep):
        if dep is None:
            return
        sid, sem, val = dep
        if self.seen.get(sid, 0) >= val:
            return
        self.e.wait_ge(sem, val)
        self.seen[sid] = val


def _deps(E, reads, writes):
    for b in reads:
        E.wait(b.w)
        if b.excl:
            for sid, d in b.r.items():
                if sid != E.sid:
                    E.wait(d)
    for b in writes:
        if b.w is not None and b.w[0] != E.sid:
            E.wait(b.w)
        for sid, d in b.r.items():
            if sid != E.sid:
                E.wait(d)


def _mark(d, reads, writes):
    for b in reads:
        b.r[d[0]] = d
    for b in writes:
        b.w = d
        b.r = {}


def op(E, fn, reads=(), writes=()):
    _deps(E, reads, writes)
    E.n += 1
    fn().then_inc(E.sem, 1)
    _mark((E.sid, E.sem, E.n), reads, writes)


def dma(Q, fn, reads=(), writes=()):
    slot = Q.k % KD
    Q.k += 1
    sid = 100 + Q.sid * 10 + slot
    sem = Q.dsems[slot]
    Q.wait((sid, sem, Q.dcnt[slot] * 16))
    for b in reads:
        Q.wait(b.w)
    for b in writes:
        Q.wait(b.w)
        for d in b.r.values():
            Q.wait(d)
    fn().then_inc(sem, 16)
    Q.dcnt[slot] += 1
    _mark((sid, sem, Q.dcnt[slot] * 16), reads, writes)


def build():
    nc = bass.Bass("TRN2", target_bir_lowering=False)
    dt = lambda n, s, d, k="ExternalInput": nc.dram_tensor(n, s, d, kind=k).ap()
    xin = dt("xin", [T, D], F32)
    diff_w_in = dt("diff_w_in", [2, D, 6144], F32)
    diff_w_out = dt("diff_w_out", [2, D, D], F32)
    win_w_in = dt("win_w_in", [2, D, 3072], F32)
    win_w_out = dt("win_w_out", [2, D, D], F32)
    ffn_w_up = dt("ffn_w_up", [4, D, 2 * FF], F32)
    ffn_w_down = dt("ffn_w_down", [4, FF, D], F32)
    lam_r = dt("lam_r", [2, 128, 256], F32)
    subg_r = dt("subg_r", [2, 128, 128], F32)
    sink_r = dt("sink_r", [2, 128, 16], F32)
    lnm_g = dt("lnm_g", [4, 128, D], F32)
    lnm_b = dt("lnm_b", [4, 128, D], F32)
    lnf_g = dt("lnf_g", [4, 128, D], F32)
    lnf_b = dt("lnf_b", [4, 128, D], F32)
    convp = dt("convp", [4, 128, 2 * NFC, 4], F32)
    ropd = dt("ropd", [2, 128, NTB, 64], F32)
    ropw = dt("ropw", [2, 128, NTB, 64], F32)
    maskb = dt("maskb", [128, 2, 32], F32)
    flags = dt("flags", [128, 2], F32)
    ident_d = dt("ident", [128, 128], BF16)
    trimask = dt("trimask", [2, 128, 512], BF16)
    y = dt("y", [T, D], F32, "ExternalOutput")
    X = nc.dram_tensor("X", [T, D], F32).ap()
    X2 = nc.dram_tensor("X2", [T, D], F32).ap()
    XT = nc.dram_tensor("XT", [16, 128, T], BF16).ap()
    XT2 = nc.dram_tensor("XT2", [16, 128, T], BF16).ap()
    QT = nc.dram_tensor("QT", [16, 128, T], BF16).ap()
    KT = nc.dram_tensor("KT", [16, 128, T], BF16).ap()
    Vd = nc.dram_tensor("Vd", [T, D], BF16).ap()
    OT = nc.dram_tensor("OT", [16, 128, T], BF16).ap()
    WUPB = nc.dram_tensor("WUPB", [4, D, 2 * FF], BF16).ap()
    WDNB = nc.dram_tensor("WDNB", [4, FF, D], BF16).ap()

    top = ExitStack()
    with top:
        nsem = 5 + 2 * KD
        sems = [top.enter_context(nc.semaphore(f"s{i}")) for i in range(nsem)]
        PE, ACT, DVE, POOL, SP = [Eng(sems[i], i) for i in range(5)]
        ENGS = [PE, ACT, DVE, POOL, SP]
        for qi, Q in enumerate((POOL, SP)):
            Q.dsems = sems[5 + qi * KD: 5 + (qi + 1) * KD]
            Q.dcnt = [0] * KD
        tE, sE, vE, gE, yE = nc.tensor, nc.scalar, nc.vector, nc.gpsimd, nc.sync
        for E_, h_ in zip(ENGS, (tE, sE, vE, gE, yE)):
            E_.e = h_

        def barrier():
            for E in ENGS:
                for F in ENGS:
                    if F is not E and F.n > 0:
                        E.wait((F.sid, F.sem, F.n))
                for Q in (POOL, SP):
                    for s in range(KD):
                        if Q.dcnt[s]:
                            E.wait((100 + Q.sid * 10 + s, Q.dsems[s], Q.dcnt[s] * 16))

        PSALL = top.enter_context(nc.psum_tensor("psall", [128, 8 * 512], F32))

        class _Bank:
            def __init__(self, i):
                self.i = i

            def __getitem__(self, key):
                return PSALL[:, self.i * 512:(self.i + 1) * 512][key]
        PS = [_Bank(i) for i in range(8)]
        PB = [Buf(excl=True) for _ in range(8)]
        ident = top.enter_context(nc.sbuf_tensor("identt", [128, 128], BF16))
        flg = top.enter_context(nc.sbuf_tensor("flg", [128, 2], F32))
        cb = Buf()
        dma(SP, lambda: yE.dma_start(out=ident[:], in_=ident_d), writes=[cb])
        dma(SP, lambda: yE.dma_start(out=flg[:], in_=flags), writes=[cb])

        uid = [0]

        def sbt(es, name, shape, dtype):
            uid[0] += 1
            return es.enter_context(nc.sbuf_tensor(f"{name}_{uid[0]}", shape, dtype)), Buf()

        def transpose_store(xb, xbB, dst3, stg, stgB, tpi):
            for q4 in range(4):
                bk = tpi[q4 % 2]
                tp = PS[bk][:].bitcast(BF16)
                for j in range(4):
                    kc = q4 * 4 + j
                    op(PE, lambda tp=tp, j=j, kc=kc: tE.transpose(tp[:, j * 128:(j + 1) * 128], xb[:, kc * 128:(kc + 1) * 128], ident[:]),
                       reads=[xbB, cb], writes=[PB[bk]])
                op(DVE, lambda tp=tp, q4=q4: vE.tensor_copy(stg[:, q4 * 4:(q4 + 1) * 4, :], tp[:, 0:512].rearrange("p (a b) -> p a b", a=4)),
                   reads=[PB[bk]], writes=[stgB])
            dma(SP, lambda: yE.dma_start(out=dst3, in_=stg[:]), reads=[stgB])

        def xt_view(XTd):
            return XTd.rearrange("k p t -> p k t")

        def layer_norm(es_t, xr, xrB, Gt, Bt, gB, Xrows, XTd, tb, tpi, tiles, defer=None):
            st, stB, mv, mvB, sc, scB, xb, xbB, stg, stgB = tiles
            for c in range(4):
                op(DVE, lambda c=c: vE.bn_stats(st[:, c, :], xr[:, c * 512:(c + 1) * 512]), reads=[xrB], writes=[stB])
            op(DVE, lambda: vE.bn_aggr(mv[:], st[:].rearrange("p a b -> p (a b)")), reads=[stB], writes=[mvB])
            op(DVE, lambda: vE.tensor_scalar_add(sc[:, 0:1], mv[:, 1:2], EPS), reads=[mvB], writes=[scB])
            op(ACT, lambda: sE.sqrt(sc[:, 1:2], sc[:, 0:1]), reads=[scB], writes=[scB])
            op(DVE, lambda: vE.reciprocal(sc[:, 2:3], sc[:, 1:2]), reads=[scB], writes=[scB])
            op(DVE, lambda: vE.tensor_scalar(xr, xr, mv[:, 0:1], sc[:, 2:3], ALU.subtract, ALU.mult), reads=[xrB, mvB, scB], writes=[xrB])
            op(DVE, lambda: vE.tensor_tensor(xr, xr, Gt[:], ALU.mult), reads=[xrB, gB], writes=[xrB])
            op(DVE, lambda: vE.tensor_tensor(xr, xr, Bt[:], ALU.add), reads=[xrB, gB], writes=[xrB])
            dma(SP, lambda: yE.dma_start(out=Xrows, in_=xr), reads=[xrB])
            op(ACT, lambda: sE.copy(xb[:], xr), reads=[xrB], writes=[xbB])
            if defer is None:
                transpose_store(xb, xbB, xt_view(XTd)[:, :, tb * 128:(tb + 1) * 128], stg, stgB, tpi)
            else:
                defer.append(lambda: transpose_store(xb, xbB, xt_view(XTd)[:, :, tb * 128:(tb + 1) * 128], stg, stgB, tpi))

        def ln_tiles(es, tag):
            st, stB = sbt(es, "st" + tag, [128, 4, 6], F32)
            mv, mvB = sbt(es, "mv" + tag, [128, 2], F32)
            sc, scB = sbt(es, "sc" + tag, [128, 4], F32)
            xb, xbB = sbt(es, "xb" + tag, [128, D], BF16)
            stg, stgB = sbt(es, "stg" + tag, [128, 16, 128], BF16)
            return (st, stB, mv, mvB, sc, scB, xb, xbB, stg, stgB)

        with ExitStack() as es:
            xrs = [sbt(es, f"pxr{i}", [128, D], F32) for i in range(2)]
            xbs = [sbt(es, f"pxb{i}", [128, D], BF16) for i in range(2)]
            stgs = [sbt(es, f"pstg{i}", [128, 16, 128], BF16) for i in range(2)]
            for tb in range(NTB):
                xr, xrB = xrs[tb % 2]
                xb, xbB = xbs[tb % 2]
                stg, stgB = stgs[tb % 2]
                dma(SP, lambda xr=xr, tb=tb: yE.dma_start(out=xr[:], in_=xin[tb * 128:(tb + 1) * 128, :]), writes=[xrB])
                op(ACT, lambda xr=xr, xb=xb: sE.copy(xb[:], xr[:]), reads=[xrB], writes=[xbB])
                transpose_store(xb, xbB, xt_view(XT)[:, :, tb * 128:(tb + 1) * 128], stg, stgB, [6, 7])
            barrier()

        for L in range(DEPTH):
            j = L // 2
            is_diff = (L % 2 == 0)
            Xsrc = xin if L == 0 else X
            w_in = diff_w_in[j] if is_diff else win_w_in[j]
            ncols = 6144 if is_diff else 3072
            m, hh = (8, 8) if is_diff else (4, 16)
            dh = 512 // m
            w_v = w_in.rearrange("(k p) n -> p k n", p=128)
            with ExitStack() as es:
                xts = [sbt(es, f"qxt{i}", [128, 16, 1024], BF16) for i in range(2)]
                wts = [sbt(es, f"qwt{i}", [128, 16, 512], BF16) for i in range(2)]
                rop, ropB = sbt(es, "rop", [128, 2, NTB, 64], F32)
                t1, t1B = sbt(es, "rt1", [128, m, hh], F32)
                t2, t2B = sbt(es, "rt2", [128, m, hh], F32)
                qbs = [sbt(es, f"qb{i}", [128, 512], BF16) for i in range(2)]
                stgs = [sbt(es, f"qstg{i}", [128, 4, 1024], BF16) for i in range(2)]
                rsrc = ropd if is_diff else ropw
                dma(SP, lambda: yE.dma_start(out=rop[:], in_=rsrc.rearrange("c p t f -> p c t f")), writes=[ropB])
                cnt = 0
                wcnt = 0
                qpend = []

                def q_load(tg):
                    xt, xtB = xts[tg % 2]
                    dma(SP, lambda: yE.dma_start(out=xt[:], in_=xt_view(XT)[:, :, tg * 1024:(tg + 1) * 1024]), writes=[xtB])
                q_load(0)
                for tg in range(4):
                    xt, xtB = xts[tg % 2]
                    if tg + 1 < 4:
                        q_load(tg + 1)
                    for nt in range(ncols // 512):
                        wt, wtB = wts[wcnt % 2]
                        stg, stgB = stgs[wcnt % 2]
                        wcnt += 1
                        dma(POOL, lambda wt=wt, nt=nt: gE.dma_start(out=wt[:], in_=w_v[:, :, nt * 512:(nt + 1) * 512]), writes=[wtB])
                        if is_diff:
                            kind = "q" if nt < 4 else ("k" if nt < 8 else "v")
                            h0 = (nt % 4) * 4
                        else:
                            kind = "q" if nt < 4 else ("k" if nt == 4 else "v")
                            h0 = nt * 4 if nt < 4 else 0
                        for tb in range(8):
                            bk = cnt % 4
                            cnt += 1
                            ps = PS[bk]
                            tbg = tg * 8 + tb
                            for kc in range(16):
                                op(PE, lambda ps=ps, xt=xt, wt=wt, kc=kc, tb=tb: tE.matmul(ps[:], lhsT=xt[:, kc, tb * 128:(tb + 1) * 128], rhs=wt[:, kc, :], start=(kc == 0), stop=(kc == 15)),
                                   reads=[xtB, wtB], writes=[PB[bk]])
                            while qpend:
                                qpend.pop(0)()
                            qb, qbB = qbs[cnt % 2]
                            if kind == "v":
                                op(DVE, lambda qb=qb, ps=ps: vE.tensor_copy(qb[:], ps[:]), reads=[PB[bk]], writes=[qbB])
                                vcols = (nt - 8) * 512 if is_diff else 0
                                dma(SP, lambda qb=qb, tbg=tbg, vcols=vcols: yE.dma_start(out=Vd[tbg * 128:(tbg + 1) * 128, vcols:vcols + 512], in_=qb[:]), reads=[qbB])
                                continue
                            ps3 = ps[:].rearrange("p (m d) -> p m d", m=m)
                            qb3 = qb[:].rearrange("p (m d) -> p m d", m=m)
                            c3 = rop[:, 0, tbg, :].rearrange("p (m h) -> p m h", m=m)
                            s3 = rop[:, 1, tbg, :].rearrange("p (m h) -> p m h", m=m)
                            x1, x2 = ps3[:, :, 0:hh], ps3[:, :, hh:2 * hh]
                            R = [PB[bk], ropB]
                            op(DVE, lambda x1=x1, c3=c3: vE.tensor_tensor(t1[:], x1, c3, ALU.mult), reads=R, writes=[t1B])
                            op(DVE, lambda x2=x2, s3=s3: vE.tensor_tensor(t2[:], x2, s3, ALU.mult), reads=R, writes=[t2B])
                            op(DVE, lambda qb3=qb3: vE.tensor_tensor(qb3[:, :, 0:hh], t1[:], t2[:], ALU.subtract), reads=[t1B, t2B], writes=[qbB])
                            op(DVE, lambda x2=x2, c3=c3: vE.tensor_tensor(t1[:], x2, c3, ALU.mult), reads=R, writes=[t1B])
                            op(DVE, lambda x1=x1, s3=s3: vE.tensor_tensor(t2[:], x1, s3, ALU.mult), reads=R, writes=[t2B])
                            op(DVE, lambda qb3=qb3: vE.tensor_tensor(qb3[:, :, hh:2 * hh], t1[:], t2[:], ALU.add), reads=[t1B, t2B], writes=[qbB])
                            op(DVE, lambda qb3=qb3, ps3=ps3: vE.tensor_copy(qb3[:, :, 2 * hh:], ps3[:, :, 2 * hh:]), reads=[PB[bk]], writes=[qbB])

                            def trp(qb=qb, qbB=qbB, stg=stg, stgB=stgB, tb=tb, tbk=4 + (cnt % 2), last=(tb == 7), kind=kind, h0=h0, tg=tg):
                                tp = PS[tbk][:].bitcast(BF16)
                                for jj in range(4):
                                    op(PE, lambda jj=jj: tE.transpose(tp[:, jj * 128:(jj + 1) * 128], qb[:, jj * 128:(jj + 1) * 128], ident[:]),
                                       reads=[qbB, cb], writes=[PB[tbk]])
                                op(DVE, lambda: vE.tensor_copy(stg[:, :, tb * 128:(tb + 1) * 128], tp[:, 0:512].rearrange("p (a b) -> p a b", a=4)),
                                   reads=[PB[tbk]], writes=[stgB])
                                if last:
                                    dstT = QT if kind == "q" else KT
                                    dma(SP, lambda: yE.dma_start(out=dstT[h0:h0 + 4].rearrange("h p t -> p h t")[:, :, tg * 1024:(tg + 1) * 1024], in_=stg[:]), reads=[stgB])
                            qpend.append(trp)
                while qpend:
                    qpend.pop(0)()
                barrier()
            if L == 0:
                for L2 in range(DEPTH):
                    srcu = ffn_w_up[L2].rearrange("(k p) n -> p k n", p=128)
                    dstu = WUPB[L2].rearrange("(k p) n -> p k n", p=128)
                    for cbk in range(16):
                        dma(POOL, lambda srcu=srcu, dstu=dstu, cbk=cbk: gE.dma_start(out=dstu[:, :, cbk * 688:(cbk + 1) * 688], in_=srcu[:, :, cbk * 688:(cbk + 1) * 688]))
                    srcd = ffn_w_down[L2].rearrange("(i p) n -> p i n", p=128)
                    dstd = WDNB[L2].rearrange("(i p) n -> p i n", p=128)
                    for i0_ in range(0, NFC, 11):
                        i1_ = min(NFC, i0_ + 11)
                        dma(POOL, lambda srcd=srcd, dstd=dstd, i0_=i0_, i1_=i1_: gE.dma_start(out=dstd[:, i0_:i1_, :], in_=srcd[:, i0_:i1_, :]))
            if is_diff:
                lam_init = 0.8 - 0.6 * math.exp(-0.3 * L)
                with ExitStack() as es:
                    qTs = [sbt(es, f"aq{i}", [128, T], BF16) for i in range(2)]
                    kTs = [sbt(es, f"ak{i}", [128, T], BF16) for i in range(2)]
                    vas = [sbt(es, f"av{i}", [128, 32, 130], BF16) for i in range(2)]
                    Pb = [sbt(es, f"ap{i}", [128, 512], BF16) for i in range(3)]
                    lm, lmB = sbt(es, "lm", [128, 256], F32)
                    gs, gsB = sbt(es, "gs", [128, 128], F32)
                    mk, mkB = sbt(es, "mk", [128, 2, 32], F32)
                    sm, smB = sbt(es, "sm", [128, 16], F32)
                    of, ofB = sbt(es, "of", [128, 128], F32)
                    sq, sqB = sbt(es, "sq", [128, 128], F32)
                    obs = [sbt(es, f"ob{i}", [128, 128], BF16) for i in range(4)]
                    ostgs = [sbt(es, f"aost{i}", [128, 512], BF16) for i in range(2)]
                    dma(SP, lambda: yE.dma_start(out=lm[:], in_=lam_r[j]), writes=[lmB])
                    dma(SP, lambda: yE.dma_start(out=gs[:], in_=subg_r[j]), writes=[gsB])
                    dma(SP, lambda: yE.dma_start(out=mk[:], in_=maskb), writes=[mkB])
                    for i in range(2):
                        op(DVE, lambda i=i: vE.memset(vas[i][0][:], 1.0), writes=[vas[i][1]])
                    op(DVE, lambda: vE.tensor_scalar_mul(gs[:], gs[:], 1.0 - lam_init), reads=[gsB], writes=[gsB])
                    op(DVE, lambda: vE.tensor_tensor(sq[:, 0:64], lm[:, 0:64], lm[:, 64:128], ALU.mult), reads=[lmB], writes=[sqB])
                    op(DVE, lambda: vE.tensor_tensor(sq[:, 64:128], lm[:, 128:192], lm[:, 192:256], ALU.mult), reads=[lmB], writes=[sqB])
                    op(DVE, lambda: vE.reduce_sum(sm[:, 0:1], sq[:, 0:64], AX.X), reads=[sqB], writes=[smB])
                    op(DVE, lambda: vE.reduce_sum(sm[:, 1:2], sq[:, 64:128], AX.X), reads=[sqB], writes=[smB])
                    op(ACT, lambda: sE.activation(sm[:, 2:4], sm[:, 0:2], AF.Exp), reads=[smB], writes=[smB])
                    op(DVE, lambda: vE.tensor_tensor(sm[:, 4:5], sm[:, 3:4], sm[:, 2:3], ALU.subtract), reads=[smB], writes=[smB])
                    op(DVE, lambda: vE.tensor_scalar_add(sm[:, 5:6], sm[:, 4:5], -lam_init), reads=[smB], writes=[smB])
                    Vv = Vd.rearrange("(kb p) e -> p kb e", p=128)
                    P2 = [sbt(es, f"ap2{i}", [128, 1024], BF16) for i in range(3)]
                    accS8, accSB = sbt(es, "accS8", [128, 8, 130], F32)
                    rz, rzB = sbt(es, "rz", [128, 28], F32)
                    of4, ofB = sbt(es, "of4", [128, 4, 128], F32)
                    sq4, sqB4 = sbt(es, "sq4", [128, 4, 128], F32)

                    def a_load(h):
                        qT, qTB = qTs[h % 2]
                        kT, kTB = kTs[h % 2]
                        va, vaB = vas[h % 2]
                        dma(SP, lambda: yE.dma_start(out=qT[:], in_=QT[h]), writes=[qTB])
                        dma(SP, lambda: yE.dma_start(out=kT[:], in_=KT[h]), writes=[kTB])
                        dma(SP, lambda: yE.dma_start(out=va[:, :, 0:128], in_=Vv[:, :, h * 128:(h + 1) * 128]), writes=[vaB])

                    steps = [(h, qg, kb) for h in range(16) for qg in range(8) for kb in range(32)]
                    accs = {}
                    for qb_ in range(4):
                        for c in range(2):
                            a = qb_ * 2 + c
                            accs[(qb_, c)] = (4 + a // 3, (a % 3) * 130)

                    def emit_qk(idx):
                        h, qg, kb = steps[idx]
                        qT, qTB = qTs[h % 2]
                        kT, kTB = kTs[h % 2]
                        sp = idx % 2
                        for c in range(2):
                            bk = 2 * sp + c
                            op(PE, lambda c=c, bk=bk: tE.matmul(PS[bk][:], lhsT=kT[64 * c:64 * c + 64, kb * 128:(kb + 1) * 128], rhs=qT[64 * c:64 * c + 64, qg * 512:(qg + 1) * 512], start=True, stop=True),
                               reads=[kTB, qTB], writes=[PB[bk]])

                    def emit_exp(idx):
                        h, qg, kb = steps[idx]
                        sp = idx % 2
                        P, PBf = P2[idx % 3]
                        seg = qg // 4
                        op(ACT, lambda: sE.activation(P[:], PSALL[:, sp * 1024:(sp + 1) * 1024], AF.Exp, bias=mk[:, seg, kb:kb + 1], scale=0.125),
                           reads=[PB[2 * sp], PB[2 * sp + 1], mkB], writes=[PBf])

                    def emit_pv(idx):
                        h, qg, kb = steps[idx]
                        va, vaB = vas[h % 2]
                        P, PBf = P2[idx % 3]
                        for c in range(2):
                            for qb_ in range(4):
                                bk, off = accs[(qb_, c)]
                                op(PE, lambda bk=bk, off=off, qb_=qb_, c=c: tE.matmul(PS[bk][:, off:off + 129], lhsT=P[:, c * 512 + qb_ * 128:c * 512 + (qb_ + 1) * 128], rhs=va[:, kb, 0:129], start=(kb == 0 and c == 0 and qb_ in (0, 2, 3)), stop=(kb == 31), skip_group_check=True),
                                   reads=[PBf, vaB], writes=[PB[bk]])

                    pending = []

                    def emit_epilogue(h, qg, ocnt):
                        ostg, ostgB = ostgs[ocnt % 2]
                        for b3, na in ((0, 3), (1, 3), (2, 2)):
                            op(DVE, lambda b3=b3, na=na: vE.tensor_copy(accS8[:, 3 * b3:3 * b3 + na, :], PS[4 + b3][:, 0:130 * na].rearrange("p (a c) -> p a c", a=na)),
                               reads=[PB[4 + b3]], writes=[accSB])
                        op(DVE, lambda: vE.reciprocal(rz[:, 0:8].rearrange("p (a o) -> p a o", o=1), accS8[:, :, 128:129]), reads=[accSB], writes=[rzB])
                        op(DVE, lambda: vE.tensor_scalar_mul(rz[:, 8:12], rz[:, 1:8:2], sm[:, 5:6]), reads=[rzB, smB], writes=[rzB])
                        for qb_ in range(4):
                            op(DVE, lambda qb_=qb_: vE.tensor_scalar_mul(of4[:, qb_, :], accS8[:, 2 * qb_, 0:128], rz[:, 2 * qb_:2 * qb_ + 1]), reads=[accSB, rzB], writes=[ofB])
                            op(DVE, lambda qb_=qb_: vE.scalar_tensor_tensor(of4[:, qb_, :], accS8[:, 2 * qb_ + 1, 0:128], rz[:, 8 + qb_:9 + qb_], of4[:, qb_, :], ALU.mult, ALU.add), reads=[accSB, rzB, ofB], writes=[ofB])
                        op(DVE, lambda: vE.tensor_tensor(sq4[:], of4[:], of4[:], ALU.mult), reads=[ofB], writes=[sqB])
                        op(DVE, lambda: vE.reduce_sum(rz[:, 12:16], sq4[:], AX.X), reads=[sqB], writes=[rzB])
                        op(DVE, lambda: vE.tensor_scalar(rz[:, 16:20], rz[:, 12:16], 1.0 / 128, EPS, ALU.mult, ALU.add), reads=[rzB], writes=[rzB])
                        op(ACT, lambda: sE.activation(rz[:, 20:24], rz[:, 16:20], AF.Ln), reads=[rzB], writes=[rzB])
                        op(ACT, lambda: sE.activation(rz[:, 24:28], rz[:, 20:24], AF.Exp, scale=-0.5), reads=[rzB], writes=[rzB])
                        for qb_ in range(4):
                            ob, obB = obs[qb_]
                            op(DVE, lambda qb_=qb_, ob=ob: vE.scalar_tensor_tensor(ob[:], of4[:, qb_, :], rz[:, 24 + qb_:25 + qb_], gs[:], ALU.mult, ALU.mult), reads=[ofB, rzB, gsB], writes=[obB])

                        def part2():
                            tp = PS[7][:].bitcast(BF16)
                            for qb_ in range(4):
                                ob, obB = obs[qb_]
                                op(PE, lambda qb_=qb_, ob=ob: tE.transpose(tp[:, qb_ * 128:(qb_ + 1) * 128], ob[:], ident[:]), reads=[obB, cb], writes=[PB[7]])
                            op(DVE, lambda: vE.tensor_copy(ostg[:], tp[:, 0:512]), reads=[PB[7]], writes=[ostgB])
                            dma(SP, lambda: yE.dma_start(out=OT[h][:, qg * 512:(qg + 1) * 512], in_=ostg[:]), reads=[ostgB])
                        pending.append(part2)

                    a_load(0)
                    emit_qk(0)
                    emit_qk(1)
                    ocnt = 0
                    for idx, (h, qg, kb) in enumerate(steps):
                        if qg == 0 and kb == 0 and h + 1 < 16:
                            a_load(h + 1)
                        emit_exp(idx)
                        emit_pv(idx)
                        if idx + 2 < len(steps):
                            emit_qk(idx + 2)
                        if kb == 8 and pending:
                            pending.pop(0)()
                        if kb == 31:
                            emit_epilogue(h, qg, ocnt)
                            ocnt += 1
                    while pending:
                        pending.pop(0)()
                    barrier()
            else:
                with ExitStack() as es:
                    q4s = [sbt(es, f"wq{i}", [128, 4, T], BF16) for i in range(2)]
                    kTs = [sbt(es, f"wk{i}", [128, T], BF16) for i in range(2)]
                    vas = [sbt(es, f"wv{i}", [128, 32, 130], BF16) for i in range(2)]
                    Pb = [sbt(es, f"wp{i}", [128, 512], BF16) for i in range(3)]
                    tm, tmB = sbt(es, "tm", [128, 2, 512], BF16)
                    sk, skB = sbt(es, "sk", [128, 16], F32)
                    sm, smB = sbt(es, "wsm", [128, 8], F32)
                    wacc, waccB = sbt(es, "wacc", [128, 4, 130], F32)
                    wz, wzB = sbt(es, "wz", [128, 8], F32)
                    obs4 = [sbt(es, f"wob4{i}", [128, 128], BF16) for i in range(4)]
                    obs = [sbt(es, f"wob{i}", [128, 128], BF16) for i in range(2)]
                    ostgs = [sbt(es, f"wost{i}", [128, 4, 128], BF16) for i in range(2)]
                    dma(SP, lambda: yE.dma_start(out=tm[:], in_=trimask.rearrange("c p f -> p c f")), writes=[tmB])
                    dma(SP, lambda: yE.dma_start(out=sk[:], in_=sink_r[j]), writes=[skB])
                    op(ACT, lambda: sE.activation(sk[:], sk[:], AF.Exp), reads=[skB], writes=[skB])
                    for i in range(2):
                        op(POOL, lambda i=i: gE.memset(vas[i][0][:], 1.0), writes=[vas[i][1]])
                    Vv = Vd.rearrange("(kb p) e -> p kb e", p=128)
                    accb = [(4, 0), (4, 130), (4, 260), (5, 0)]
                    wsteps = []
                    for g in range(4):
                        for n in range(32):
                            kbs = [kb for kb in (n - 1, n, n + 1) if 0 <= kb < 32]
                            for i_, kb in enumerate(kbs):
                                wsteps.append((g, n, kb, i_, len(kbs)))

                    def w_load(g):
                        q4, q4B = q4s[g % 2]
                        kT, kTB = kTs[g % 2]
                        va, vaB = vas[g % 2]
                        dma(SP, lambda: yE.dma_start(out=q4[:], in_=QT[4 * g:4 * g + 4].rearrange("h p t -> p h t")), writes=[q4B])
                        dma(SP, lambda: yE.dma_start(out=kT[:], in_=KT[g]), writes=[kTB])
                        dma(SP, lambda: yE.dma_start(out=va[:, :, 0:128], in_=Vv[:, :, g * 128:(g + 1) * 128]), writes=[vaB])

                    def w_qk(idx):
                        g, n, kb, i_, nk = wsteps[idx]
                        q4, q4B = q4s[g % 2]
                        kT, kTB = kTs[g % 2]
                        sb_ = idx % 4
                        op(PE, lambda: tE.matmul(PS[sb_][:], lhsT=kT[:, kb * 128:(kb + 1) * 128], rhs=q4[:, :, n * 128:(n + 1) * 128], start=True, stop=True),
                           reads=[kTB, q4B], writes=[PB[sb_]])

                    def w_exp(idx):
                        g, n, kb, i_, nk = wsteps[idx]
                        sb_ = idx % 4
                        P, PBf = Pb[idx % 3]
                        if (n, kb) in ((15, 16), (16, 15)):
                            op(ACT, lambda: sE.activation(P[:], PS[sb_][:], AF.Exp, bias=flg[:, 1:2], scale=128 ** -0.5), reads=[PB[sb_], cb], writes=[PBf])
                        else:
                            op(ACT, lambda: sE.activation(P[:], PS[sb_][:], AF.Exp, scale=128 ** -0.5), reads=[PB[sb_]], writes=[PBf])
                        if kb != n:
                            mi = 0 if kb < n else 1
                            op(POOL, lambda: gE.tensor_tensor(P[:], P[:], tm[:, mi, :], ALU.mult), reads=[PBf, tmB], writes=[PBf])

                    def w_pv(idx):
                        g, n, kb, i_, nk = wsteps[idx]
                        va, vaB = vas[g % 2]
                        P, PBf = Pb[idx % 3]
                        for r in range(4):
                            bk, off = accb[r]
                            op(PE, lambda bk=bk, off=off, r=r: tE.matmul(PS[bk][:, off:off + 129], lhsT=P[:, r * 128:(r + 1) * 128], rhs=va[:, kb, 0:129], start=(i_ == 0 and r in (0, 3)), stop=(i_ == nk - 1), skip_group_check=True),
                               reads=[PBf, vaB], writes=[PB[bk]])

                    wpend = []

                    def w_epi(g, n, ocnt):
                        ostg, ostgB = ostgs[ocnt % 2]
                        op(DVE, lambda: vE.tensor_copy(wacc[:, 0:3, :], PS[4][:, 0:390].rearrange("p (a c) -> p a c", a=3)), reads=[PB[4]], writes=[waccB])
                        op(DVE, lambda: vE.tensor_copy(wacc[:, 3:4, :], PS[5][:, 0:130].rearrange("p (a c) -> p a c", a=1)), reads=[PB[5]], writes=[waccB])
                        op(DVE, lambda: vE.tensor_tensor(wz[:, 0:4].rearrange("p (a o) -> p a o", o=1), wacc[:, :, 128:129], sk[:, 4 * g:4 * g + 4].rearrange("p (a o) -> p a o", o=1), ALU.add), reads=[waccB, skB], writes=[wzB])
                        op(DVE, lambda: vE.reciprocal(wz[:, 4:8], wz[:, 0:4]), reads=[wzB], writes=[wzB])
                        for r in range(4):
                            ob, obB = obs4[r]
                            op(DVE, lambda r=r, ob=ob: vE.tensor_scalar_mul(ob[:], wacc[:, r, 0:128], wz[:, 4 + r:5 + r]), reads=[waccB, wzB], writes=[obB])

                        def part2():
                            tp = PS[7][:].bitcast(BF16)
                            for r in range(4):
                                ob, obB = obs4[r]
                                op(PE, lambda r=r, ob=ob: tE.transpose(tp[:, r * 128:(r + 1) * 128], ob[:], ident[:]), reads=[obB, cb], writes=[PB[7]])
                            op(DVE, lambda: vE.tensor_copy(ostg[:], tp[:, 0:512].rearrange("p (a b) -> p a b", a=4)), reads=[PB[7]], writes=[ostgB])
                            dma(SP, lambda: yE.dma_start(out=OT[4 * g:4 * g + 4].rearrange("h p t -> p h t")[:, :, n * 128:(n + 1) * 128], in_=ostg[:]), reads=[ostgB])
                        wpend.append(part2)

                    w_load(0)
                    w_qk(0)
                    w_qk(1)
                    ocnt = 0
                    for idx, (g, n, kb, i_, nk) in enumerate(wsteps):
                        if n == 0 and i_ == 0 and g + 1 < 4:
                            w_load(g + 1)
                        w_exp(idx)
                        w_pv(idx)
                        if idx + 2 < len(wsteps):
                            w_qk(idx + 2)
                        if i_ == 1 and wpend:
                            wpend.pop(0)()
                        if i_ == nk - 1:
                            w_epi(g, n, ocnt)
                            ocnt += 1
                    while wpend:
                        wpend.pop(0)()
                    barrier()
            w_out = (diff_w_out if is_diff else win_w_out)[j].rearrange("(k p) n -> p k n", p=128)
            with ExitStack() as es:
                wo, woB = sbt(es, "wo", [128, 16, D], BF16)
                Gt, gB = sbt(es, "lng", [128, D], F32)
                Bt, _ = sbt(es, "lnb", [128, D], F32)
                ots = [sbt(es, f"oot{i}", [128, 16, 128], BF16) for i in range(2)]
                xrs = [sbt(es, f"oxr{i}", [128, D], F32) for i in range(2)]
                lts = [ln_tiles(es, f"o{i}") for i in range(2)]
                for q in range(4):
                    dma(POOL, lambda q=q: gE.dma_start(out=wo[:, 4 * q:4 * q + 4, :], in_=w_out[:, 4 * q:4 * q + 4, :]), writes=[woB])
                dma(SP, lambda: yE.dma_start(out=Gt[:], in_=lnm_g[L]), writes=[gB])
                dma(SP, lambda: yE.dma_start(out=Bt[:], in_=lnm_b[L]), writes=[gB])
                cnt = 0
                opend = []

                def op_load(tb):
                    ot, otB = ots[tb % 2]
                    xr, xrB = xrs[tb % 2]
                    dma(SP, lambda: yE.dma_start(out=ot[:], in_=OT.rearrange("h p t -> p h t")[:, :, tb * 128:(tb + 1) * 128]), writes=[otB])
                    dma(SP, lambda: yE.dma_start(out=xr[:], in_=Xsrc[tb * 128:(tb + 1) * 128, :]), writes=[xrB])
                op_load(0)
                for tb in range(NTB):
                    ot, otB = ots[tb % 2]
                    xr, xrB = xrs[tb % 2]
                    if tb + 1 < NTB:
                        op_load(tb + 1)
                    for nt in range(4):
                        bk = cnt % 6
                        cnt += 1
                        ps = PS[bk]
                        for h in range(16):
                            op(PE, lambda ps=ps, ot=ot, h=h, nt=nt: tE.matmul(ps[:], lhsT=ot[:, h, :], rhs=wo[:, h, nt * 512:(nt + 1) * 512], start=(h == 0), stop=(h == 15)),
                               reads=[otB, woB], writes=[PB[bk]])
                        op(DVE, lambda xr=xr, ps=ps, nt=nt: vE.scalar_tensor_tensor(xr[:, nt * 512:(nt + 1) * 512], xr[:, nt * 512:(nt + 1) * 512], ALPHA, ps[:], ALU.mult, ALU.add),
                           reads=[PB[bk], xrB], writes=[xrB])
                    while opend:
                        opend.pop(0)()
                    layer_norm(es, xr[:], xrB, Gt, Bt, gB, X2[tb * 128:(tb + 1) * 128, :], XT2, tb, [6, 7], lts[tb % 2], defer=opend)
                while opend:
                    opend.pop(0)()
                barrier()
            wup = WUPB[L].rearrange("(k p) n -> p k n", p=128)
            wdn = WDNB[L].rearrange("(i p) n -> p i n", p=128)
            Xdst = y if L == DEPTH - 1 else X
            with ExitStack() as es:
                xt, xtB = sbt(es, "fxt", [128, 16, 514], BF16)
                xr4, xr4B = sbt(es, "fxr", [128, 4, D], F32)
                wgs = [sbt(es, f"fwg{i}", [128, 16, 512], BF16) for i in range(2)]
                wus = [sbt(es, f"fwu{i}", [128, 16, 512], BF16) for i in range(2)]
                tgs = [sbt(es, f"ftg{i}", [128, 512], F32) for i in range(2)]
                tus = [sbt(es, f"ftu{i}", [128, 512], F32) for i in range(2)]
                sgs = [sbt(es, f"fsg{i}", [128, 512], F32) for i in range(2)]
                aT, aTB = sbt(es, "faT", [128, 11, 512], BF16)
                wds = [sbt(es, f"fwd{i}", [128, 11, 512], BF16) for i in range(2)]
                cp, cpB = sbt(es, "fcp", [128, 2 * NFC, 4], F32)
                Gt, gB = sbt(es, "flng", [128, D], F32)
                Bt, _ = sbt(es, "flnb", [128, D], F32)
                lts = [ln_tiles(es, f"f{i}") for i in range(2)]
                dma(SP, lambda: yE.dma_start(out=cp[:], in_=convp[L]), writes=[cpB])
                dma(SP, lambda: yE.dma_start(out=Gt[:], in_=lnf_g[L]), writes=[gB])
                dma(SP, lambda: yE.dma_start(out=Bt[:], in_=lnf_b[L]), writes=[gB])
                XT2v = xt_view(XT2)

                def load_xt(tg):
                    t0 = tg * 512
                    dma(SP, lambda: yE.dma_start(out=xt[:, :, 1:513], in_=XT2v[:, :, t0:t0 + 512]), writes=[xtB])
                    with nc.allow_non_contiguous_dma(reason="halo column"):
                        if tg > 0:
                            dma(SP, lambda: yE.dma_start(out=xt[:, :, 0:1], in_=XT2v[:, :, t0 - 1:t0]), writes=[xtB])
                        else:
                            op(DVE, lambda: vE.memset(xt[:, :, 0:1], 0.0), writes=[xtB])
                        if tg < 7:
                            dma(SP, lambda: yE.dma_start(out=xt[:, :, 513:514], in_=XT2v[:, :, t0 + 512:t0 + 513]), writes=[xtB])
                        else:
                            op(DVE, lambda: vE.memset(xt[:, :, 513:514], 0.0), writes=[xtB])
                    if tg == 4:
                        op(DVE, lambda: vE.tensor_scalar_mul(xt[:, :, 0:1], xt[:, :, 0:1], flg[:, 0:1]), reads=[xtB, cb], writes=[xtB])
                    if tg == 3:
                        op(DVE, lambda: vE.tensor_scalar_mul(xt[:, :, 513:514], xt[:, :, 513:514], flg[:, 0:1]), reads=[xtB, cb], writes=[xtB])

                ccnt = 0
                wcnt = 0
                dcn = 0
                ycnt = 0
                load_xt(0)
                for tg in range(8):
                    t0 = tg * 512
                    dma(SP, lambda t0=t0: yE.dma_start(out=xr4[:], in_=X2[t0:t0 + 512, :].rearrange("(a p) d -> p a d", p=128)), writes=[xr4B])
                    for qf in range(4):
                        c0 = qf * 11
                        c1 = min(NFC, c0 + 11)
                        ca = c0
                        while ca < c1:
                            ncw = min(4, c1 - ca)
                            wg, wgB = wgs[wcnt % 2]
                            wu, wuB = wus[wcnt % 2]
                            wcnt += 1
                            dma(POOL, lambda wg=wg, ca=ca, ncw=ncw: gE.dma_start(out=wg[:, :, 0:ncw * 128], in_=wup[:, :, ca * 128:(ca + ncw) * 128]), writes=[wgB])
                            dma(POOL, lambda wu=wu, ca=ca, ncw=ncw: gE.dma_start(out=wu[:, :, 0:ncw * 128], in_=wup[:, :, FF + ca * 128:FF + (ca + ncw) * 128]), writes=[wuB])
                            for ii in range(ncw):
                                i = ca + ii
                                tg_, tgB = tgs[ccnt % 2]
                                tu_, tuB = tus[ccnt % 2]
                                sg_, sgB = sgs[ccnt % 2]
                                gb, ub, hb = ccnt % 2, 2 + ccnt % 2, 4 + ccnt % 2
                                ccnt += 1
                                for (w_, wB_, bnk, hoff) in ((wg, wgB, gb, 0), (wu, wuB, ub, 2)):
                                    for kc in range(16):
                                        op(PE, lambda w_=w_, bnk=bnk, kc=kc, ii=ii: tE.matmul(PS[bnk][:], lhsT=w_[:, kc, ii * 128:(ii + 1) * 128], rhs=xt[:, kc, 1:513], start=(kc == 0), stop=(kc == 15)),
                                           reads=[wB_, xtB], writes=[PB[bnk]])
                                    for kc in range(16):
                                        op(PE, lambda w_=w_, hb=hb, hoff=hoff, kc=kc, ii=ii: tE.matmul(PS[hb][:, hoff:hoff + 2], lhsT=w_[:, kc, ii * 128:(ii + 1) * 128], rhs=xt[:, kc, 0:514:513], start=(kc == 0), stop=(kc == 15)),
                                           reads=[wB_, xtB], writes=[PB[hb]])
                                for (tt, ttB, bnk, hoff, ci) in ((tg_, tgB, gb, 0, i), (tu_, tuB, ub, 2, NFC + i)):
                                    Gp, Hp = PS[bnk], PS[hb]
                                    w0, w1, w2, bb = cp[:, ci, 0:1], cp[:, ci, 1:2], cp[:, ci, 2:3], cp[:, ci, 3:4]
                                    op(DVE, lambda tt=tt, Gp=Gp, w1=w1, bb=bb: vE.tensor_scalar(tt[:], Gp[:], w1, bb, ALU.mult, ALU.add), reads=[PB[bnk], cpB], writes=[ttB])
                                    op(DVE, lambda tt=tt, Gp=Gp, w0=w0: vE.scalar_tensor_tensor(tt[:, 1:512], Gp[:, 0:511], w0, tt[:, 1:512], ALU.mult, ALU.add), reads=[PB[bnk], cpB, ttB], writes=[ttB])
                                    op(DVE, lambda tt=tt, Hp=Hp, w0=w0, hoff=hoff: vE.scalar_tensor_tensor(tt[:, 0:1], Hp[:, hoff:hoff + 1], w0, tt[:, 0:1], ALU.mult, ALU.add), reads=[PB[hb], cpB, ttB], writes=[ttB])
                                    op(DVE, lambda tt=tt, Gp=Gp, w2=w2: vE.scalar_tensor_tensor(tt[:, 0:511], Gp[:, 1:512], w2, tt[:, 0:511], ALU.mult, ALU.add), reads=[PB[bnk], cpB, ttB], writes=[ttB])
                                    op(DVE, lambda tt=tt, Hp=Hp, w2=w2, hoff=hoff: vE.scalar_tensor_tensor(tt[:, 511:512], Hp[:, hoff + 1:hoff + 2], w2, tt[:, 511:512], ALU.mult, ALU.add), reads=[PB[hb], cpB, ttB], writes=[ttB])
                                op(ACT, lambda sg_=sg_, tg_=tg_: sE.activation(sg_[:], tg_[:], AF.Silu), reads=[tgB], writes=[sgB])
                                op(DVE, lambda sg_=sg_, tu_=tu_, i=i, c0=c0: vE.tensor_tensor(aT[:, i - c0, :], sg_[:], tu_[:], ALU.mult), reads=[sgB, tuB], writes=[aTB])
                            ca += ncw
                        if qf == 3 and tg < 7:
                            load_xt(tg + 1)
                        nch = c1 - c0
                        for nt in range(4):
                            wd, wdB = wds[dcn % 2]
                            dcn += 1
                            dma(POOL, lambda wd=wd, nt=nt, c0=c0, nch=nch: gE.dma_start(out=wd[:, 0:nch, :], in_=wdn[:, c0:c0 + nch, nt * 512:(nt + 1) * 512]), writes=[wdB])
                            for tb in range(4):
                                bk = 6 + ycnt % 2
                                ycnt += 1
                                for jj in range(nch):
                                    op(PE, lambda bk=bk, jj=jj, tb=tb, wd=wd, nch=nch: tE.matmul(PS[bk][:], lhsT=aT[:, jj, tb * 128:(tb + 1) * 128], rhs=wd[:, jj, :], start=(jj == 0), stop=(jj == nch - 1)),
                                       reads=[aTB, wdB], writes=[PB[bk]])
                                dst = xr4[:, tb, nt * 512:(nt + 1) * 512]
                                if qf == 0:
                                    op(DVE, lambda dst=dst, bk=bk: vE.scalar_tensor_tensor(dst, dst, ALPHA, PS[bk][:], ALU.mult, ALU.add), reads=[PB[bk], xr4B], writes=[xr4B])
                                else:
                                    op(DVE, lambda dst=dst, bk=bk: vE.tensor_tensor(dst, dst, PS[bk][:], ALU.add), reads=[PB[bk], xr4B], writes=[xr4B])
                    fpend = []
                    for tb in range(4):
                        tbg = tg * 4 + tb
                        layer_norm(es, xr4[:, tb, :], xr4B, Gt, Bt, gB, Xdst[tbg * 128:(tbg + 1) * 128, :], XT, tbg, [6, 7], lts[tb % 2], defer=fpend)
                        if len(fpend) > 1:
                            fpend.pop(0)()
                    while fpend:
                        fpend.pop(0)()
                barrier()
        barrier()

    return nc


def _rope_rep(pos, rot, m):
    inv = (np.float32(500000.0) ** (-(np.arange(0, rot, 2, dtype=np.float32)) / np.float32(rot))).astype(np.float32)
    ang = (pos[:, None].astype(np.float32) * inv[None, :]).astype(np.float32)
    c, s = np.cos(ang).astype(np.float32), np.sin(ang).astype(np.float32)
    rep = lambda a: np.tile(a[:, None, :], (1, m, 1)).reshape(T, -1)
    out = np.stack([rep(c), rep(s)], 0)
    return np.ascontiguousarray(out.reshape(2, NTB, 128, -1).transpose(0, 2, 1, 3))


def kernel(**inp):
    f = lambda k: np.ascontiguousarray(np.asarray(inp[k], dtype=np.float32))
    xp, xs = f("x_prompt"), f("x_sample")
    rep = lambda a: np.ascontiguousarray(np.broadcast_to(a[:, None, :], (a.shape[0], 128, a.shape[1])))
    cw, cbias = f("ffn_conv_w"), f("ffn_conv_b")
    cpar = np.concatenate([cw, cbias[:, None, :]], axis=1)
    convp = np.ascontiguousarray(cpar.reshape(4, 4, 2 * NFC, 128).transpose(0, 3, 2, 1))
    shared = {
        "diff_w_in": f("diff_w_in"), "diff_w_out": f("diff_w_out"), "win_w_in": f("win_w_in"),
        "win_w_out": f("win_w_out"), "ffn_w_up": f("ffn_w_up"), "ffn_w_down": f("ffn_w_down"),
        "lam_r": rep(f("diff_lam").reshape(2, 256)), "subg_r": rep(f("diff_subln_g")), "sink_r": rep(f("win_sink")),
        "lnm_g": rep(f("ln_mix_g")), "lnm_b": rep(f("ln_mix_b")), "lnf_g": rep(f("ln_ffn_g")), "lnf_b": rep(f("ln_ffn_b")),
        "convp": convp, "ident": np.eye(128).astype(ml_dtypes.bfloat16),
    }
    jj, ii = np.meshgrid(np.arange(128), np.arange(128), indexing="ij")
    mL = (ii <= jj).astype(np.float32)
    mU = (jj <= ii).astype(np.float32)
    shared["trimask"] = np.stack([np.tile(mL, (1, 4)), np.tile(mU, (1, 4))], 0).astype(ml_dtypes.bfloat16)
    in_maps = []
    for c in range(8):
        two = c < 4
        if two:
            xc = np.concatenate([xp[2 * c], xp[2 * c + 1]], 0)
            pos = np.concatenate([np.arange(2048), np.arange(2048)])
        else:
            xc = xs[c % 2]
            pos = np.arange(4096)
        mb = np.zeros((128, 2, 32), np.float32)
        if two:
            mb[:, 0, 16:] = NEG
            mb[:, 1, :16] = NEG
        fl = np.zeros((128, 2), np.float32)
        fl[:, 0] = 0.0 if two else 1.0
        fl[:, 1] = NEG if two else 0.0
        d = dict(shared)
        d.update({"xin": np.ascontiguousarray(xc), "ropd": _rope_rep(pos, 16, 8), "ropw": _rope_rep(pos, 32, 4), "maskb": mb, "flags": fl})
        in_maps.append(d)
    nc = build()
    res = run_bass_kernel_spmd(nc, in_maps, core_ids=list(range(8)))
    ys = [np.asarray(r["y"], dtype=np.float32) for r in res.results]
    y_prompt = np.stack([ys[c // 2][(c % 2) * 2048:(c % 2 + 1) * 2048] for c in range(8)], 0)
    y_sample = np.stack([ys[4], ys[5]], 0)
    return (y_prompt, y_sample)
```

```python
import math
from contextlib import ExitStack
import numpy as np
import ml_dtypes
import concourse.bass as bass
import concourse.mybir as mybir
from concourse.bass_utils import run_bass_kernel_spmd

F32 = mybir.dt.float32
BF16 = mybir.dt.bfloat16
AF = mybir.ActivationFunctionType
ALU = mybir.AluOpType
AX = mybir.AxisListType

T = 4096
D = 2048
NTB = 32
FF = 5504
NFC = 43
DEPTH = 4
ALPHA = (2 * DEPTH) ** 0.25
EPS = 1e-5
NEG = -30000.0
KD = 6


class Buf:
    def __init__(self, excl=False):
        self.w = None
        self.r = {}
        self.excl = excl


class Eng:
    def __init__(self, sem, sid):
        self.sem, self.sid, self.n, self.seen, self.q = sem, sid, 0, {}, []
        self.dsems, self.dcnt, self.k = [], [], 0

    def wait(self, dep):
        if dep is None:
            return
        sid, sem, val = dep
        if self.seen.get(sid, 0) >= val:
            return
        self.e.wait_ge(sem, val)
        self.seen[sid] = val


def _deps(E, reads, writes):
    for b in reads:
        E.wait(b.w)
        if b.excl:
            for sid, d in b.r.items():
                if sid != E.sid:
                    E.wait(d)
    for b in writes:
        if b.w is not None and b.w[0] != E.sid:
            E.wait(b.w)
        for sid, d in b.r.items():
            if sid != E.sid:
                E.wait(d)


def _mark(d, reads, writes):
    for b in reads:
        b.r[d[0]] = d
    for b in writes:
        b.w = d
        b.r = {}


def op(E, fn, reads=(), writes=()):
    _deps(E, reads, writes)
    E.n += 1
    fn().then_inc(E.sem, 1)
    _mark((E.sid, E.sem, E.n), reads, writes)


def dma(Q, fn, reads=(), writes=()):
    slot = Q.k % KD
    Q.k += 1
    sid = 100 + Q.sid * 10 + slot
    sem = Q.dsems[slot]
    Q.wait((sid, sem, Q.dcnt[slot] * 16))
    for b in reads:
        Q.wait(b.w)
    for b in writes:
        Q.wait(b.w)
        for d in b.r.values():
            Q.wait(d)
    fn().then_inc(sem, 16)
    Q.dcnt[slot] += 1
    _mark((sid, sem, Q.dcnt[slot] * 16), reads, writes)


def build():
    nc = bass.Bass("TRN2", target_bir_lowering=False)
    dt = lambda n, s, d, k="ExternalInput": nc.dram_tensor(n, s, d, kind=k).ap()
    xin = dt("xin", [T, D], F32)
    diff_w_in = dt("diff_w_in", [2, D, 6144], F32)
    diff_w_out = dt("diff_w_out", [2, D, D], F32)
    win_w_in = dt("win_w_in", [2, D, 3072], F32)
    win_w_out = dt("win_w_out", [2, D, D], F32)
    ffn_w_up = dt("ffn_w_up", [4, D, 2 * FF], F32)
    ffn_w_down = dt("ffn_w_down", [4, FF, D], F32)
    lam_r = dt("lam_r", [2, 128, 256], F32)
    subg_r = dt("subg_r", [2, 128, 128], F32)
    sink_r = dt("sink_r", [2, 128, 16], F32)
    lnm_g = dt("lnm_g", [4, 128, D], F32)
    lnm_b = dt("lnm_b", [4, 128, D], F32)
    lnf_g = dt("lnf_g", [4, 128, D], F32)
    lnf_b = dt("lnf_b", [4, 128, D], F32)
    convp = dt("convp", [4, 128, 2 * NFC, 4], F32)
    ropd = dt("ropd", [2, 128, NTB, 64], F32)
    ropw = dt("ropw", [2, 128, NTB, 64], F32)
    maskb = dt("maskb", [128, 2, 32], F32)
    flags = dt("flags", [128, 2], F32)
    ident_d = dt("ident", [128, 128], BF16)
    trimask = dt("trimask", [2, 128, 512], BF16)
    y = dt("y", [T, D], F32, "ExternalOutput")
    X = nc.dram_tensor("X", [T, D], F32).ap()
    X2 = nc.dram_tensor("X2", [T, D], F32).ap()
    XT = nc.dram_tensor("XT", [16, 128, T], BF16).ap()
    XT2 = nc.dram_tensor("XT2", [16, 128, T], BF16).ap()
    QT = nc.dram_tensor("QT", [16, 128, T], BF16).ap()
    KT = nc.dram_tensor("KT", [16, 128, T], BF16).ap()
    Vd = nc.dram_tensor("Vd", [T, D], BF16).ap()
    OT = nc.dram_tensor("OT", [16, 128, T], BF16).ap()
    WUPB = nc.dram_tensor("WUPB", [4, D, 2 * FF], BF16).ap()
    WDNB = nc.dram_tensor("WDNB", [4, FF, D], BF16).ap()

    top = ExitStack()
    with top:
        nsem = 5 + 2 * KD
        sems = [top.enter_context(nc.semaphore(f"s{i}")) for i in range(nsem)]
        PE, ACT, DVE, POOL, SP = [Eng(sems[i], i) for i in range(5)]
        ENGS = [PE, ACT, DVE, POOL, SP]
        for qi, Q in enumerate((POOL, SP)):
            Q.dsems = sems[5 + qi * KD: 5 + (qi + 1) * KD]
            Q.dcnt = [0] * KD
        tE, sE, vE, gE, yE = nc.tensor, nc.scalar, nc.vector, nc.gpsimd, nc.sync
        for E_, h_ in zip(ENGS, (tE, sE, vE, gE, yE)):
            E_.e = h_

        def barrier():
            for E in ENGS:
                for F in ENGS:
                    if F is not E and F.n > 0:
                        E.wait((F.sid, F.sem, F.n))
                for Q in (POOL, SP):
                    for s in range(KD):
                        if Q.dcnt[s]:
                            E.wait((100 + Q.sid * 10 + s, Q.dsems[s], Q.dcnt[s] * 16))

        PSALL = top.enter_context(nc.psum_tensor("psall", [128, 8 * 512], F32))

        class _Bank:
            def __init__(self, i):
                self.i = i

            def __getitem__(self, key):
                return PSALL[:, self.i * 512:(self.i + 1) * 512][key]
        PS = [_Bank(i) for i in range(8)]
        PB = [Buf(excl=True) for _ in range(8)]
        ident = top.enter_context(nc.sbuf_tensor("identt", [128, 128], BF16))
        flg = top.enter_context(nc.sbuf_tensor("flg", [128, 2], F32))
        cb = Buf()
        dma(SP, lambda: yE.dma_start(out=ident[:], in_=ident_d), writes=[cb])
        dma(SP, lambda: yE.dma_start(out=flg[:], in_=flags), writes=[cb])

        uid = [0]

        def sbt(es, name, shape, dtype):
            uid[0] += 1
            return es.enter_context(nc.sbuf_tensor(f"{name}_{uid[0]}", shape, dtype)), Buf()

        def transpose_store(xb, xbB, dst3, stg, stgB, tpi):
            for q4 in range(4):
                bk = tpi[q4 % 2]
                tp = PS[bk][:].bitcast(BF16)
                for j in range(4):
                    kc = q4 * 4 + j
                    op(PE, lambda tp=tp, j=j, kc=kc: tE.transpose(tp[:, j * 128:(j + 1) * 128], xb[:, kc * 128:(kc + 1) * 128], ident[:]),
                       reads=[xbB, cb], writes=[PB[bk]])
                op(DVE, lambda tp=tp, q4=q4: vE.tensor_copy(stg[:, q4 * 4:(q4 + 1) * 4, :], tp[:, 0:512].rearrange("p (a b) -> p a b", a=4)),
                   reads=[PB[bk]], writes=[stgB])
            dma(SP, lambda: yE.dma_start(out=dst3, in_=stg[:]), reads=[stgB])

        def xt_view(XTd):
            return XTd.rearrange("k p t -> p k t")

        def layer_norm(es_t, xr, xrB, Gt, Bt, gB, Xrows, XTd, tb, tpi, tiles, defer=None):
            st, stB, mv, mvB, sc, scB, xb, xbB, stg, stgB = tiles
            for c in range(4):
                op(DVE, lambda c=c: vE.bn_stats(st[:, c, :], xr[:, c * 512:(c + 1) * 512]), reads=[xrB], writes=[stB])
            op(DVE, lambda: vE.bn_aggr(mv[:], st[:].rearrange("p a b -> p (a b)")), reads=[stB], writes=[mvB])
            op(DVE, lambda: vE.tensor_scalar_add(sc[:, 0:1], mv[:, 1:2], EPS), reads=[mvB], writes=[scB])
            op(ACT, lambda: sE.sqrt(sc[:, 1:2], sc[:, 0:1]), reads=[scB], writes=[scB])
            op(DVE, lambda: vE.reciprocal(sc[:, 2:3], sc[:, 1:2]), reads=[scB], writes=[scB])
            op(DVE, lambda: vE.tensor_scalar(xr, xr, mv[:, 0:1], sc[:, 2:3], ALU.subtract, ALU.mult), reads=[xrB, mvB, scB], writes=[xrB])
            op(DVE, lambda: vE.tensor_tensor(xr, xr, Gt[:], ALU.mult), reads=[xrB, gB], writes=[xrB])
            op(DVE, lambda: vE.tensor_tensor(xr, xr, Bt[:], ALU.add), reads=[xrB, gB], writes=[xrB])
            dma(SP, lambda: yE.dma_start(out=Xrows, in_=xr), reads=[xrB])
            op(ACT, lambda: sE.copy(xb[:], xr), reads=[xrB], writes=[xbB])
            if defer is None:
                transpose_store(xb, xbB, xt_view(XTd)[:, :, tb * 128:(tb + 1) * 128], stg, stgB, tpi)
            else:
                defer.append(lambda: transpose_store(xb, xbB, xt_view(XTd)[:, :, tb * 128:(tb + 1) * 128], stg, stgB, tpi))

        def ln_tiles(es, tag):
            st, stB = sbt(es, "st" + tag, [128, 4, 6], F32)
            mv, mvB = sbt(es, "mv" + tag, [128, 2], F32)
            sc, scB = sbt(es, "sc" + tag, [128, 4], F32)
            xb, xbB = sbt(es, "xb" + tag, [128, D], BF16)
            stg, stgB = sbt(es, "stg" + tag, [128, 16, 128], BF16)
            return (st, stB, mv, mvB, sc, scB, xb, xbB, stg, stgB)

        with ExitStack() as es:
            xrs = [sbt(es, f"pxr{i}", [128, D], F32) for i in range(2)]
            xbs = [sbt(es, f"pxb{i}", [128, D], BF16) for i in range(2)]
            stgs = [sbt(es, f"pstg{i}", [128, 16, 128], BF16) for i in range(2)]
            for tb in range(NTB):
                xr, xrB = xrs[tb % 2]
                xb, xbB = xbs[tb % 2]
                stg, stgB = stgs[tb % 2]
                dma(SP, lambda xr=xr, tb=tb: yE.dma_start(out=xr[:], in_=xin[tb * 128:(tb + 1) * 128, :]), writes=[xrB])
                op(ACT, lambda xr=xr, xb=xb: sE.copy(xb[:], xr[:]), reads=[xrB], writes=[xbB])
                transpose_store(xb, xbB, xt_view(XT)[:, :, tb * 128:(tb + 1) * 128], stg, stgB, [6, 7])
            barrier()

        for L in range(DEPTH):
            j = L // 2
            is_diff = (L % 2 == 0)
            Xsrc = xin if L == 0 else X
            w_in = diff_w_in[j] if is_diff else win_w_in[j]
            ncols = 6144 if is_diff else 3072
            m, hh = (8, 8) if is_diff else (4, 16)
            dh = 512 // m
            w_v = w_in.rearrange("(k p) n -> p k n", p=128)
            with ExitStack() as es:
                xts = [sbt(es, f"qxt{i}", [128, 16, 1024], BF16) for i in range(2)]
                wts = [sbt(es, f"qwt{i}", [128, 16, 512], BF16) for i in range(2)]
                rop, ropB = sbt(es, "rop", [128, 2, NTB, 64], F32)
                t1, t1B = sbt(es, "rt1", [128, m, hh], F32)
                t2, t2B = sbt(es, "rt2", [128, m, hh], F32)
                qbs = [sbt(es, f"qb{i}", [128, 512], BF16) for i in range(2)]
                stgs = [sbt(es, f"qstg{i}", [128, 4, 1024], BF16) for i in range(2)]
                rsrc = ropd if is_diff else ropw
                dma(SP, lambda: yE.dma_start(out=rop[:], in_=rsrc.rearrange("c p t f -> p c t f")), writes=[ropB])
                cnt = 0
                wcnt = 0
                qpend = []

                def q_load(tg):
                    xt, xtB = xts[tg % 2]
                    dma(SP, lambda: yE.dma_start(out=xt[:], in_=xt_view(XT)[:, :, tg * 1024:(tg + 1) * 1024]), writes=[xtB])
                q_load(0)
                for tg in range(4):
                    xt, xtB = xts[tg % 2]
                    if tg + 1 < 4:
                        q_load(tg + 1)
                    for nt in range(ncols // 512):
                        wt, wtB = wts[wcnt % 2]
                        stg, stgB = stgs[wcnt % 2]
                        wcnt += 1
                        dma(POOL, lambda wt=wt, nt=nt: gE.dma_start(out=wt[:], in_=w_v[:, :, nt * 512:(nt + 1) * 512]), writes=[wtB])
                        if is_diff:
                            kind = "q" if nt < 4 else ("k" if nt < 8 else "v")
                            h0 = (nt % 4) * 4
                        else:
                            kind = "q" if nt < 4 else ("k" if nt == 4 else "v")
                            h0 = nt * 4 if nt < 4 else 0
                        for tb in range(8):
                            bk = cnt % 4
                            cnt += 1
                            ps = PS[bk]
                            tbg = tg * 8 + tb
                            for kc in range(16):
                                op(PE, lambda ps=ps, xt=xt, wt=wt, kc=kc, tb=tb: tE.matmul(ps[:], lhsT=xt[:, kc, tb * 128:(tb + 1) * 128], rhs=wt[:, kc, :], start=(kc == 0), stop=(kc == 15)),
                                   reads=[xtB, wtB], writes=[PB[bk]])
                            while qpend:
                                qpend.pop(0)()
                            qb, qbB = qbs[cnt % 2]
                            if kind == "v":
                                op(DVE, lambda qb=qb, ps=ps: vE.tensor_copy(qb[:], ps[:]), reads=[PB[bk]], writes=[qbB])
                                vcols = (nt - 8) * 512 if is_diff else 0
                                dma(SP, lambda qb=qb, tbg=tbg, vcols=vcols: yE.dma_start(out=Vd[tbg * 128:(tbg + 1) * 128, vcols:vcols + 512], in_=qb[:]), reads=[qbB])
                                continue
                            ps3 = ps[:].rearrange("p (m d) -> p m d", m=m)
                            qb3 = qb[:].rearrange("p (m d) -> p m d", m=m)
                            c3 = rop[:, 0, tbg, :].rearrange("p (m h) -> p m h", m=m)
                            s3 = rop[:, 1, tbg, :].rearrange("p (m h) -> p m h", m=m)
                            x1, x2 = ps3[:, :, 0:hh], ps3[:, :, hh:2 * hh]
                            R = [PB[bk], ropB]
                            op(DVE, lambda x1=x1, c3=c3: vE.tensor_tensor(t1[:], x1, c3, ALU.mult), reads=R, writes=[t1B])
                            op(DVE, lambda x2=x2, s3=s3: vE.tensor_tensor(t2[:], x2, s3, ALU.mult), reads=R, writes=[t2B])
                            op(DVE, lambda qb3=qb3: vE.tensor_tensor(qb3[:, :, 0:hh], t1[:], t2[:], ALU.subtract), reads=[t1B, t2B], writes=[qbB])
                            op(DVE, lambda x2=x2, c3=c3: vE.tensor_tensor(t1[:], x2, c3, ALU.mult), reads=R, writes=[t1B])
                            op(DVE, lambda x1=x1, s3=s3: vE.tensor_tensor(t2[:], x1, s3, ALU.mult), reads=R, writes=[t2B])
                            op(DVE, lambda qb3=qb3: vE.tensor_tensor(qb3[:, :, hh:2 * hh], t1[:], t2[:], ALU.add), reads=[t1B, t2B], writes=[qbB])
                            op(DVE, lambda qb3=qb3, ps3=ps3: vE.tensor_copy(qb3[:, :, 2 * hh:], ps3[:, :, 2 * hh:]), reads=[PB[bk]], writes=[qbB])

                            def trp(qb=qb, qbB=qbB, stg=stg, stgB=stgB, tb=tb, tbk=4 + (cnt % 2), last=(tb == 7), kind=kind, h0=h0, tg=tg):
                                tp = PS[tbk][:].bitcast(BF16)
                                for jj in range(4):
                                    op(PE, lambda jj=jj: tE.transpose(tp[:, jj * 128:(jj + 1) * 128], qb[:, jj * 128:(jj + 1) * 128], ident[:]),
                                       reads=[qbB, cb], writes=[PB[tbk]])
                                op(DVE, lambda: vE.tensor_copy(stg[:, :, tb * 128:(tb + 1) * 128], tp[:, 0:512].rearrange("p (a b) -> p a b", a=4)),
                                   reads=[PB[tbk]], writes=[stgB])
                                if last:
                                    dstT = QT if kind == "q" else KT
                                    dma(SP, lambda: yE.dma_start(out=dstT[h0:h0 + 4].rearrange("h p t -> p h t")[:, :, tg * 1024:(tg + 1) * 1024], in_=stg[:]), reads=[stgB])
                            qpend.append(trp)
                while qpend:
                    qpend.pop(0)()
                barrier()
            if L == 0:
                for L2 in range(DEPTH):
                    srcu = ffn_w_up[L2].rearrange("(k p) n -> p k n", p=128)
                    dstu = WUPB[L2].rearrange("(k p) n -> p k n", p=128)
                    for cbk in range(16):
                        dma(POOL, lambda srcu=srcu, dstu=dstu, cbk=cbk: gE.dma_start(out=dstu[:, :, cbk * 688:(cbk + 1) * 688], in_=srcu[:, :, cbk * 688:(cbk + 1) * 688]))
                    srcd = ffn_w_down[L2].rearrange("(i p) n -> p i n", p=128)
                    dstd = WDNB[L2].rearrange("(i p) n -> p i n", p=128)
                    for i0_ in range(0, NFC, 11):
                        i1_ = min(NFC, i0_ + 11)
                        dma(POOL, lambda srcd=srcd, dstd=dstd, i0_=i0_, i1_=i1_: gE.dma_start(out=dstd[:, i0_:i1_, :], in_=srcd[:, i0_:i1_, :]))
            if is_diff:
                lam_init = 0.8 - 0.6 * math.exp(-0.3 * L)
                with ExitStack() as es:
                    qTs = [sbt(es, f"aq{i}", [128, T], BF16) for i in range(2)]
                    kTs = [sbt(es, f"ak{i}", [128, T], BF16) for i in range(2)]
                    vas = [sbt(es, f"av{i}", [128, 32, 130], BF16) for i in range(2)]
                    Pb = [sbt(es, f"ap{i}", [128, 512], BF16) for i in range(3)]
                    lm, lmB = sbt(es, "lm", [128, 256], F32)
                    gs, gsB = sbt(es, "gs", [128, 128], F32)
                    mk, mkB = sbt(es, "mk", [128, 2, 32], F32)
                    sm, smB = sbt(es, "sm", [128, 16], F32)
                    of, ofB = sbt(es, "of", [128, 128], F32)
                    sq, sqB = sbt(es, "sq", [128, 128], F32)
                    obs = [sbt(es, f"ob{i}", [128, 128], BF16) for i in range(4)]
                    ostgs = [sbt(es, f"aost{i}", [128, 512], BF16) for i in range(2)]
                    dma(SP, lambda: yE.dma_start(out=lm[:], in_=lam_r[j]), writes=[lmB])
                    dma(SP, lambda: yE.dma_start(out=gs[:], in_=subg_r[j]), writes=[gsB])
                    dma(SP, lambda: yE.dma_start(out=mk[:], in_=maskb), writes=[mkB])
                    for i in range(2):
                        op(DVE, lambda i=i: vE.memset(vas[i][0][:], 1.0), writes=[vas[i][1]])
                    op(DVE, lambda: vE.tensor_scalar_mul(gs[:], gs[:], 1.0 - lam_init), reads=[gsB], writes=[gsB])
                    op(DVE, lambda: vE.tensor_tensor(sq[:, 0:64], lm[:, 0:64], lm[:, 64:128], ALU.mult), reads=[lmB], writes=[sqB])
                    op(DVE, lambda: vE.tensor_tensor(sq[:, 64:128], lm[:, 128:192], lm[:, 192:256], ALU.mult), reads=[lmB], writes=[sqB])
                    op(DVE, lambda: vE.reduce_sum(sm[:, 0:1], sq[:, 0:64], AX.X), reads=[sqB], writes=[smB])
                    op(DVE, lambda: vE.reduce_sum(sm[:, 1:2], sq[:, 64:128], AX.X), reads=[sqB], writes=[smB])
                    op(ACT, lambda: sE.activation(sm[:, 2:4], sm[:, 0:2], AF.Exp), reads=[smB], writes=[smB])
                    op(DVE, lambda: vE.tensor_tensor(sm[:, 4:5], sm[:, 3:4], sm[:, 2:3], ALU.subtract), reads=[smB], writes=[smB])
                    op(DVE, lambda: vE.tensor_scalar_add(sm[:, 5:6], sm[:, 4:5], -lam_init), reads=[smB], writes=[smB])
                    Vv = Vd.rearrange("(kb p) e -> p kb e", p=128)
                    P2 = [sbt(es, f"ap2{i}", [128, 1024], BF16) for i in range(3)]
                    accS8, accSB = sbt(es, "accS8", [128, 8, 130], F32)
                    rz, rzB = sbt(es, "rz", [128, 28], F32)
                    of4, ofB = sbt(es, "of4", [128, 4, 128], F32)
                    sq4, sqB4 = sbt(es, "sq4", [128, 4, 128], F32)

                    def a_load(h):
                        qT, qTB = qTs[h % 2]
                        kT, kTB = kTs[h % 2]
                        va, vaB = vas[h % 2]
                        dma(SP, lambda: yE.dma_start(out=qT[:], in_=QT[h]), writes=[qTB])
                        dma(SP, lambda: yE.dma_start(out=kT[:], in_=KT[h]), writes=[kTB])
                        dma(SP, lambda: yE.dma_start(out=va[:, :, 0:128], in_=Vv[:, :, h * 128:(h + 1) * 128]), writes=[vaB])

                    steps = [(h, qg, kb) for h in range(16) for qg in range(8) for kb in range(32)]
                    accs = {}
                    for qb_ in range(4):
                        for c in range(2):
                            a = qb_ * 2 + c
                            accs[(qb_, c)] = (4 + a // 3, (a % 3) * 130)

                    def emit_qk(idx):
                        h, qg, kb = steps[idx]
                        qT, qTB = qTs[h % 2]
                        kT, kTB = kTs[h % 2]
                        sp = idx % 2
                        for c in range(2):
                            bk = 2 * sp + c
                            op(PE, lambda c=c, bk=bk: tE.matmul(PS[bk][:], lhsT=kT[64 * c:64 * c + 64, kb * 128:(kb + 1) * 128], rhs=qT[64 * c:64 * c + 64, qg * 512:(qg + 1) * 512], start=True, stop=True),
                               reads=[kTB, qTB], writes=[PB[bk]])

                    def emit_exp(idx):
                        h, qg, kb = steps[idx]
                        sp = idx % 2
                        P, PBf = P2[idx % 3]
                        seg = qg // 4
                        op(ACT, lambda: sE.activation(P[:], PSALL[:, sp * 1024:(sp + 1) * 1024], AF.Exp, bias=mk[:, seg, kb:kb + 1], scale=0.125),
                           reads=[PB[2 * sp], PB[2 * sp + 1], mkB], writes=[PBf])

                    def emit_pv(idx):
                        h, qg, kb = steps[idx]
                        va, vaB = vas[h % 2]
                        P, PBf = P2[idx % 3]
                        for c in range(2):
                            for qb_ in range(4):
                                bk, off = accs[(qb_, c)]
                                op(PE, lambda bk=bk, off=off, qb_=qb_, c=c: tE.matmul(PS[bk][:, off:off + 129], lhsT=P[:, c * 512 + qb_ * 128:c * 512 + (qb_ + 1) * 128], rhs=va[:, kb, 0:129], start=(kb == 0 and c == 0 and qb_ in (0, 2, 3)), stop=(kb == 31), skip_group_check=True),
                                   reads=[PBf, vaB], writes=[PB[bk]])

                    pending = []
                    pending_act = []
                    rz2, rz2B = sbt(es, "rz2", [128, 8], F32)

                    def emit_epilogue(h, qg, ocnt):
                        ostg, ostgB = ostgs[ocnt % 2]
                        for b3, na in ((0, 3), (1, 3), (2, 2)):
                            op(DVE, lambda b3=b3, na=na: vE.tensor_copy(accS8[:, 3 * b3:3 * b3 + na, :], PS[4 + b3][:, 0:130 * na].rearrange("p (a c) -> p a c", a=na)),
                               reads=[PB[4 + b3]], writes=[accSB])
                        op(DVE, lambda: vE.reciprocal(rz[:, 0:8].rearrange("p (a o) -> p a o", o=1), accS8[:, :, 128:129]), reads=[accSB], writes=[rzB])
                        op(DVE, lambda: vE.tensor_scalar_mul(rz[:, 8:12], rz[:, 1:8:2], sm[:, 5:6]), reads=[rzB, smB], writes=[rzB])
                        for qb_ in range(4):
                            op(DVE, lambda qb_=qb_: vE.tensor_scalar_mul(of4[:, qb_, :], accS8[:, 2 * qb_, 0:128], rz[:, 2 * qb_:2 * qb_ + 1]), reads=[accSB, rzB], writes=[ofB])
                            op(DVE, lambda qb_=qb_: vE.scalar_tensor_tensor(of4[:, qb_, :], accS8[:, 2 * qb_ + 1, 0:128], rz[:, 8 + qb_:9 + qb_], of4[:, qb_, :], ALU.mult, ALU.add), reads=[accSB, rzB, ofB], writes=[ofB])
                        op(DVE, lambda: vE.tensor_tensor(sq4[:], of4[:], of4[:], ALU.mult), reads=[ofB], writes=[sqB])
                        op(DVE, lambda: vE.reduce_sum(rz[:, 12:16], sq4[:], AX.X), reads=[sqB], writes=[rzB])
                        op(DVE, lambda: vE.tensor_scalar(rz[:, 16:20], rz[:, 12:16], 1.0 / 128, EPS, ALU.mult, ALU.add), reads=[rzB], writes=[rzB])
                        def part1c():
                            op(ACT, lambda: sE.activation(rz2[:, 0:4], rz[:, 16:20], AF.Ln), reads=[rzB], writes=[rz2B])
                            op(ACT, lambda: sE.activation(rz2[:, 4:8], rz2[:, 0:4], AF.Exp, scale=-0.5), reads=[rz2B], writes=[rz2B])
                            for qb_ in range(4):
                                ob, obB = obs[qb_]
                                op(DVE, lambda qb_=qb_, ob=ob: vE.scalar_tensor_tensor(ob[:], of4[:, qb_, :], rz2[:, 4 + qb_:5 + qb_], gs[:], ALU.mult, ALU.mult), reads=[ofB, rz2B, gsB], writes=[obB])
                        pending_act.append(part1c)

                        def part2():
                            tp = PS[7][:].bitcast(BF16)
                            for qb_ in range(4):
                                ob, obB = obs[qb_]
                                op(PE, lambda qb_=qb_, ob=ob: tE.transpose(tp[:, qb_ * 128:(qb_ + 1) * 128], ob[:], ident[:]), reads=[obB, cb], writes=[PB[7]])
                            op(DVE, lambda: vE.tensor_copy(ostg[:], tp[:, 0:512]), reads=[PB[7]], writes=[ostgB])
                            dma(SP, lambda: yE.dma_start(out=OT[h][:, qg * 512:(qg + 1) * 512], in_=ostg[:]), reads=[ostgB])
                        pending.append(part2)

                    a_load(0)
                    emit_qk(0)
                    emit_qk(1)
                    ocnt = 0
                    for idx, (h, qg, kb) in enumerate(steps):
                        if qg == 0 and kb == 0 and h + 1 < 16:
                            a_load(h + 1)
                        emit_exp(idx)
                        emit_pv(idx)
                        if idx + 2 < len(steps):
                            emit_qk(idx + 2)
                        if kb == 6 and pending_act:
                            pending_act.pop(0)()
                        if kb == 14 and pending:
                            pending.pop(0)()
                        if kb == 31:
                            emit_epilogue(h, qg, ocnt)
                            ocnt += 1
                    while pending_act:
                        pending_act.pop(0)()
                    while pending:
                        pending.pop(0)()
                    barrier()
            else:
                with ExitStack() as es:
                    q4s = [sbt(es, f"wq{i}", [128, 4, T], BF16) for i in range(2)]
                    kTs = [sbt(es, f"wk{i}", [128, T], BF16) for i in range(2)]
                    vas = [sbt(es, f"wv{i}", [128, 32, 130], BF16) for i in range(2)]
                    Pb = [sbt(es, f"wp{i}", [128, 512], BF16) for i in range(3)]
                    tm, tmB = sbt(es, "tm", [128, 2, 512], BF16)
                    sk, skB = sbt(es, "sk", [128, 16], F32)
                    sm, smB = sbt(es, "wsm", [128, 8], F32)
                    wacc, waccB = sbt(es, "wacc", [128, 4, 130], F32)
                    wz, wzB = sbt(es, "wz", [128, 8], F32)
                    obs4 = [sbt(es, f"wob4{i}", [128, 128], BF16) for i in range(4)]
                    obs = [sbt(es, f"wob{i}", [128, 128], BF16) for i in range(2)]
                    ostgs = [sbt(es, f"wost{i}", [128, 4, 128], BF16) for i in range(2)]
                    dma(SP, lambda: yE.dma_start(out=tm[:], in_=trimask.rearrange("c p f -> p c f")), writes=[tmB])
                    dma(SP, lambda: yE.dma_start(out=sk[:], in_=sink_r[j]), writes=[skB])
                    op(ACT, lambda: sE.activation(sk[:], sk[:], AF.Exp), reads=[skB], writes=[skB])
                    for i in range(2):
                        op(POOL, lambda i=i: gE.memset(vas[i][0][:], 1.0), writes=[vas[i][1]])
                    Vv = Vd.rearrange("(kb p) e -> p kb e", p=128)
                    accb = [(4, 0), (4, 130), (4, 260), (5, 0)]
                    wsteps = []
                    for g in range(4):
                        for n in range(32):
                            kbs = [kb for kb in (n - 1, n, n + 1) if 0 <= kb < 32]
                            for i_, kb in enumerate(kbs):
                                wsteps.append((g, n, kb, i_, len(kbs)))

                    def w_load(g):
                        q4, q4B = q4s[g % 2]
                        kT, kTB = kTs[g % 2]
                        va, vaB = vas[g % 2]
                        dma(SP, lambda: yE.dma_start(out=q4[:], in_=QT[4 * g:4 * g + 4].rearrange("h p t -> p h t")), writes=[q4B])
                        dma(SP, lambda: yE.dma_start(out=kT[:], in_=KT[g]), writes=[kTB])
                        dma(SP, lambda: yE.dma_start(out=va[:, :, 0:128], in_=Vv[:, :, g * 128:(g + 1) * 128]), writes=[vaB])

                    def w_qk(idx):
                        g, n, kb, i_, nk = wsteps[idx]
                        q4, q4B = q4s[g % 2]
                        kT, kTB = kTs[g % 2]
                        sb_ = idx % 4
                        op(PE, lambda: tE.matmul(PS[sb_][:], lhsT=kT[:, kb * 128:(kb + 1) * 128], rhs=q4[:, :, n * 128:(n + 1) * 128], start=True, stop=True),
                           reads=[kTB, q4B], writes=[PB[sb_]])

                    def w_exp(idx):
                        g, n, kb, i_, nk = wsteps[idx]
                        sb_ = idx % 4
                        P, PBf = Pb[idx % 3]
                        if (n, kb) in ((15, 16), (16, 15)):
                            op(ACT, lambda: sE.activation(P[:], PS[sb_][:], AF.Exp, bias=flg[:, 1:2], scale=128 ** -0.5), reads=[PB[sb_], cb], writes=[PBf])
                        else:
                            op(ACT, lambda: sE.activation(P[:], PS[sb_][:], AF.Exp, scale=128 ** -0.5), reads=[PB[sb_]], writes=[PBf])
                        if kb != n:
                            mi = 0 if kb < n else 1
                            op(POOL, lambda: gE.tensor_tensor(P[:], P[:], tm[:, mi, :], ALU.mult), reads=[PBf, tmB], writes=[PBf])

                    def w_pv(idx):
                        g, n, kb, i_, nk = wsteps[idx]
                        va, vaB = vas[g % 2]
                        P, PBf = Pb[idx % 3]
                        for r in range(4):
                            bk, off = accb[r]
                            op(PE, lambda bk=bk, off=off, r=r: tE.matmul(PS[bk][:, off:off + 129], lhsT=P[:, r * 128:(r + 1) * 128], rhs=va[:, kb, 0:129], start=(i_ == 0 and r in (0, 3)), stop=(i_ == nk - 1), skip_group_check=True),
                               reads=[PBf, vaB], writes=[PB[bk]])

                    wpend = []

                    def w_epi(g, n, ocnt):
                        ostg, ostgB = ostgs[ocnt % 2]
                        op(DVE, lambda: vE.tensor_copy(wacc[:, 0:3, :], PS[4][:, 0:390].rearrange("p (a c) -> p a c", a=3)), reads=[PB[4]], writes=[waccB])
                        op(DVE, lambda: vE.tensor_copy(wacc[:, 3:4, :], PS[5][:, 0:130].rearrange("p (a c) -> p a c", a=1)), reads=[PB[5]], writes=[waccB])
                        op(DVE, lambda: vE.tensor_tensor(wz[:, 0:4].rearrange("p (a o) -> p a o", o=1), wacc[:, :, 128:129], sk[:, 4 * g:4 * g + 4].rearrange("p (a o) -> p a o", o=1), ALU.add), reads=[waccB, skB], writes=[wzB])
                        op(DVE, lambda: vE.reciprocal(wz[:, 4:8], wz[:, 0:4]), reads=[wzB], writes=[wzB])
                        for r in range(4):
                            ob, obB = obs4[r]
                            op(DVE, lambda r=r, ob=ob: vE.tensor_scalar_mul(ob[:], wacc[:, r, 0:128], wz[:, 4 + r:5 + r]), reads=[waccB, wzB], writes=[obB])

                        def part2():
                            tp = PS[7][:].bitcast(BF16)
                            for r in range(4):
                                ob, obB = obs4[r]
                                op(PE, lambda r=r, ob=ob: tE.transpose(tp[:, r * 128:(r + 1) * 128], ob[:], ident[:]), reads=[obB, cb], writes=[PB[7]])
                            op(DVE, lambda: vE.tensor_copy(ostg[:], tp[:, 0:512].rearrange("p (a b) -> p a b", a=4)), reads=[PB[7]], writes=[ostgB])
                            dma(SP, lambda: yE.dma_start(out=OT[4 * g:4 * g + 4].rearrange("h p t -> p h t")[:, :, n * 128:(n + 1) * 128], in_=ostg[:]), reads=[ostgB])
                        wpend.append(part2)

                    w_load(0)
                    w_qk(0)
                    w_qk(1)
                    ocnt = 0
                    for idx, (g, n, kb, i_, nk) in enumerate(wsteps):
                        if n == 0 and i_ == 0 and g + 1 < 4:
                            w_load(g + 1)
                        w_exp(idx)
                        w_pv(idx)
                        if idx + 2 < len(wsteps):
                            w_qk(idx + 2)
                        if i_ == 1 and wpend:
                            wpend.pop(0)()
                        if i_ == nk - 1:
                            w_epi(g, n, ocnt)
                            ocnt += 1
                    while wpend:
                        wpend.pop(0)()
                    barrier()
            w_out = (diff_w_out if is_diff else win_w_out)[j].rearrange("(k p) n -> p k n", p=128)
            with ExitStack() as es:
                wo, woB = sbt(es, "wo", [128, 16, D], BF16)
                Gt, gB = sbt(es, "lng", [128, D], F32)
                Bt, _ = sbt(es, "lnb", [128, D], F32)
                ots = [sbt(es, f"oot{i}", [128, 16, 128], BF16) for i in range(2)]
                xrs = [sbt(es, f"oxr{i}", [128, D], F32) for i in range(2)]
                lts = [ln_tiles(es, f"o{i}") for i in range(2)]
                for q in range(4):
                    dma(POOL, lambda q=q: gE.dma_start(out=wo[:, 4 * q:4 * q + 4, :], in_=w_out[:, 4 * q:4 * q + 4, :]), writes=[woB])
                dma(SP, lambda: yE.dma_start(out=Gt[:], in_=lnm_g[L]), writes=[gB])
                dma(SP, lambda: yE.dma_start(out=Bt[:], in_=lnm_b[L]), writes=[gB])
                cnt = 0
                opend = []

                def op_load(tb):
                    ot, otB = ots[tb % 2]
                    xr, xrB = xrs[tb % 2]
                    dma(SP, lambda: yE.dma_start(out=ot[:], in_=OT.rearrange("h p t -> p h t")[:, :, tb * 128:(tb + 1) * 128]), writes=[otB])
                    dma(SP, lambda: yE.dma_start(out=xr[:], in_=Xsrc[tb * 128:(tb + 1) * 128, :]), writes=[xrB])
                op_load(0)
                for tb in range(NTB):
                    ot, otB = ots[tb % 2]
                    xr, xrB = xrs[tb % 2]
                    if tb + 1 < NTB:
                        op_load(tb + 1)
                    for nt in range(4):
                        bk = cnt % 6
                        cnt += 1
                        ps = PS[bk]
                        for h in range(16):
                            op(PE, lambda ps=ps, ot=ot, h=h, nt=nt: tE.matmul(ps[:], lhsT=ot[:, h, :], rhs=wo[:, h, nt * 512:(nt + 1) * 512], start=(h == 0), stop=(h == 15)),
                               reads=[otB, woB], writes=[PB[bk]])
                        op(DVE, lambda xr=xr, ps=ps, nt=nt: vE.scalar_tensor_tensor(xr[:, nt * 512:(nt + 1) * 512], xr[:, nt * 512:(nt + 1) * 512], ALPHA, ps[:], ALU.mult, ALU.add),
                           reads=[PB[bk], xrB], writes=[xrB])
                    while opend:
                        opend.pop(0)()
                    layer_norm(es, xr[:], xrB, Gt, Bt, gB, X2[tb * 128:(tb + 1) * 128, :], XT2, tb, [6, 7], lts[tb % 2], defer=opend)
                while opend:
                    opend.pop(0)()
                barrier()
            wup = WUPB[L].rearrange("(k p) n -> p k n", p=128)
            wdn = WDNB[L].rearrange("(i p) n -> p i n", p=128)
            Xdst = y if L == DEPTH - 1 else X
            with ExitStack() as es:
                xt, xtB = sbt(es, "fxt", [128, 16, 514], BF16)
                xr4, xr4B = sbt(es, "fxr", [128, 4, D], F32)
                wgs = [sbt(es, f"fwg{i}", [128, 16, 512], BF16) for i in range(2)]
                wus = [sbt(es, f"fwu{i}", [128, 16, 512], BF16) for i in range(2)]
                tgs = [sbt(es, f"ftg{i}", [128, 512], F32) for i in range(2)]
                tus = [sbt(es, f"ftu{i}", [128, 512], F32) for i in range(2)]
                sgs = [sbt(es, f"fsg{i}", [128, 512], F32) for i in range(2)]
                aT, aTB = sbt(es, "faT", [128, 11, 512], BF16)
                wds = [sbt(es, f"fwd{i}", [128, 11, 512], BF16) for i in range(2)]
                cp, cpB = sbt(es, "fcp", [128, 2 * NFC, 4], F32)
                Gt, gB = sbt(es, "flng", [128, D], F32)
                Bt, _ = sbt(es, "flnb", [128, D], F32)
                lts = [ln_tiles(es, f"f{i}") for i in range(2)]
                dma(SP, lambda: yE.dma_start(out=cp[:], in_=convp[L]), writes=[cpB])
                dma(SP, lambda: yE.dma_start(out=Gt[:], in_=lnf_g[L]), writes=[gB])
                dma(SP, lambda: yE.dma_start(out=Bt[:], in_=lnf_b[L]), writes=[gB])
                XT2v = xt_view(XT2)

                def load_xt(tg):
                    t0 = tg * 512
                    dma(SP, lambda: yE.dma_start(out=xt[:, :, 1:513], in_=XT2v[:, :, t0:t0 + 512]), writes=[xtB])
                    with nc.allow_non_contiguous_dma(reason="halo column"):
                        if tg > 0:
                            dma(SP, lambda: yE.dma_start(out=xt[:, :, 0:1], in_=XT2v[:, :, t0 - 1:t0]), writes=[xtB])
                        else:
                            op(DVE, lambda: vE.memset(xt[:, :, 0:1], 0.0), writes=[xtB])
                        if tg < 7:
                            dma(SP, lambda: yE.dma_start(out=xt[:, :, 513:514], in_=XT2v[:, :, t0 + 512:t0 + 513]), writes=[xtB])
                        else:
                            op(DVE, lambda: vE.memset(xt[:, :, 513:514], 0.0), writes=[xtB])
                    if tg == 4:
                        op(DVE, lambda: vE.tensor_scalar_mul(xt[:, :, 0:1], xt[:, :, 0:1], flg[:, 0:1]), reads=[xtB, cb], writes=[xtB])
                    if tg == 3:
                        op(DVE, lambda: vE.tensor_scalar_mul(xt[:, :, 513:514], xt[:, :, 513:514], flg[:, 0:1]), reads=[xtB, cb], writes=[xtB])

                cnts = {"c": 0, "w": 0, "d": 0, "y": 0}
                aTs = [(aT, aTB), sbt(es, "faT2", [128, 11, 512], BF16)]

                def up_tile(qc, qf, ca, ncw):
                    aT_, aTB_ = aTs[qc % 2]
                    c0 = qf * 11
                    wg, wgB = wgs[cnts["w"] % 2]
                    wu, wuB = wus[cnts["w"] % 2]
                    cnts["w"] += 1
                    dma(POOL, lambda: gE.dma_start(out=wg[:, :, 0:ncw * 128], in_=wup[:, :, ca * 128:(ca + ncw) * 128]), writes=[wgB])
                    dma(POOL, lambda: gE.dma_start(out=wu[:, :, 0:ncw * 128], in_=wup[:, :, FF + ca * 128:FF + (ca + ncw) * 128]), writes=[wuB])
                    for ii in range(ncw):
                        i = ca + ii
                        ccnt = cnts["c"]
                        cnts["c"] += 1
                        tg_, tgB = tgs[ccnt % 2]
                        tu_, tuB = tus[ccnt % 2]
                        sg_, sgB = sgs[ccnt % 2]
                        gb, ub, hb = ccnt % 2, 2 + ccnt % 2, 4 + ccnt % 2
                        for (w_, wB_, bnk, hoff) in ((wg, wgB, gb, 0), (wu, wuB, ub, 2)):
                            for kc in range(16):
                                op(PE, lambda w_=w_, bnk=bnk, kc=kc, ii=ii: tE.matmul(PS[bnk][:], lhsT=w_[:, kc, ii * 128:(ii + 1) * 128], rhs=xt[:, kc, 1:513], start=(kc == 0), stop=(kc == 15)),
                                   reads=[wB_, xtB], writes=[PB[bnk]])
                            for kc in range(16):
                                op(PE, lambda w_=w_, hb=hb, hoff=hoff, kc=kc, ii=ii: tE.matmul(PS[hb][:, hoff:hoff + 2], lhsT=w_[:, kc, ii * 128:(ii + 1) * 128], rhs=xt[:, kc, 0:514:513], start=(kc == 0), stop=(kc == 15)),
                                   reads=[wB_, xtB], writes=[PB[hb]])
                        for (tt, ttB, bnk, hoff, ci) in ((tg_, tgB, gb, 0, i), (tu_, tuB, ub, 2, NFC + i)):
                            Gp, Hp = PS[bnk], PS[hb]
                            w0, w1, w2, bb = cp[:, ci, 0:1], cp[:, ci, 1:2], cp[:, ci, 2:3], cp[:, ci, 3:4]
                            op(DVE, lambda tt=tt, Gp=Gp, w1=w1, bb=bb: vE.tensor_scalar(tt[:], Gp[:], w1, bb, ALU.mult, ALU.add), reads=[PB[bnk], cpB], writes=[ttB])
                            op(DVE, lambda tt=tt, Gp=Gp, w0=w0: vE.scalar_tensor_tensor(tt[:, 1:512], Gp[:, 0:511], w0, tt[:, 1:512], ALU.mult, ALU.add), reads=[PB[bnk], cpB, ttB], writes=[ttB])
                            op(DVE, lambda tt=tt, Hp=Hp, w0=w0, hoff=hoff: vE.scalar_tensor_tensor(tt[:, 0:1], Hp[:, hoff:hoff + 1], w0, tt[:, 0:1], ALU.mult, ALU.add), reads=[PB[hb], cpB, ttB], writes=[ttB])
                            op(DVE, lambda tt=tt, Gp=Gp, w2=w2: vE.scalar_tensor_tensor(tt[:, 0:511], Gp[:, 1:512], w2, tt[:, 0:511], ALU.mult, ALU.add), reads=[PB[bnk], cpB, ttB], writes=[ttB])
                            op(DVE, lambda tt=tt, Hp=Hp, w2=w2, hoff=hoff: vE.scalar_tensor_tensor(tt[:, 511:512], Hp[:, hoff + 1:hoff + 2], w2, tt[:, 511:512], ALU.mult, ALU.add), reads=[PB[hb], cpB, ttB], writes=[ttB])
                        op(ACT, lambda sg_=sg_, tg_=tg_: sE.activation(sg_[:], tg_[:], AF.Silu), reads=[tgB], writes=[sgB])
                        op(DVE, lambda sg_=sg_, tu_=tu_, i=i: vE.tensor_tensor(aT_[:, i - c0, :], sg_[:], tu_[:], ALU.mult), reads=[sgB, tuB], writes=[aTB_])

                def down(qc, qf):
                    aT_, aTB_ = aTs[qc % 2]
                    c0 = qf * 11
                    nch = min(NFC, c0 + 11) - c0
                    for nt in range(4):
                        wd, wdB = wds[cnts["d"] % 2]
                        cnts["d"] += 1
                        dma(POOL, lambda wd=wd, nt=nt: gE.dma_start(out=wd[:, 0:nch, :], in_=wdn[:, c0:c0 + nch, nt * 512:(nt + 1) * 512]), writes=[wdB])
                        for tb in range(4):
                            bk = 6 + cnts["y"] % 2
                            cnts["y"] += 1
                            for jj in range(nch):
                                op(PE, lambda bk=bk, jj=jj, tb=tb, wd=wd: tE.matmul(PS[bk][:], lhsT=aT_[:, jj, tb * 128:(tb + 1) * 128], rhs=wd[:, jj, :], start=(jj == 0), stop=(jj == nch - 1)),
                                   reads=[aTB_, wdB], writes=[PB[bk]])
                            dst = xr4[:, tb, nt * 512:(nt + 1) * 512]
                            if qf == 0:
                                op(DVE, lambda dst=dst, bk=bk: vE.scalar_tensor_tensor(dst, dst, ALPHA, PS[bk][:], ALU.mult, ALU.add), reads=[PB[bk], xr4B], writes=[xr4B])
                            else:
                                op(DVE, lambda dst=dst, bk=bk: vE.tensor_tensor(dst, dst, PS[bk][:], ALU.add), reads=[PB[bk], xr4B], writes=[xr4B])

                def qtiles(qf):
                    c0 = qf * 11
                    c1 = min(NFC, c0 + 11)
                    out = []
                    ca = c0
                    while ca < c1:
                        ncw = min(4, c1 - ca)
                        out.append((ca, ncw))
                        ca += ncw
                    return out

                load_xt(0)
                dma(SP, lambda: yE.dma_start(out=xr4[:], in_=X2[0:512, :].rearrange("(a p) d -> p a d", p=128)), writes=[xr4B])
                qc = 0
                up_tile(qc, 0, *qtiles(0)[0])
                for tg in range(8):
                    for qf in range(4):
                        for (ca, ncw) in qtiles(qf)[1:]:
                            up_tile(qc, qf, ca, ncw)
                        if qf < 3:
                            up_tile(qc + 1, qf + 1, *qtiles(qf + 1)[0])
                            down(qc, qf)
                        else:
                            if tg < 7:
                                load_xt(tg + 1)
                            down(qc, qf)
                            if tg < 7:
                                up_tile(qc + 1, 0, *qtiles(0)[0])
                        qc += 1
                    fpend = []
                    for tb in range(4):
                        tbg = tg * 4 + tb
                        layer_norm(es, xr4[:, tb, :], xr4B, Gt, Bt, gB, Xdst[tbg * 128:(tbg + 1) * 128, :], XT, tbg, [6, 7], lts[tb % 2], defer=fpend)
                        if len(fpend) > 1:
                            fpend.pop(0)()
                    while fpend:
                        fpend.pop(0)()
                    if tg < 7:
                        t0n = (tg + 1) * 512
                        dma(SP, lambda t0n=t0n: yE.dma_start(out=xr4[:], in_=X2[t0n:t0n + 512, :].rearrange("(a p) d -> p a d", p=128)), writes=[xr4B])
                barrier()
        barrier()

    return nc


def _rope_rep(pos, rot, m):
    inv = (np.float32(500000.0) ** (-(np.arange(0, rot, 2, dtype=np.float32)) / np.float32(rot))).astype(np.float32)
    ang = (pos[:, None].astype(np.float32) * inv[None, :]).astype(np.float32)
    c, s = np.cos(ang).astype(np.float32), np.sin(ang).astype(np.float32)
    rep = lambda a: np.tile(a[:, None, :], (1, m, 1)).reshape(T, -1)
    out = np.stack([rep(c), rep(s)], 0)
    return np.ascontiguousarray(out.reshape(2, NTB, 128, -1).transpose(0, 2, 1, 3))


def kernel(**inp):
    f = lambda k: np.ascontiguousarray(np.asarray(inp[k], dtype=np.float32))
    xp, xs = f("x_prompt"), f("x_sample")
    rep = lambda a: np.ascontiguousarray(np.broadcast_to(a[:, None, :], (a.shape[0], 128, a.shape[1])))
    cw, cbias = f("ffn_conv_w"), f("ffn_conv_b")
    cpar = np.concatenate([cw, cbias[:, None, :]], axis=1)
    convp = np.ascontiguousarray(cpar.reshape(4, 4, 2 * NFC, 128).transpose(0, 3, 2, 1))
    shared = {
        "diff_w_in": f("diff_w_in"), "diff_w_out": f("diff_w_out"), "win_w_in": f("win_w_in"),
        "win_w_out": f("win_w_out"), "ffn_w_up": f("ffn_w_up"), "ffn_w_down": f("ffn_w_down"),
        "lam_r": rep(f("diff_lam").reshape(2, 256)), "subg_r": rep(f("diff_subln_g")), "sink_r": rep(f("win_sink")),
        "lnm_g": rep(f("ln_mix_g")), "lnm_b": rep(f("ln_mix_b")), "lnf_g": rep(f("ln_ffn_g")), "lnf_b": rep(f("ln_ffn_b")),
        "convp": convp, "ident": np.eye(128).astype(ml_dtypes.bfloat16),
    }
    jj, ii = np.meshgrid(np.arange(128), np.arange(128), indexing="ij")
    mL = (ii <= jj).astype(np.float32)
    mU = (jj <= ii).astype(np.float32)
    shared["trimask"] = np.stack([np.tile(mL, (1, 4)), np.tile(mU, (1, 4))], 0).astype(ml_dtypes.bfloat16)
    in_maps = []
    for c in range(8):
        two = c < 4
        if two:
            xc = np.concatenate([xp[2 * c], xp[2 * c + 1]], 0)
            pos = np.concatenate([np.arange(2048), np.arange(2048)])
        else:
            xc = xs[c % 2]
            pos = np.arange(4096)
        mb = np.zeros((128, 2, 32), np.float32)
        if two:
            mb[:, 0, 16:] = NEG
            mb[:, 1, :16] = NEG
        fl = np.zeros((128, 2), np.float32)
        fl[:, 0] = 0.0 if two else 1.0
        fl[:, 1] = NEG if two else 0.0
        d = dict(shared)
        d.update({"xin": np.ascontiguousarray(xc), "ropd": _rope_rep(pos, 16, 8), "ropw": _rope_rep(pos, 32, 4), "maskb": mb, "flags": fl})
        in_maps.append(d)
    nc = build()
    res = run_bass_kernel_spmd(nc, in_maps, core_ids=list(range(8)))
    ys = [np.asarray(r["y"], dtype=np.float32) for r in res.results]
    y_prompt = np.stack([ys[c // 2][(c % 2) * 2048:(c % 2 + 1) * 2048] for c in range(8)], 0)
    y_sample = np.stack([ys[4], ys[5]], 0)
    return (y_prompt, y_sample)
```

```python
import math
from contextlib import ExitStack
import numpy as np
import ml_dtypes
import concourse.bass as bass
import concourse.mybir as mybir
from concourse.bass_utils import run_bass_kernel_spmd

F32 = mybir.dt.float32
BF16 = mybir.dt.bfloat16
AF = mybir.ActivationFunctionType
ALU = mybir.AluOpType
AX = mybir.AxisListType

T = 4096
D = 2048
NTB = 32
FF = 5504
NFC = 43
DEPTH = 4
ALPHA = (2 * DEPTH) ** 0.25
EPS = 1e-5
NEG = -30000.0
KD = 6


class Buf:
    def __init__(self, excl=False):
        self.w = None
        self.r = {}
        self.excl = excl


class Eng:
    def __init__(self, sem, sid):
        self.sem, self.sid, self.n, self.seen, self.q = sem, sid, 0, {}, []
        self.dsems, self.dcnt, self.k = [], [], 0

    def wait(self, dep):
        if dep is None:
            return
        sid, sem, val = dep
        if self.seen.get(sid, 0) >= val:
            return
        self.e.wait_ge(sem, val)
        self.seen[sid] = val


def _deps(E, reads, writes):
    for b in reads:
        E.wait(b.w)
        if b.excl:
            for sid, d in b.r.items():
                if sid != E.sid:
                    E.wait(d)
    for b in writes:
        if b.w is not None and b.w[0] != E.sid:
            E.wait(b.w)
        for sid, d in b.r.items():
            if sid != E.sid:
                E.wait(d)


def _mark(d, reads, writes):
    for b in reads:
        b.r[d[0]] = d
    for b in writes:
        b.w = d
        b.r = {}


def op(E, fn, reads=(), writes=()):
    _deps(E, reads, writes)
    E.n += 1
    fn().then_inc(E.sem, 1)
    _mark((E.sid, E.sem, E.n), reads, writes)


def dma(Q, fn, reads=(), writes=()):
    slot = Q.k % KD
    Q.k += 1
    sid = 100 + Q.sid * 10 + slot
    sem = Q.dsems[slot]
    Q.wait((sid, sem, Q.dcnt[slot] * 16))
    for b in reads:
        Q.wait(b.w)
    for b in writes:
        Q.wait(b.w)
        for d in b.r.values():
            Q.wait(d)
    fn().then_inc(sem, 16)
    Q.dcnt[slot] += 1
    _mark((sid, sem, Q.dcnt[slot] * 16), reads, writes)


def build():
    nc = bass.Bass("TRN2", target_bir_lowering=False)
    dt = lambda n, s, d, k="ExternalInput": nc.dram_tensor(n, s, d, kind=k).ap()
    xin = dt("xin", [T, D], F32)
    diff_w_in = dt("diff_w_in", [2, D, 6144], F32)
    diff_w_out = dt("diff_w_out", [2, D, D], F32)
    win_w_in = dt("win_w_in", [2, D, 3072], F32)
    win_w_out = dt("win_w_out", [2, D, D], F32)
    ffn_w_up = dt("ffn_w_up", [4, D, 2 * FF], F32)
    ffn_w_down = dt("ffn_w_down", [4, FF, D], F32)
    lam_r = dt("lam_r", [2, 128, 256], F32)
    subg_r = dt("subg_r", [2, 128, 128], F32)
    sink_r = dt("sink_r", [2, 128, 16], F32)
    lnm_g = dt("lnm_g", [4, 128, D], F32)
    lnm_b = dt("lnm_b", [4, 128, D], F32)
    lnf_g = dt("lnf_g", [4, 128, D], F32)
    lnf_b = dt("lnf_b", [4, 128, D], F32)
    convp = dt("convp", [4, 128, 2 * NFC, 4], F32)
    ropd = dt("ropd", [2, 128, NTB, 64], F32)
    ropw = dt("ropw", [2, 128, NTB, 64], F32)
    maskb = dt("maskb", [128, 2, 32], F32)
    flags = dt("flags", [128, 2], F32)
    ident_d = dt("ident", [128, 128], BF16)
    trimask = dt("trimask", [2, 128, 512], BF16)
    y = dt("y", [T, D], F32, "ExternalOutput")
    X = nc.dram_tensor("X", [T, D], F32).ap()
    X2 = nc.dram_tensor("X2", [T, D], F32).ap()
    XT = nc.dram_tensor("XT", [16, 128, T], BF16).ap()
    XT2 = nc.dram_tensor("XT2", [16, 128, T], BF16).ap()
    QT = nc.dram_tensor("QT", [16, 128, T], BF16).ap()
    KT = nc.dram_tensor("KT", [16, 128, T], BF16).ap()
    Vd = nc.dram_tensor("Vd", [T, D], BF16).ap()
    OT = nc.dram_tensor("OT", [16, 128, T], BF16).ap()
    WUPB = nc.dram_tensor("WUPB", [4, D, 2 * FF], BF16).ap()
    WDNB = nc.dram_tensor("WDNB", [4, FF, D], BF16).ap()

    top = ExitStack()
    with top:
        nsem = 5 + 2 * KD
        sems = [top.enter_context(nc.semaphore(f"s{i}")) for i in range(nsem)]
        PE, ACT, DVE, POOL, SP = [Eng(sems[i], i) for i in range(5)]
        ENGS = [PE, ACT, DVE, POOL, SP]
        for qi, Q in enumerate((POOL, SP)):
            Q.dsems = sems[5 + qi * KD: 5 + (qi + 1) * KD]
            Q.dcnt = [0] * KD
        tE, sE, vE, gE, yE = nc.tensor, nc.scalar, nc.vector, nc.gpsimd, nc.sync
        for E_, h_ in zip(ENGS, (tE, sE, vE, gE, yE)):
            E_.e = h_

        def barrier():
            for E in ENGS:
                for F in ENGS:
                    if F is not E and F.n > 0:
                        E.wait((F.sid, F.sem, F.n))
                for Q in (POOL, SP):
                    for s in range(KD):
                        if Q.dcnt[s]:
                            E.wait((100 + Q.sid * 10 + s, Q.dsems[s], Q.dcnt[s] * 16))

        PSALL = top.enter_context(nc.psum_tensor("psall", [128, 8 * 512], F32))

        class _Bank:
            def __init__(self, i):
                self.i = i

            def __getitem__(self, key):
                return PSALL[:, self.i * 512:(self.i + 1) * 512][key]
        PS = [_Bank(i) for i in range(8)]
        PB = [Buf(excl=True) for _ in range(8)]
        ident = top.enter_context(nc.sbuf_tensor("identt", [128, 128], BF16))
        flg = top.enter_context(nc.sbuf_tensor("flg", [128, 2], F32))
        cb = Buf()
        dma(SP, lambda: yE.dma_start(out=ident[:], in_=ident_d), writes=[cb])
        dma(SP, lambda: yE.dma_start(out=flg[:], in_=flags), writes=[cb])

        uid = [0]

        def sbt(es, name, shape, dtype):
            uid[0] += 1
            return es.enter_context(nc.sbuf_tensor(f"{name}_{uid[0]}", shape, dtype)), Buf()

        def transpose_store(xb, xbB, dst3, stg, stgB, tpi):
            for q4 in range(4):
                bk = tpi[q4 % 2]
                tp = PS[bk][:].bitcast(BF16)
                for j in range(4):
                    kc = q4 * 4 + j
                    op(PE, lambda tp=tp, j=j, kc=kc: tE.transpose(tp[:, j * 128:(j + 1) * 128], xb[:, kc * 128:(kc + 1) * 128], ident[:]),
                       reads=[xbB, cb], writes=[PB[bk]])
                op(DVE, lambda tp=tp, q4=q4: vE.tensor_copy(stg[:, q4 * 4:(q4 + 1) * 4, :], tp[:, 0:512].rearrange("p (a b) -> p a b", a=4)),
                   reads=[PB[bk]], writes=[stgB])
            dma(SP, lambda: yE.dma_start(out=dst3, in_=stg[:]), reads=[stgB])

        def xt_view(XTd):
            return XTd.rearrange("k p t -> p k t")

        def layer_norm(es_t, xr, xrB, Gt, Bt, gB, Xrows, XTd, tb, tpi, tiles, defer=None):
            st, stB, mv, mvB, sc, scB, xb, xbB, stg, stgB = tiles
            for c in range(4):
                op(DVE, lambda c=c: vE.bn_stats(st[:, c, :], xr[:, c * 512:(c + 1) * 512]), reads=[xrB], writes=[stB])
            op(DVE, lambda: vE.bn_aggr(mv[:], st[:].rearrange("p a b -> p (a b)")), reads=[stB], writes=[mvB])
            op(DVE, lambda: vE.tensor_scalar_add(sc[:, 0:1], mv[:, 1:2], EPS), reads=[mvB], writes=[scB])
            op(ACT, lambda: sE.sqrt(sc[:, 1:2], sc[:, 0:1]), reads=[scB], writes=[scB])
            op(DVE, lambda: vE.reciprocal(sc[:, 2:3], sc[:, 1:2]), reads=[scB], writes=[scB])
            op(DVE, lambda: vE.tensor_scalar(xr, xr, mv[:, 0:1], sc[:, 2:3], ALU.subtract, ALU.mult), reads=[xrB, mvB, scB], writes=[xrB])
            op(DVE, lambda: vE.tensor_tensor(xr, xr, Gt[:], ALU.mult), reads=[xrB, gB], writes=[xrB])
            op(DVE, lambda: vE.tensor_tensor(xr, xr, Bt[:], ALU.add), reads=[xrB, gB], writes=[xrB])
            dma(SP, lambda: yE.dma_start(out=Xrows, in_=xr), reads=[xrB])
            op(ACT, lambda: sE.copy(xb[:], xr), reads=[xrB], writes=[xbB])
            if defer is None:
                transpose_store(xb, xbB, xt_view(XTd)[:, :, tb * 128:(tb + 1) * 128], stg, stgB, tpi)
            else:
                defer.append(lambda: transpose_store(xb, xbB, xt_view(XTd)[:, :, tb * 128:(tb + 1) * 128], stg, stgB, tpi))

        def ln_tiles(es, tag):
            st, stB = sbt(es, "st" + tag, [128, 4, 6], F32)
            mv, mvB = sbt(es, "mv" + tag, [128, 2], F32)
            sc, scB = sbt(es, "sc" + tag, [128, 4], F32)
            xb, xbB = sbt(es, "xb" + tag, [128, D], BF16)
            stg, stgB = sbt(es, "stg" + tag, [128, 16, 128], BF16)
            return (st, stB, mv, mvB, sc, scB, xb, xbB, stg, stgB)

        with ExitStack() as es:
            xrs = [sbt(es, f"pxr{i}", [128, D], F32) for i in range(2)]
            xbs = [sbt(es, f"pxb{i}", [128, D], BF16) for i in range(2)]
            stgs = [sbt(es, f"pstg{i}", [128, 16, 128], BF16) for i in range(2)]
            for tb in range(NTB):
                xr, xrB = xrs[tb % 2]
                xb, xbB = xbs[tb % 2]
                stg, stgB = stgs[tb % 2]
                dma(SP, lambda xr=xr, tb=tb: yE.dma_start(out=xr[:], in_=xin[tb * 128:(tb + 1) * 128, :]), writes=[xrB])
                op(ACT, lambda xr=xr, xb=xb: sE.copy(xb[:], xr[:]), reads=[xrB], writes=[xbB])
                transpose_store(xb, xbB, xt_view(XT)[:, :, tb * 128:(tb + 1) * 128], stg, stgB, [6, 7])
            barrier()

        for L in range(DEPTH):
            j = L // 2
            is_diff = (L % 2 == 0)
            Xsrc = xin if L == 0 else X
            w_in = diff_w_in[j] if is_diff else win_w_in[j]
            ncols = 6144 if is_diff else 3072
            m, hh = (8, 8) if is_diff else (4, 16)
            dh = 512 // m
            w_v = w_in.rearrange("(k p) n -> p k n", p=128)
            with ExitStack() as es:
                xts = [sbt(es, f"qxt{i}", [128, 16, 1024], BF16) for i in range(2)]
                wts = [sbt(es, f"qwt{i}", [128, 16, 512], BF16) for i in range(2)]
                rop, ropB = sbt(es, "rop", [128, 2, NTB, 64], F32)
                t1, t1B = sbt(es, "rt1", [128, m, hh], F32)
                t2, t2B = sbt(es, "rt2", [128, m, hh], F32)
                qbs = [sbt(es, f"qb{i}", [128, 512], BF16) for i in range(2)]
                stgs = [sbt(es, f"qstg{i}", [128, 4, 1024], BF16) for i in range(2)]
                rsrc = ropd if is_diff else ropw
                dma(SP, lambda: yE.dma_start(out=rop[:], in_=rsrc.rearrange("c p t f -> p c t f")), writes=[ropB])
                cnt = 0
                wcnt = 0
                qpend = []

                def q_load(tg):
                    xt, xtB = xts[tg % 2]
                    dma(SP, lambda: yE.dma_start(out=xt[:], in_=xt_view(XT)[:, :, tg * 1024:(tg + 1) * 1024]), writes=[xtB])
                q_load(0)
                for tg in range(4):
                    xt, xtB = xts[tg % 2]
                    if tg + 1 < 4:
                        q_load(tg + 1)
                    for nt in range(ncols // 512):
                        wt, wtB = wts[wcnt % 2]
                        stg, stgB = stgs[wcnt % 2]
                        wcnt += 1
                        dma(POOL, lambda wt=wt, nt=nt: gE.dma_start(out=wt[:], in_=w_v[:, :, nt * 512:(nt + 1) * 512]), writes=[wtB])
                        if is_diff:
                            kind = "q" if nt < 4 else ("k" if nt < 8 else "v")
                            h0 = (nt % 4) * 4
                        else:
                            kind = "q" if nt < 4 else ("k" if nt == 4 else "v")
                            h0 = nt * 4 if nt < 4 else 0
                        for tb in range(8):
                            bk = cnt % 4
                            cnt += 1
                            ps = PS[bk]
                            tbg = tg * 8 + tb
                            for kc in range(16):
                                op(PE, lambda ps=ps, xt=xt, wt=wt, kc=kc, tb=tb: tE.matmul(ps[:], lhsT=xt[:, kc, tb * 128:(tb + 1) * 128], rhs=wt[:, kc, :], start=(kc == 0), stop=(kc == 15)),
                                   reads=[xtB, wtB], writes=[PB[bk]])
                            while qpend:
                                qpend.pop(0)()
                            qb, qbB = qbs[cnt % 2]
                            if kind == "v":
                                op(DVE, lambda qb=qb, ps=ps: vE.tensor_copy(qb[:], ps[:]), reads=[PB[bk]], writes=[qbB])
                                vcols = (nt - 8) * 512 if is_diff else 0
                                dma(SP, lambda qb=qb, tbg=tbg, vcols=vcols: yE.dma_start(out=Vd[tbg * 128:(tbg + 1) * 128, vcols:vcols + 512], in_=qb[:]), reads=[qbB])
                                continue
                            ps3 = ps[:].rearrange("p (m d) -> p m d", m=m)
                            qb3 = qb[:].rearrange("p (m d) -> p m d", m=m)
                            c3 = rop[:, 0, tbg, :].rearrange("p (m h) -> p m h", m=m)
                            s3 = rop[:, 1, tbg, :].rearrange("p (m h) -> p m h", m=m)
                            x1, x2 = ps3[:, :, 0:hh], ps3[:, :, hh:2 * hh]
                            R = [PB[bk], ropB]
                            op(DVE, lambda x1=x1, c3=c3: vE.tensor_tensor(t1[:], x1, c3, ALU.mult), reads=R, writes=[t1B])
                            op(DVE, lambda x2=x2, s3=s3: vE.tensor_tensor(t2[:], x2, s3, ALU.mult), reads=R, writes=[t2B])
                            op(DVE, lambda qb3=qb3: vE.tensor_tensor(qb3[:, :, 0:hh], t1[:], t2[:], ALU.subtract), reads=[t1B, t2B], writes=[qbB])
                            op(DVE, lambda x2=x2, c3=c3: vE.tensor_tensor(t1[:], x2, c3, ALU.mult), reads=R, writes=[t1B])
                            op(DVE, lambda x1=x1, s3=s3: vE.tensor_tensor(t2[:], x1, s3, ALU.mult), reads=R, writes=[t2B])
                            op(DVE, lambda qb3=qb3: vE.tensor_tensor(qb3[:, :, hh:2 * hh], t1[:], t2[:], ALU.add), reads=[t1B, t2B], writes=[qbB])
                            op(DVE, lambda qb3=qb3, ps3=ps3: vE.tensor_copy(qb3[:, :, 2 * hh:], ps3[:, :, 2 * hh:]), reads=[PB[bk]], writes=[qbB])

                            def trp(qb=qb, qbB=qbB, stg=stg, stgB=stgB, tb=tb, tbk=4 + (cnt % 2), last=(tb == 7), kind=kind, h0=h0, tg=tg):
                                tp = PS[tbk][:].bitcast(BF16)
                                for jj in range(4):
                                    op(PE, lambda jj=jj: tE.transpose(tp[:, jj * 128:(jj + 1) * 128], qb[:, jj * 128:(jj + 1) * 128], ident[:]),
                                       reads=[qbB, cb], writes=[PB[tbk]])
                                op(DVE, lambda: vE.tensor_copy(stg[:, :, tb * 128:(tb + 1) * 128], tp[:, 0:512].rearrange("p (a b) -> p a b", a=4)),
                                   reads=[PB[tbk]], writes=[stgB])
                                if last:
                                    dstT = QT if kind == "q" else KT
                                    dma(SP, lambda: yE.dma_start(out=dstT[h0:h0 + 4].rearrange("h p t -> p h t")[:, :, tg * 1024:(tg + 1) * 1024], in_=stg[:]), reads=[stgB])
                            qpend.append(trp)
                while qpend:
                    qpend.pop(0)()
                barrier()
            if L == 0:
                for L2 in range(DEPTH):
                    srcu = ffn_w_up[L2].rearrange("(k p) n -> p k n", p=128)
                    dstu = WUPB[L2].rearrange("(k p) n -> p k n", p=128)
                    for cbk in range(16):
                        dma(POOL, lambda srcu=srcu, dstu=dstu, cbk=cbk: gE.dma_start(out=dstu[:, :, cbk * 688:(cbk + 1) * 688], in_=srcu[:, :, cbk * 688:(cbk + 1) * 688]))
                    srcd = ffn_w_down[L2].rearrange("(i p) n -> p i n", p=128)
                    dstd = WDNB[L2].rearrange("(i p) n -> p i n", p=128)
                    for i0_ in range(0, NFC, 11):
                        i1_ = min(NFC, i0_ + 11)
                        dma(POOL, lambda srcd=srcd, dstd=dstd, i0_=i0_, i1_=i1_: gE.dma_start(out=dstd[:, i0_:i1_, :], in_=srcd[:, i0_:i1_, :]))
            if is_diff:
                lam_init = 0.8 - 0.6 * math.exp(-0.3 * L)
                with ExitStack() as es:
                    qTs = [sbt(es, f"aq{i}", [128, T], BF16) for i in range(2)]
                    kTs = [sbt(es, f"ak{i}", [128, T], BF16) for i in range(2)]
                    vas = [sbt(es, f"av{i}", [128, 32, 130], BF16) for i in range(2)]
                    Pb = [sbt(es, f"ap{i}", [128, 512], BF16) for i in range(3)]
                    lm, lmB = sbt(es, "lm", [128, 256], F32)
                    gs, gsB = sbt(es, "gs", [128, 128], F32)
                    mk, mkB = sbt(es, "mk", [128, 2, 32], F32)
                    sm, smB = sbt(es, "sm", [128, 16], F32)
                    of, ofB = sbt(es, "of", [128, 128], F32)
                    sq, sqB = sbt(es, "sq", [128, 128], F32)
                    obs = [sbt(es, f"ob{i}", [128, 128], BF16) for i in range(4)]
                    ostgs = [sbt(es, f"aost{i}", [128, 512], BF16) for i in range(2)]
                    dma(SP, lambda: yE.dma_start(out=lm[:], in_=lam_r[j]), writes=[lmB])
                    dma(SP, lambda: yE.dma_start(out=gs[:], in_=subg_r[j]), writes=[gsB])
                    dma(SP, lambda: yE.dma_start(out=mk[:], in_=maskb), writes=[mkB])
                    for i in range(2):
                        op(DVE, lambda i=i: vE.memset(vas[i][0][:], 1.0), writes=[vas[i][1]])
                    op(DVE, lambda: vE.tensor_scalar_mul(gs[:], gs[:], 1.0 - lam_init), reads=[gsB], writes=[gsB])
                    op(DVE, lambda: vE.tensor_tensor(sq[:, 0:64], lm[:, 0:64], lm[:, 64:128], ALU.mult), reads=[lmB], writes=[sqB])
                    op(DVE, lambda: vE.tensor_tensor(sq[:, 64:128], lm[:, 128:192], lm[:, 192:256], ALU.mult), reads=[lmB], writes=[sqB])
                    op(DVE, lambda: vE.reduce_sum(sm[:, 0:1], sq[:, 0:64], AX.X), reads=[sqB], writes=[smB])
                    op(DVE, lambda: vE.reduce_sum(sm[:, 1:2], sq[:, 64:128], AX.X), reads=[sqB], writes=[smB])
                    op(ACT, lambda: sE.activation(sm[:, 2:4], sm[:, 0:2], AF.Exp), reads=[smB], writes=[smB])
                    op(DVE, lambda: vE.tensor_tensor(sm[:, 4:5], sm[:, 3:4], sm[:, 2:3], ALU.subtract), reads=[smB], writes=[smB])
                    op(DVE, lambda: vE.tensor_scalar_add(sm[:, 5:6], sm[:, 4:5], -lam_init), reads=[smB], writes=[smB])
                    Vv = Vd.rearrange("(kb p) e -> p kb e", p=128)
                    P2 = [sbt(es, f"ap2{i}", [128, 1024], BF16) for i in range(3)]
                    accS8, accSB = sbt(es, "accS8", [128, 8, 130], F32)
                    rz, rzB = sbt(es, "rz", [128, 28], F32)
                    of4, ofB = sbt(es, "of4", [128, 4, 128], F32)
                    sq4, sqB4 = sbt(es, "sq4", [128, 4, 128], F32)

                    def a_load(h):
                        qT, qTB = qTs[h % 2]
                        kT, kTB = kTs[h % 2]
                        va, vaB = vas[h % 2]
                        dma(SP, lambda: yE.dma_start(out=qT[:], in_=QT[h]), writes=[qTB])
                        dma(SP, lambda: yE.dma_start(out=kT[:], in_=KT[h]), writes=[kTB])
                        dma(SP, lambda: yE.dma_start(out=va[:, :, 0:128], in_=Vv[:, :, h * 128:(h + 1) * 128]), writes=[vaB])

                    steps = [(h, qg, kb) for h in range(16) for qg in range(8) for kb in range(32)]
                    accs = {}
                    for qb_ in range(4):
                        for c in range(2):
                            a = qb_ * 2 + c
                            accs[(qb_, c)] = (4 + a // 3, (a % 3) * 130)

                    def emit_qk(idx):
                        h, qg, kb = steps[idx]
                        qT, qTB = qTs[h % 2]
                        kT, kTB = kTs[h % 2]
                        sp = idx % 2
                        for c in range(2):
                            bk = 2 * sp + c
                            op(PE, lambda c=c, bk=bk: tE.matmul(PS[bk][:], lhsT=kT[64 * c:64 * c + 64, kb * 128:(kb + 1) * 128], rhs=qT[64 * c:64 * c + 64, qg * 512:(qg + 1) * 512], start=True, stop=True),
                               reads=[kTB, qTB], writes=[PB[bk]])

                    def emit_exp(idx):
                        h, qg, kb = steps[idx]
                        sp = idx % 2
                        P, PBf = P2[idx % 3]
                        seg = qg // 4
                        op(ACT, lambda: sE.activation(P[:], PSALL[:, sp * 1024:(sp + 1) * 1024], AF.Exp, bias=mk[:, seg, kb:kb + 1], scale=0.125),
                           reads=[PB[2 * sp], PB[2 * sp + 1], mkB], writes=[PBf])

                    def emit_pv(idx):
                        h, qg, kb = steps[idx]
                        va, vaB = vas[h % 2]
                        P, PBf = P2[idx % 3]
                        for c in range(2):
                            for qb_ in range(4):
                                bk, off = accs[(qb_, c)]
                                op(PE, lambda bk=bk, off=off, qb_=qb_, c=c: tE.matmul(PS[bk][:, off:off + 129], lhsT=P[:, c * 512 + qb_ * 128:c * 512 + (qb_ + 1) * 128], rhs=va[:, kb, 0:129], start=(kb == 0 and c == 0 and qb_ in (0, 2, 3)), stop=(kb == 31), skip_group_check=True),
                                   reads=[PBf, vaB], writes=[PB[bk]])

                    pending = []
                    pending_act = []
                    rz2, rz2B = sbt(es, "rz2", [128, 8], F32)

                    def emit_epilogue(h, qg, ocnt):
                        ostg, ostgB = ostgs[ocnt % 2]
                        for b3, na in ((0, 3), (1, 3), (2, 2)):
                            op(DVE, lambda b3=b3, na=na: vE.tensor_copy(accS8[:, 3 * b3:3 * b3 + na, :], PS[4 + b3][:, 0:130 * na].rearrange("p (a c) -> p a c", a=na)),
                               reads=[PB[4 + b3]], writes=[accSB])
                        op(DVE, lambda: vE.reciprocal(rz[:, 0:8].rearrange("p (a o) -> p a o", o=1), accS8[:, :, 128:129]), reads=[accSB], writes=[rzB])
                        op(DVE, lambda: vE.tensor_scalar_mul(rz[:, 8:12], rz[:, 1:8:2], sm[:, 5:6]), reads=[rzB, smB], writes=[rzB])
                        for qb_ in range(4):
                            op(DVE, lambda qb_=qb_: vE.tensor_scalar_mul(of4[:, qb_, :], accS8[:, 2 * qb_, 0:128], rz[:, 2 * qb_:2 * qb_ + 1]), reads=[accSB, rzB], writes=[ofB])
                            op(DVE, lambda qb_=qb_: vE.scalar_tensor_tensor(of4[:, qb_, :], accS8[:, 2 * qb_ + 1, 0:128], rz[:, 8 + qb_:9 + qb_], of4[:, qb_, :], ALU.mult, ALU.add), reads=[accSB, rzB, ofB], writes=[ofB])
                        op(DVE, lambda: vE.tensor_tensor(sq4[:], of4[:], of4[:], ALU.mult), reads=[ofB], writes=[sqB])
                        op(DVE, lambda: vE.reduce_sum(rz[:, 12:16], sq4[:], AX.X), reads=[sqB], writes=[rzB])
                        op(DVE, lambda: vE.tensor_scalar(rz[:, 16:20], rz[:, 12:16], 1.0 / 128, EPS, ALU.mult, ALU.add), reads=[rzB], writes=[rzB])
                        def part1c():
                            op(ACT, lambda: sE.activation(rz2[:, 0:4], rz[:, 16:20], AF.Ln), reads=[rzB], writes=[rz2B])
                            op(ACT, lambda: sE.activation(rz2[:, 4:8], rz2[:, 0:4], AF.Exp, scale=-0.5), reads=[rz2B], writes=[rz2B])
                            for qb_ in range(4):
                                ob, obB = obs[qb_]
                                op(DVE, lambda qb_=qb_, ob=ob: vE.scalar_tensor_tensor(ob[:], of4[:, qb_, :], rz2[:, 4 + qb_:5 + qb_], gs[:], ALU.mult, ALU.mult), reads=[ofB, rz2B, gsB], writes=[obB])
                        pending_act.append(part1c)

                        def part2():
                            tp = PS[7][:].bitcast(BF16)
                            for qb_ in range(4):
                                ob, obB = obs[qb_]
                                op(PE, lambda qb_=qb_, ob=ob: tE.transpose(tp[:, qb_ * 128:(qb_ + 1) * 128], ob[:], ident[:]), reads=[obB, cb], writes=[PB[7]])
                            op(DVE, lambda: vE.tensor_copy(ostg[:], tp[:, 0:512]), reads=[PB[7]], writes=[ostgB])
                            dma(SP, lambda: yE.dma_start(out=OT[h][:, qg * 512:(qg + 1) * 512], in_=ostg[:]), reads=[ostgB])
                        pending.append(part2)

                    a_load(0)
                    emit_qk(0)
                    emit_qk(1)
                    ocnt = 0
                    for idx, (h, qg, kb) in enumerate(steps):
                        if qg == 0 and kb == 0 and h + 1 < 16:
                            a_load(h + 1)
                        emit_exp(idx)
                        if idx + 2 < len(steps):
                            emit_qk(idx + 2)
                        emit_pv(idx)
                        if kb == 6 and pending_act:
                            pending_act.pop(0)()
                        if kb == 14 and pending:
                            pending.pop(0)()
                        if kb == 31:
                            emit_epilogue(h, qg, ocnt)
                            ocnt += 1
                    while pending_act:
                        pending_act.pop(0)()
                    while pending:
                        pending.pop(0)()
                    barrier()
            else:
                with ExitStack() as es:
                    q4s = [sbt(es, f"wq{i}", [128, 4, T], BF16) for i in range(2)]
                    kTs = [sbt(es, f"wk{i}", [128, T], BF16) for i in range(2)]
                    vas = [sbt(es, f"wv{i}", [128, 32, 130], BF16) for i in range(2)]
                    Pb = [sbt(es, f"wp{i}", [128, 512], BF16) for i in range(3)]
                    tm, tmB = sbt(es, "tm", [128, 2, 512], BF16)
                    sk, skB = sbt(es, "sk", [128, 16], F32)
                    sm, smB = sbt(es, "wsm", [128, 8], F32)
                    wacc, waccB = sbt(es, "wacc", [128, 4, 130], F32)
                    wz, wzB = sbt(es, "wz", [128, 8], F32)
                    obs4 = [sbt(es, f"wob4{i}", [128, 128], BF16) for i in range(4)]
                    obs = [sbt(es, f"wob{i}", [128, 128], BF16) for i in range(2)]
                    ostgs = [sbt(es, f"wost{i}", [128, 4, 128], BF16) for i in range(2)]
                    dma(SP, lambda: yE.dma_start(out=tm[:], in_=trimask.rearrange("c p f -> p c f")), writes=[tmB])
                    dma(SP, lambda: yE.dma_start(out=sk[:], in_=sink_r[j]), writes=[skB])
                    op(ACT, lambda: sE.activation(sk[:], sk[:], AF.Exp), reads=[skB], writes=[skB])
                    for i in range(2):
                        op(POOL, lambda i=i: gE.memset(vas[i][0][:], 1.0), writes=[vas[i][1]])
                    Vv = Vd.rearrange("(kb p) e -> p kb e", p=128)
                    accb = [(4, 0), (4, 130), (4, 260), (5, 0)]
                    wsteps = []
                    for g in range(4):
                        for n in range(32):
                            kbs = [kb for kb in (n - 1, n, n + 1) if 0 <= kb < 32]
                            for i_, kb in enumerate(kbs):
                                wsteps.append((g, n, kb, i_, len(kbs)))

                    def w_load(g):
                        q4, q4B = q4s[g % 2]
                        kT, kTB = kTs[g % 2]
                        va, vaB = vas[g % 2]
                        dma(SP, lambda: yE.dma_start(out=q4[:], in_=QT[4 * g:4 * g + 4].rearrange("h p t -> p h t")), writes=[q4B])
                        dma(SP, lambda: yE.dma_start(out=kT[:], in_=KT[g]), writes=[kTB])
                        dma(SP, lambda: yE.dma_start(out=va[:, :, 0:128], in_=Vv[:, :, g * 128:(g + 1) * 128]), writes=[vaB])

                    def w_qk(idx):
                        g, n, kb, i_, nk = wsteps[idx]
                        q4, q4B = q4s[g % 2]
                        kT, kTB = kTs[g % 2]
                        sb_ = idx % 4
                        op(PE, lambda: tE.matmul(PS[sb_][:], lhsT=kT[:, kb * 128:(kb + 1) * 128], rhs=q4[:, :, n * 128:(n + 1) * 128], start=True, stop=True),
                           reads=[kTB, q4B], writes=[PB[sb_]])

                    def w_exp(idx):
                        g, n, kb, i_, nk = wsteps[idx]
                        sb_ = idx % 4
                        P, PBf = Pb[idx % 3]
                        if (n, kb) in ((15, 16), (16, 15)):
                            op(ACT, lambda: sE.activation(P[:], PS[sb_][:], AF.Exp, bias=flg[:, 1:2], scale=128 ** -0.5), reads=[PB[sb_], cb], writes=[PBf])
                        else:
                            op(ACT, lambda: sE.activation(P[:], PS[sb_][:], AF.Exp, scale=128 ** -0.5), reads=[PB[sb_]], writes=[PBf])
                        if kb != n:
                            mi = 0 if kb < n else 1
                            op(POOL, lambda: gE.tensor_tensor(P[:], P[:], tm[:, mi, :], ALU.mult), reads=[PBf, tmB], writes=[PBf])

                    def w_pv(idx):
                        g, n, kb, i_, nk = wsteps[idx]
                        va, vaB = vas[g % 2]
                        P, PBf = Pb[idx % 3]
                        for r in range(4):
                            bk, off = accb[r]
                            op(PE, lambda bk=bk, off=off, r=r: tE.matmul(PS[bk][:, off:off + 129], lhsT=P[:, r * 128:(r + 1) * 128], rhs=va[:, kb, 0:129], start=(i_ == 0 and r in (0, 3)), stop=(i_ == nk - 1), skip_group_check=True),
                               reads=[PBf, vaB], writes=[PB[bk]])

                    wpend = []

                    def w_epi(g, n, ocnt):
                        ostg, ostgB = ostgs[ocnt % 2]
                        op(DVE, lambda: vE.tensor_copy(wacc[:, 0:3, :], PS[4][:, 0:390].rearrange("p (a c) -> p a c", a=3)), reads=[PB[4]], writes=[waccB])
                        op(DVE, lambda: vE.tensor_copy(wacc[:, 3:4, :], PS[5][:, 0:130].rearrange("p (a c) -> p a c", a=1)), reads=[PB[5]], writes=[waccB])
                        op(DVE, lambda: vE.tensor_tensor(wz[:, 0:4].rearrange("p (a o) -> p a o", o=1), wacc[:, :, 128:129], sk[:, 4 * g:4 * g + 4].rearrange("p (a o) -> p a o", o=1), ALU.add), reads=[waccB, skB], writes=[wzB])
                        op(DVE, lambda: vE.reciprocal(wz[:, 4:8], wz[:, 0:4]), reads=[wzB], writes=[wzB])
                        for r in range(4):
                            ob, obB = obs4[r]
                            op(DVE, lambda r=r, ob=ob: vE.tensor_scalar_mul(ob[:], wacc[:, r, 0:128], wz[:, 4 + r:5 + r]), reads=[waccB, wzB], writes=[obB])

                        def part2():
                            tp = PS[7][:].bitcast(BF16)
                            for r in range(4):
                                ob, obB = obs4[r]
                                op(PE, lambda r=r, ob=ob: tE.transpose(tp[:, r * 128:(r + 1) * 128], ob[:], ident[:]), reads=[obB, cb], writes=[PB[7]])
                            op(DVE, lambda: vE.tensor_copy(ostg[:], tp[:, 0:512].rearrange("p (a b) -> p a b", a=4)), reads=[PB[7]], writes=[ostgB])
                            dma(SP, lambda: yE.dma_start(out=OT[4 * g:4 * g + 4].rearrange("h p t -> p h t")[:, :, n * 128:(n + 1) * 128], in_=ostg[:]), reads=[ostgB])
                        wpend.append(part2)

                    w_load(0)
                    w_qk(0)
                    w_qk(1)
                    ocnt = 0
                    for idx, (g, n, kb, i_, nk) in enumerate(wsteps):
                        if n == 0 and i_ == 0 and g + 1 < 4:
                            w_load(g + 1)
                        w_exp(idx)
                        w_pv(idx)
                        if idx + 2 < len(wsteps):
                            w_qk(idx + 2)
                        if i_ == 1 and wpend:
                            wpend.pop(0)()
                        if i_ == nk - 1:
                            w_epi(g, n, ocnt)
                            ocnt += 1
                    while wpend:
                        wpend.pop(0)()
                    barrier()
            w_out = (diff_w_out if is_diff else win_w_out)[j].rearrange("(k p) n -> p k n", p=128)
            with ExitStack() as es:
                wo, woB = sbt(es, "wo", [128, 16, D], BF16)
                Gt, gB = sbt(es, "lng", [128, D], F32)
                Bt, _ = sbt(es, "lnb", [128, D], F32)
                ots = [sbt(es, f"oot{i}", [128, 16, 128], BF16) for i in range(2)]
                xrs = [sbt(es, f"oxr{i}", [128, D], F32) for i in range(2)]
                lts = [ln_tiles(es, f"o{i}") for i in range(2)]
                for q in range(4):
                    dma(POOL, lambda q=q: gE.dma_start(out=wo[:, 4 * q:4 * q + 4, :], in_=w_out[:, 4 * q:4 * q + 4, :]), writes=[woB])
                dma(SP, lambda: yE.dma_start(out=Gt[:], in_=lnm_g[L]), writes=[gB])
                dma(SP, lambda: yE.dma_start(out=Bt[:], in_=lnm_b[L]), writes=[gB])
                cnt = 0
                opend = []

                def op_load(tb):
                    ot, otB = ots[tb % 2]
                    xr, xrB = xrs[tb % 2]
                    dma(SP, lambda: yE.dma_start(out=ot[:], in_=OT.rearrange("h p t -> p h t")[:, :, tb * 128:(tb + 1) * 128]), writes=[otB])
                    dma(SP, lambda: yE.dma_start(out=xr[:], in_=Xsrc[tb * 128:(tb + 1) * 128, :]), writes=[xrB])
                op_load(0)
                for tb in range(NTB):
                    ot, otB = ots[tb % 2]
                    xr, xrB = xrs[tb % 2]
                    if tb + 1 < NTB:
                        op_load(tb + 1)
                    for nt in range(4):
                        bk = cnt % 6
                        cnt += 1
                        ps = PS[bk]
                        for h in range(16):
                            op(PE, lambda ps=ps, ot=ot, h=h, nt=nt: tE.matmul(ps[:], lhsT=ot[:, h, :], rhs=wo[:, h, nt * 512:(nt + 1) * 512], start=(h == 0), stop=(h == 15)),
                               reads=[otB, woB], writes=[PB[bk]])
                        op(DVE, lambda xr=xr, ps=ps, nt=nt: vE.scalar_tensor_tensor(xr[:, nt * 512:(nt + 1) * 512], xr[:, nt * 512:(nt + 1) * 512], ALPHA, ps[:], ALU.mult, ALU.add),
                           reads=[PB[bk], xrB], writes=[xrB])
                    while opend:
                        opend.pop(0)()
                    layer_norm(es, xr[:], xrB, Gt, Bt, gB, X2[tb * 128:(tb + 1) * 128, :], XT2, tb, [6, 7], lts[tb % 2], defer=opend)
                while opend:
                    opend.pop(0)()
                barrier()
            wup = WUPB[L].rearrange("(k p) n -> p k n", p=128)
            wdn = WDNB[L].rearrange("(i p) n -> p i n", p=128)
            Xdst = y if L == DEPTH - 1 else X
            with ExitStack() as es:
                xt, xtB = sbt(es, "fxt", [128, 16, 514], BF16)
                xr4, xr4B = sbt(es, "fxr", [128, 4, D], F32)
                wgs = [sbt(es, f"fwg{i}", [128, 16, 512], BF16) for i in range(2)]
                wus = [sbt(es, f"fwu{i}", [128, 16, 512], BF16) for i in range(2)]
                tgs = [sbt(es, f"ftg{i}", [128, 512], F32) for i in range(2)]
                tus = [sbt(es, f"ftu{i}", [128, 512], F32) for i in range(2)]
                sgs = [sbt(es, f"fsg{i}", [128, 512], F32) for i in range(2)]
                aT, aTB = sbt(es, "faT", [128, 11, 512], BF16)
                wds = [sbt(es, f"fwd{i}", [128, 11, 512], BF16) for i in range(2)]
                cp, cpB = sbt(es, "fcp", [128, 2 * NFC, 4], F32)
                Gt, gB = sbt(es, "flng", [128, D], F32)
                Bt, _ = sbt(es, "flnb", [128, D], F32)
                lts = [ln_tiles(es, f"f{i}") for i in range(2)]
                dma(SP, lambda: yE.dma_start(out=cp[:], in_=convp[L]), writes=[cpB])
                dma(SP, lambda: yE.dma_start(out=Gt[:], in_=lnf_g[L]), writes=[gB])
                dma(SP, lambda: yE.dma_start(out=Bt[:], in_=lnf_b[L]), writes=[gB])
                XT2v = xt_view(XT2)

                def load_xt(tg):
                    t0 = tg * 512
                    dma(SP, lambda: yE.dma_start(out=xt[:, :, 1:513], in_=XT2v[:, :, t0:t0 + 512]), writes=[xtB])
                    with nc.allow_non_contiguous_dma(reason="halo column"):
                        if tg > 0:
                            dma(SP, lambda: yE.dma_start(out=xt[:, :, 0:1], in_=XT2v[:, :, t0 - 1:t0]), writes=[xtB])
                        else:
                            op(DVE, lambda: vE.memset(xt[:, :, 0:1], 0.0), writes=[xtB])
                        if tg < 7:
                            dma(SP, lambda: yE.dma_start(out=xt[:, :, 513:514], in_=XT2v[:, :, t0 + 512:t0 + 513]), writes=[xtB])
                        else:
                            op(DVE, lambda: vE.memset(xt[:, :, 513:514], 0.0), writes=[xtB])
                    if tg == 4:
                        op(DVE, lambda: vE.tensor_scalar_mul(xt[:, :, 0:1], xt[:, :, 0:1], flg[:, 0:1]), reads=[xtB, cb], writes=[xtB])
                    if tg == 3:
                        op(DVE, lambda: vE.tensor_scalar_mul(xt[:, :, 513:514], xt[:, :, 513:514], flg[:, 0:1]), reads=[xtB, cb], writes=[xtB])

                cnts = {"c": 0, "w": 0, "d": 0, "y": 0}
                aTs = [(aT, aTB), sbt(es, "faT2", [128, 11, 512], BF16)]

                def up_tile(qc, qf, ca, ncw):
                    aT_, aTB_ = aTs[qc % 2]
                    c0 = qf * 11
                    wg, wgB = wgs[cnts["w"] % 2]
                    wu, wuB = wus[cnts["w"] % 2]
                    cnts["w"] += 1
                    dma(POOL, lambda: gE.dma_start(out=wg[:, :, 0:ncw * 128], in_=wup[:, :, ca * 128:(ca + ncw) * 128]), writes=[wgB])
                    dma(POOL, lambda: gE.dma_start(out=wu[:, :, 0:ncw * 128], in_=wup[:, :, FF + ca * 128:FF + (ca + ncw) * 128]), writes=[wuB])
                    for ii in range(ncw):
                        i = ca + ii
                        ccnt = cnts["c"]
                        cnts["c"] += 1
                        tg_, tgB = tgs[ccnt % 2]
                        tu_, tuB = tus[ccnt % 2]
                        sg_, sgB = sgs[ccnt % 2]
                        gb, ub, hb = ccnt % 2, 2 + ccnt % 2, 4 + ccnt % 2
                        for (w_, wB_, bnk, hoff) in ((wg, wgB, gb, 0), (wu, wuB, ub, 2)):
                            for kc in range(16):
                                op(PE, lambda w_=w_, bnk=bnk, kc=kc, ii=ii: tE.matmul(PS[bnk][:], lhsT=w_[:, kc, ii * 128:(ii + 1) * 128], rhs=xt[:, kc, 1:513], start=(kc == 0), stop=(kc == 15)),
                                   reads=[wB_, xtB], writes=[PB[bnk]])
                            for kc in range(16):
                                op(PE, lambda w_=w_, hb=hb, hoff=hoff, kc=kc, ii=ii: tE.matmul(PS[hb][:, hoff:hoff + 2], lhsT=w_[:, kc, ii * 128:(ii + 1) * 128], rhs=xt[:, kc, 0:514:513], start=(kc == 0), stop=(kc == 15)),
                                   reads=[wB_, xtB], writes=[PB[hb]])
                        for (tt, ttB, bnk, hoff, ci) in ((tg_, tgB, gb, 0, i), (tu_, tuB, ub, 2, NFC + i)):
                            Gp, Hp = PS[bnk], PS[hb]
                            w0, w1, w2, bb = cp[:, ci, 0:1], cp[:, ci, 1:2], cp[:, ci, 2:3], cp[:, ci, 3:4]
                            op(DVE, lambda tt=tt, Gp=Gp, w1=w1, bb=bb: vE.tensor_scalar(tt[:], Gp[:], w1, bb, ALU.mult, ALU.add), reads=[PB[bnk], cpB], writes=[ttB])
                            op(DVE, lambda tt=tt, Gp=Gp, w0=w0: vE.scalar_tensor_tensor(tt[:, 1:512], Gp[:, 0:511], w0, tt[:, 1:512], ALU.mult, ALU.add), reads=[PB[bnk], cpB, ttB], writes=[ttB])
                            op(DVE, lambda tt=tt, Hp=Hp, w0=w0, hoff=hoff: vE.scalar_tensor_tensor(tt[:, 0:1], Hp[:, hoff:hoff + 1], w0, tt[:, 0:1], ALU.mult, ALU.add), reads=[PB[hb], cpB, ttB], writes=[ttB])
                            op(DVE, lambda tt=tt, Gp=Gp, w2=w2: vE.scalar_tensor_tensor(tt[:, 0:511], Gp[:, 1:512], w2, tt[:, 0:511], ALU.mult, ALU.add), reads=[PB[bnk], cpB, ttB], writes=[ttB])
                            op(DVE, lambda tt=tt, Hp=Hp, w2=w2, hoff=hoff: vE.scalar_tensor_tensor(tt[:, 511:512], Hp[:, hoff + 1:hoff + 2], w2, tt[:, 511:512], ALU.mult, ALU.add), reads=[PB[hb], cpB, ttB], writes=[ttB])
                        op(ACT, lambda sg_=sg_, tg_=tg_: sE.activation(sg_[:], tg_[:], AF.Silu), reads=[tgB], writes=[sgB])
                        op(DVE, lambda sg_=sg_, tu_=tu_, i=i: vE.tensor_tensor(aT_[:, i - c0, :], sg_[:], tu_[:], ALU.mult), reads=[sgB, tuB], writes=[aTB_])

                def down(qc, qf):
                    aT_, aTB_ = aTs[qc % 2]
                    c0 = qf * 11
                    nch = min(NFC, c0 + 11) - c0
                    for nt in range(4):
                        wd, wdB = wds[cnts["d"] % 2]
                        cnts["d"] += 1
                        dma(POOL, lambda wd=wd, nt=nt: gE.dma_start(out=wd[:, 0:nch, :], in_=wdn[:, c0:c0 + nch, nt * 512:(nt + 1) * 512]), writes=[wdB])
                        for tb in range(4):
                            bk = 6 + cnts["y"] % 2
                            cnts["y"] += 1
                            for jj in range(nch):
                                op(PE, lambda bk=bk, jj=jj, tb=tb, wd=wd: tE.matmul(PS[bk][:], lhsT=aT_[:, jj, tb * 128:(tb + 1) * 128], rhs=wd[:, jj, :], start=(jj == 0), stop=(jj == nch - 1)),
                                   reads=[aTB_, wdB], writes=[PB[bk]])
                            dst = xr4[:, tb, nt * 512:(nt + 1) * 512]
                            if qf == 0:
                                op(DVE, lambda dst=dst, bk=bk: vE.scalar_tensor_tensor(dst, dst, ALPHA, PS[bk][:], ALU.mult, ALU.add), reads=[PB[bk], xr4B], writes=[xr4B])
                            else:
                                op(DVE, lambda dst=dst, bk=bk: vE.tensor_tensor(dst, dst, PS[bk][:], ALU.add), reads=[PB[bk], xr4B], writes=[xr4B])

                def qtiles(qf):
                    c0 = qf * 11
                    c1 = min(NFC, c0 + 11)
                    out = []
                    ca = c0
                    while ca < c1:
                        ncw = min(4, c1 - ca)
                        out.append((ca, ncw))
                        ca += ncw
                    return out

                load_xt(0)
                dma(SP, lambda: yE.dma_start(out=xr4[:], in_=X2[0:512, :].rearrange("(a p) d -> p a d", p=128)), writes=[xr4B])
                qc = 0
                up_tile(qc, 0, *qtiles(0)[0])
                for tg in range(8):
                    for qf in range(4):
                        for (ca, ncw) in qtiles(qf)[1:]:
                            up_tile(qc, qf, ca, ncw)
                        if qf < 3:
                            up_tile(qc + 1, qf + 1, *qtiles(qf + 1)[0])
                            down(qc, qf)
                        else:
                            if tg < 7:
                                load_xt(tg + 1)
                            down(qc, qf)
                            if tg < 7:
                                up_tile(qc + 1, 0, *qtiles(0)[0])
                        qc += 1
                    fpend = []
                    for tb in range(4):
                        tbg = tg * 4 + tb
                        layer_norm(es, xr4[:, tb, :], xr4B, Gt, Bt, gB, Xdst[tbg * 128:(tbg + 1) * 128, :], XT, tbg, [6, 7], lts[tb % 2], defer=fpend)
                        if len(fpend) > 1:
                            fpend.pop(0)()
                    while fpend:
                        fpend.pop(0)()
                    if tg < 7:
                        t0n = (tg + 1) * 512
                        dma(SP, lambda t0n=t0n: yE.dma_start(out=xr4[:], in_=X2[t0n:t0n + 512, :].rearrange("(a p) d -> p a d", p=128)), writes=[xr4B])
                barrier()
        barrier()

    return nc


def _rope_rep(pos, rot, m):
    inv = (np.float32(500000.0) ** (-(np.arange(0, rot, 2, dtype=np.float32)) / np.float32(rot))).astype(np.float32)
    ang = (pos[:, None].astype(np.float32) * inv[None, :]).astype(np.float32)
    c, s = np.cos(ang).astype(np.float32), np.sin(ang).astype(np.float32)
    rep = lambda a: np.tile(a[:, None, :], (1, m, 1)).reshape(T, -1)
    out = np.stack([rep(c), rep(s)], 0)
    return np.ascontiguousarray(out.reshape(2, NTB, 128, -1).transpose(0, 2, 1, 3))


def kernel(**inp):
    f = lambda k: np.ascontiguousarray(np.asarray(inp[k], dtype=np.float32))
    xp, xs = f("x_prompt"), f("x_sample")
    rep = lambda a: np.ascontiguousarray(np.broadcast_to(a[:, None, :], (a.shape[0], 128, a.shape[1])))
    cw, cbias = f("ffn_conv_w"), f("ffn_conv_b")
    cpar = np.concatenate([cw, cbias[:, None, :]], axis=1)
    convp = np.ascontiguousarray(cpar.reshape(4, 4, 2 * NFC, 128).transpose(0, 3, 2, 1))
    shared = {
        "diff_w_in": f("diff_w_in"), "diff_w_out": f("diff_w_out"), "win_w_in": f("win_w_in"),
        "win_w_out": f("win_w_out"), "ffn_w_up": f("ffn_w_up"), "ffn_w_down": f("ffn_w_down"),
        "lam_r": rep(f("diff_lam").reshape(2, 256)), "subg_r": rep(f("diff_subln_g")), "sink_r": rep(f("win_sink")),
        "lnm_g": rep(f("ln_mix_g")), "lnm_b": rep(f("ln_mix_b")), "lnf_g": rep(f("ln_ffn_g")), "lnf_b": rep(f("ln_ffn_b")),
        "convp": convp, "ident": np.eye(128).astype(ml_dtypes.bfloat16),
    }
    jj, ii = np.meshgrid(np.arange(128), np.arange(128), indexing="ij")
    mL = (ii <= jj).astype(np.float32)
    mU = (jj <= ii).astype(np.float32)
    shared["trimask"] = np.stack([np.tile(mL, (1, 4)), np.tile(mU, (1, 4))], 0).astype(ml_dtypes.bfloat16)
    in_maps = []
    for c in range(8):
        two = c < 4
        if two:
            xc = np.concatenate([xp[2 * c], xp[2 * c + 1]], 0)
            pos = np.concatenate([np.arange(2048), np.arange(2048)])
        else:
            xc = xs[c % 2]
            pos = np.arange(4096)
        mb = np.zeros((128, 2, 32), np.float32)
        if two:
            mb[:, 0, 16:] = NEG
            mb[:, 1, :16] = NEG
        fl = np.zeros((128, 2), np.float32)
        fl[:, 0] = 0.0 if two else 1.0
        fl[:, 1] = NEG if two else 0.0
        d = dict(shared)
        d.update({"xin": np.ascontiguousarray(xc), "ropd": _rope_rep(pos, 16, 8), "ropw": _rope_rep(pos, 32, 4), "maskb": mb, "flags": fl})
        in_maps.append(d)
    nc = build()
    res = run_bass_kernel_spmd(nc, in_maps, core_ids=list(range(8)))
    ys = [np.asarray(r["y"], dtype=np.float32) for r in res.results]
    y_prompt = np.stack([ys[c // 2][(c % 2) * 2048:(c % 2 + 1) * 2048] for c in range(8)], 0)
    y_sample = np.stack([ys[4], ys[5]], 0)
    return (y_prompt, y_sample)
```
